# Optimizing a Trainium2 kernel written in Bass

```python
import math
import jax, jax.numpy as jnp
from jax import lax
import numpy as np

D_MODEL = 1024
BATCH = 2
SEQ = 8192
DEPTH = 1

CHUNK = 64
CONV_CH = D_MODEL // 2
CONV_WIDTH = 31
RET_HEADS = 8
RET_DV = (D_MODEL // 2) // RET_HEADS
RET_DK = RET_DV // 2
D_MIX = CONV_CH + RET_HEADS * RET_DV
N_MEM = 256
XATTN_HEADS = 4
XATTN_HEAD_DIM = D_MODEL // XATTN_HEADS
D_FF = 4 * D_MODEL
ROPE_BASE = 10000.0
EPS = 1e-6

IN_SPLITS = (CONV_CH, CONV_CH, RET_HEADS * RET_DK, RET_HEADS * RET_DK, RET_HEADS * RET_DV, RET_HEADS * RET_DV)
D_IN = sum(IN_SPLITS)

kernel_name = "hybrid_conv_retention_memxattn_block"


def rmsnorm(x, w):
    xf = x.astype(jnp.float32)
    y = xf * lax.rsqrt(jnp.mean(xf * xf, axis=-1, keepdims=True) + EPS)
    return (y * w.astype(jnp.float32)).astype(x.dtype)


def apply_rotary(t, positions):
    half = t.shape[-1] // 2
    inv_freq = ROPE_BASE ** (-jnp.arange(half, dtype=jnp.float32) / half)
    ang = positions.astype(jnp.float32)[..., None] * inv_freq
    cos = jnp.cos(ang)[:, :, None, :]
    sin = jnp.sin(ang)[:, :, None, :]
    t1, t2 = t[..., :half].astype(jnp.float32), t[..., half:].astype(jnp.float32)
    return jnp.concatenate([t1 * cos - t2 * sin, t1 * sin + t2 * cos], axis=-1)


def conv_module(a, b, conv_w, conv_b, ln_w, ln_b):
    u = a * jax.nn.sigmoid(b)
    y = lax.conv_general_dilated(
        u, conv_w[:, None, :], window_strides=(1,), padding=[(CONV_WIDTH - 1, 0)],
        dimension_numbers=("NWC", "WIO", "NWC"), feature_group_count=CONV_CH) + conv_b
    yf = y.astype(jnp.float32)
    mu = jnp.mean(yf, axis=-1, keepdims=True)
    var = jnp.mean(jnp.square(yf - mu), axis=-1, keepdims=True)
    yn = (yf - mu) * lax.rsqrt(var + EPS) * ln_w + ln_b
    return jax.nn.silu(yn)


def retention(q, k, v, positions):
    B, S = q.shape[:2]
    N = S // CHUNK
    q = apply_rotary(q.reshape(B, S, RET_HEADS, RET_DK), positions) * (RET_DK ** -0.5)
    k = apply_rotary(k.reshape(B, S, RET_HEADS, RET_DK), positions)
    v = v.reshape(B, S, RET_HEADS, RET_DV).astype(jnp.float32)
    qc = q.reshape(B, N, CHUNK, RET_HEADS, RET_DK)
    kc = k.reshape(B, N, CHUNK, RET_HEADS, RET_DK)
    vc = v.reshape(B, N, CHUNK, RET_HEADS, RET_DV)

    log_g = jnp.log1p(-jnp.exp2(-5.0 - jnp.arange(RET_HEADS, dtype=jnp.float32)))
    idx = jnp.arange(CHUNK, dtype=jnp.float32)
    dist = jnp.abs(idx[:, None] - idx[None, :])
    decay_in = jnp.exp(log_g[:, None, None] * dist)
    zeta = jnp.exp(log_g[:, None] * (CHUNK - 1 - idx)[None, :])
    xi = jnp.exp(log_g[:, None] * (idx + 1.0)[None, :])
    g_chunk = jnp.exp(log_g * CHUNK)

    scores = jnp.einsum('bnihd,bnjhd->bnhij', qc, kc) * decay_in
    y_in = jnp.einsum('bnhij,bnjhe->bnihe', scores, vc)

    kv = jnp.einsum('bnjhd,bnjhe,hj->nbhde', kc, vc, zeta)

    def step(state, kv_n):
        return g_chunk[None, :, None, None] * state + kv_n, state

    init = jnp.zeros((B, RET_HEADS, RET_DK, RET_DV), jnp.float32)
    _, state_prev = lax.scan(step, init, kv)
    y_x = jnp.einsum('bnihd,nbhde,hi->bnihe', qc, state_prev, xi)
    return (y_in + y_x).reshape(B, S, RET_HEADS, RET_DV)


def head_group_mixer(h, positions, w_in, conv_w, conv_b, conv_ln_w, conv_ln_b, ret_gn_w, w_out):
    B, S, _ = h.shape
    proj = h @ w_in
    cuts = list(np.cumsum(IN_SPLITS)[:-1])
    ca, cb, rq, rk, rv, rg = jnp.split(proj, cuts, axis=-1)
    y_conv = conv_module(ca, cb, conv_w, conv_b, conv_ln_w, conv_ln_b)
    y = retention(rq, rk, rv, positions)
    mu = jnp.mean(y, axis=-1, keepdims=True)
    var = jnp.mean(jnp.square(y - mu), axis=-1, keepdims=True)
    y = (y - mu) * lax.rsqrt(var + EPS) * ret_gn_w.reshape(RET_HEADS, RET_DV)
    y_ret = jax.nn.silu(rg.astype(jnp.float32)) * y.reshape(B, S, RET_HEADS * RET_DV)
    merged = jnp.concatenate([y_conv, y_ret], axis=-1).astype(h.dtype)
    return merged @ w_out


def memory_cross_attention(h, mem_n, xq_w, xkv_w, xo_w):
    B, S, _ = h.shape
    q = (h @ xq_w).reshape(B, S, XATTN_HEADS, XATTN_HEAD_DIM)
    kv = (mem_n @ xkv_w).reshape(B, N_MEM, 2, XATTN_HEADS, XATTN_HEAD_DIM)
    k, v = kv[:, :, 0], kv[:, :, 1]
    s = jnp.einsum('bshd,bmhd->bhsm', q.astype(jnp.float32), k.astype(jnp.float32)) * (XATTN_HEAD_DIM ** -0.5)
    p = jax.nn.softmax(s, axis=-1)
    o = jnp.einsum('bhsm,bmhd->bshd', p, v.astype(jnp.float32)).reshape(B, S, D_MODEL)
    return o.astype(h.dtype) @ xo_w


def sq_relu_mlp(h, up_w, down_w):
    return jnp.square(jax.nn.relu(h @ up_w)) @ down_w


def setup_inputs(seed: int = 0) -> dict:
    key = jax.random.key(seed)
    ks = jax.random.split(key, 24)
    f32 = jnp.float32

    def w(k, shape, fan_in):
        return jax.random.normal(k, shape, f32) * (fan_in ** -0.5)

    def gain(k, n):
        return 1.0 + 0.02 * jax.random.normal(k, (n,), f32)

    x = jax.random.normal(ks[0], (BATCH, SEQ, D_MODEL), f32)
    mem = jax.random.normal(ks[1], (BATCH, N_MEM, D_MODEL), f32)
    start = jax.random.randint(ks[2], (BATCH, 1), 0, 1000, dtype=jnp.int32) * CHUNK
    positions = (start + jnp.arange(SEQ, dtype=jnp.int32)[None, :]).astype(jnp.int32)
    return {
        "x": x,
        "mem": mem,
        "positions": positions,
        "norm_mix_w": gain(ks[3], D_MODEL),
        "w_in": w(ks[4], (D_MODEL, D_IN), D_MODEL),
        "conv_w": w(ks[5], (CONV_WIDTH, CONV_CH), CONV_WIDTH),
        "conv_b": 0.02 * jax.random.normal(ks[6], (CONV_CH,), f32),
        "conv_ln_w": gain(ks[7], CONV_CH),
        "conv_ln_b": 0.02 * jax.random.normal(ks[8], (CONV_CH,), f32),
        "ret_gn_w": gain(ks[9], RET_HEADS * RET_DV),
        "w_out": w(ks[10], (D_MIX, D_MODEL), D_MIX),
        "norm_xattn_w": gain(ks[11], D_MODEL),
        "norm_mem_w": gain(ks[12], D_MODEL),
        "xq_w": w(ks[13], (D_MODEL, D_MODEL), D_MODEL),
        "xkv_w": w(ks[14], (D_MODEL, 2 * D_MODEL), D_MODEL),
        "xo_w": w(ks[15], (D_MODEL, D_MODEL), D_MODEL),
        "norm_mlp_w": gain(ks[16], D_MODEL),
        "mlp_up_w": w(ks[17], (D_MODEL, D_FF), D_MODEL),
        "mlp_down_w": w(ks[18], (D_FF, D_MODEL), D_FF),
        "norm_f_w": gain(ks[19], D_MODEL),
    }


def reference(x, mem, positions, norm_mix_w, w_in, conv_w, conv_b, conv_ln_w, conv_ln_b, ret_gn_w, w_out,
              norm_xattn_w, norm_mem_w, xq_w, xkv_w, xo_w, norm_mlp_w, mlp_up_w, mlp_down_w, norm_f_w):
    h = x
    mem_n = rmsnorm(mem, norm_mem_w)
    for _ in range(DEPTH):
        h = h + head_group_mixer(rmsnorm(h, norm_mix_w), positions, w_in, conv_w, conv_b,
                                 conv_ln_w, conv_ln_b, ret_gn_w, w_out)
        h = h + memory_cross_attention(rmsnorm(h, norm_xattn_w), mem_n, xq_w, xkv_w, xo_w)
        h = h + sq_relu_mlp(rmsnorm(h, norm_mlp_w), mlp_up_w, mlp_down_w)
    return rmsnorm(h, norm_f_w).astype(x.dtype)
```

```python
import contextlib
import struct
import numpy as np
import ml_dtypes
import concourse.bass as bass
import concourse.mybir as mybir
from concourse.bass_utils import run_bass_kernel_spmd

F32 = mybir.dt.float32
BF16 = mybir.dt.bfloat16
I32 = mybir.dt.int32
AF = mybir.ActivationFunctionType
ALU = mybir.AluOpType
AX = mybir.AxisListType

D = 1024
SEQ = 8192
NCORE = 8
TPC = 2048
TH = 1024
NPASS = TPC // TH
NTP = TH // 128
NM = TH // 512
NPRE = (SEQ - TPC) // 128
H = 8
DK = 32
DV = 64
CW = 31
DFF = 4096
EPS = 1e-6
PI = float(np.pi)

ENGS = ("pe", "act", "dve", "pool", "sp")


class Op:
    __slots__ = ("eng", "fn", "deps", "needed", "sem", "val", "is_dma", "ndma", "name")

    def __init__(self, eng, fn, name=""):
        self.eng = eng
        self.fn = fn
        self.deps = []
        self.needed = False
        self.sem = None
        self.val = None
        self.is_dma = False
        self.ndma = 0
        self.name = name


class Sched:
    def __init__(self):
        self.ops = {e: [] for e in ENGS}
        self.last_w = {}
        self.readers = {}
        self.dma_keys = {}
        self.frozen = False

    def _dep(self, op, other):
        if other is None or other is op:
            return
        if other.is_dma:
            other = self.dma_keys[other.sem][-1]
            if other is op:
                return
        op.deps.append(other)
        other.needed = True

    def add(self, eng, fn, reads=(), writes=(), name="", dma_key=None, ndma=0):
        if self.frozen:
            return None
        op = Op(eng, fn, name)
        if dma_key is not None:
            op.is_dma = True
            op.ndma = ndma
            op.sem = dma_key
        for r in reads:
            w = self.last_w.get(r)
            if w is not None:
                if not (w.eng == eng and eng == "pe" and not w.is_dma and not op.is_dma):
                    self._dep(op, w)
            if r[:2] in ("PF", "PT"):
                for rd in self.readers.get(r, []):
                    if rd.eng != eng:
                        self._dep(op, rd)
            self.readers.setdefault(r, []).append(op)
        for r in writes:
            w = self.last_w.get(r)
            if w is not None:
                same = (w.eng == eng and eng == "pe" and not w.is_dma and not op.is_dma)
                if not same:
                    self._dep(op, w)
            for rd in self.readers.get(r, []):
                same = (rd.eng == eng and eng == "pe" and not rd.is_dma and not op.is_dma)
                if not same:
                    self._dep(op, rd)
            self.last_w[r] = op
            self.readers[r] = []
        if dma_key is not None:
            self.dma_keys.setdefault(dma_key, []).append(op)
        self.ops[eng].append(op)
        return op

    def barrier(self):
        if self.frozen:
            return
        lasts = []
        for e in ENGS:
            for op in reversed(self.ops[e]):
                if not op.is_dma and op.fn is not None:
                    lasts.append(op)
                    break
        dl = [lst[-1] for lst in self.dma_keys.values() if lst]
        for e in ENGS:
            op = Op(e, None, "barrier")
            for o in lasts:
                if o.eng != e or e != "pe":
                    op.deps.append(o)
                    o.needed = True
            for o in dl:
                op.deps.append(o)
            self.ops[e].append(op)

    def emit(self, block_engines, sems, dma_sems):
        for e in ENGS:
            cnt = 0
            for op in self.ops[e]:
                if op.is_dma:
                    continue
                if op.needed:
                    cnt += 1
                    op.val = cnt
                op.sem = ("eng", e)
        for key, lst in self.dma_keys.items():
            cnt = 0
            for op in lst:
                cnt += 16 * op.ndma
                op.val = cnt
        self.stats = {}

        def semh(op):
            if op.is_dma:
                return dma_sems[op.sem]
            return sems[op.sem[1]]

        def run(e, eng):
            seen = {}
            nwait = 0
            for op in self.ops[e]:
                need = {}
                for d in op.deps:
                    if d.val is None:
                        raise RuntimeError(f"dep {d.name} has no val")
                    if d.sem not in need or need[d.sem][0] < d.val:
                        need[d.sem] = (d.val, semh(d))
                for k, (v, hh) in need.items():
                    if seen.get(k, 0) >= v:
                        continue
                    eng.wait_ge(hh, v)
                    nwait += 1
                    seen[k] = v
                if op.fn is None:
                    continue
                if op.is_dma:
                    op.fn(eng, dma_sems[op.sem])
                else:
                    ins = op.fn(eng)
                    if op.needed:
                        ins.then_inc(sems[e], 1)
            self.stats[e] = (len(self.ops[e]), nwait)

        for e in ENGS:
            if not self.ops[e]:
                continue

            def section(eng, e=e):
                run(e, eng)
            block_engines[e](section)


def _split2pi():
    def trunc(v, bits):
        i = struct.unpack("<I", struct.pack("<f", np.float32(v)))[0]
        i &= ~((1 << (23 - bits)) - 1) & 0xFFFFFFFF
        return struct.unpack("<f", struct.pack("<I", i))[0]
    tp = 2 * np.pi
    c1 = trunc(tp, 8)
    c2 = trunc(tp - c1, 10)
    c3 = float(np.float32(tp - c1 - c2))
    return float(c1), float(c2), c3


C1, C2, C3 = _split2pi()


def _host_consts():
    lg = np.log1p(-np.exp2(-5.0 - np.arange(H, dtype=np.float64)))
    scale = DK ** -0.5
    c = {}
    c["ident"] = np.eye(128, dtype=np.float32).astype(ml_dtypes.bfloat16)
    c["onesm"] = np.full((128, 128), 1.0 / 512.0, dtype=np.float32).astype(ml_dtypes.bfloat16)
    j = np.arange(128)[:, None]
    i = np.arange(128)[None, :]
    cj, ci = j // 64, i // 64
    mask = np.zeros((128, H, 128), dtype=np.float64)
    for h in range(H):
        m = np.where(cj == ci, np.exp(lg[h] * np.abs(i - j)),
                     np.where(cj < ci, np.exp(lg[h] * (i - j)), 0.0))
        mask[:, h, :] = m * scale
    c["mask"] = mask.astype(np.float32)
    xi = np.zeros((64, 4, 128), dtype=np.float64)
    g128 = np.zeros((64, 4, 64), dtype=np.float64)
    for hp in range(4):
        for a in range(2):
            h = 2 * hp + a
            xi[32 * a:32 * a + 32, hp, :] = scale * np.exp(lg[h] * (np.arange(128) + 1.0))[None, :]
            g128[32 * a:32 * a + 32, hp, :] = np.exp(lg[h] * 128.0)
    c["xifm"] = xi.astype(np.float32)
    c["g128"] = g128.astype(np.float32)
    c["zeta"] = np.exp(lg[None, :] * (127.0 - np.arange(128))[:, None]).astype(np.float32)
    dist = (NPRE * 128 - 1) - (np.arange(NPRE)[None, :, None] * 128 + np.arange(128)[:, None, None])
    c["wpre"] = np.exp(lg[None, None, :] * dist).astype(np.float32)
    half = DK // 2
    invf = (np.float32(10000.0) ** (-(np.arange(half, dtype=np.float32)) / np.float32(half))).astype(np.float32)
    c["invf"] = np.broadcast_to(invf[None, :], (128, half)).copy()
    return c


def build_nc():
    nc = bass.Bass("TRN2", target_bir_lowering=False)

    def din(name, shape, dt=F32):
        return nc.dram_tensor(name, list(shape), dt, kind="ExternalInput").ap()

    xmain = din("xmain", [TPC, D])
    xpre = din("xpre", [NPRE * 128, D])
    posm = din("posm", [128, TPC // 128], I32)
    posp = din("posp", [128, NPRE], I32)
    memx = din("memx", [256, D])
    gvec = din("gvec", [5, D])
    w_in = din("w_in", [D, 2560])
    w_out = din("w_out", [D, D])
    xq_w = din("xq_w", [D, D])
    xkv_w = din("xkv_w", [D, 2 * D])
    xo_w = din("xo_w", [D, D])
    up_w = din("up_w", [D, DFF])
    down_w = din("down_w", [DFF, D])
    convw_d = din("convw", [128, 4, CW])
    convb_d = din("convb", [128, 4])
    lnw_d = din("lnw", [128, 4])
    lnb_d = din("lnb", [128, 4])
    gnw_d = din("gnw", [1, 512])
    ident_d = din("ident", [128, 128], BF16)
    onesm_d = din("onesm", [128, 128], BF16)
    mask_d = din("mask", [128, H, 128])
    xifm_d = din("xifm", [64, 4, 128])
    g128_d = din("g128", [64, 4, 64])
    zeta_d = din("zeta", [128, H])
    wpre_d = din("wpre", [128, NPRE, H])
    invf_d = din("invf", [128, 16])
    out_d = nc.dram_tensor("out", [TPC, D], F32, kind="ExternalOutput").ap()

    S = Sched()
    dma_key_names = set()
    del DBG_OUT[:]

    def checkpoint(tag, dumps=()):
        if STOP != tag:
            return
        S.barrier()
        for (name, ap, shape, dt, rres) in dumps:
            dd = nc.dram_tensor("dbg_" + name, list(shape), dt, kind="ExternalOutput").ap()
            DBG_OUT.append("dbg_" + name)

            def fn(e, s, dd=dd, ap=ap):
                e.dma_start(out=dd, in_=ap).then_inc(s, 16)
            dma_key_names.add("dbg")
            S.add("sp", fn, reads=list(rres), writes=["dbgout"], dma_key="dbg", ndma=1)
        S.barrier()
        S.frozen = True

    with contextlib.ExitStack() as es:
      try:
            _cnt = [0]

            def sbuf(stack, name, shape, dt):
                _cnt[0] += 1
                return stack.enter_context(nc.sbuf_tensor(f"sb{_cnt[0]}_{name}", list(shape), dt))

            h = sbuf(es, "h", [128, NTP, D], F32)
            W = sbuf(es, "W", [128, 2 * 16384], BF16)
            gsl = sbuf(es, "gsl", [128, 2, D], F32)
            ident = sbuf(es, "ident", [128, 128], BF16)
            onesm = sbuf(es, "onesm", [128, 128], BF16)
            mask = sbuf(es, "mask", [128, H, 128], F32)
            xifm = sbuf(es, "xifm", [64, 4, 128], F32)
            g128 = sbuf(es, "g128", [64, 4, 64], F32)
            zeta = sbuf(es, "zeta", [128, H], F32)
            invf = sbuf(es, "invf", [128, 16], F32)
            convw = sbuf(es, "convw", [128, 4, CW], F32)
            convb = sbuf(es, "convb", [128, 4], F32)
            lnw = sbuf(es, "lnw", [128, 4], F32)
            lnb = sbuf(es, "lnb", [128, 4], F32)
            gnw = sbuf(es, "gnw", [128, 512], F32)
            neghalf = sbuf(es, "neghalf", [128, 8], F32)
            cosm = sbuf(es, "cosm", [128, TPC // 128, 16], F32)
            sinm = sbuf(es, "sinm", [128, TPC // 128, 16], F32)
            kTm = sbuf(es, "kTm", [128, 8, 256], BF16)
            vm = sbuf(es, "vm", [128, 2, D], BF16)
            Sst = sbuf(es, "Sst", [64, 4, 64], F32)
            Sbf = sbuf(es, "Sbf", [64, 4, 128], BF16)
            uhalo = sbuf(es, "uhalo", [128, 4, 32], BF16)
            ss = sbuf(es, "ss", [128, 64], F32)
            rstd = sbuf(es, "rstd", [128, 64], F32)
            junk = sbuf(es, "junk", [128, D], BF16)

            PT = es.enter_context(nc.psum_tensor("PT", [128, 2, 1024], BF16))
            PF = es.enter_context(nc.psum_tensor("PF", [128, 6, 512], F32))

            sems = {e: es.enter_context(nc.semaphore("s_" + e)) for e in ENGS}

            def dma(eng, key, fn, n, reads=(), writes=()):
                dma_key_names.add(key)
                S.add(eng, fn, reads=reads, writes=writes, dma_key=key, ndma=n)

            def wview(off, k, n):
                return W[:, off:off + k * n].rearrange("p (k n) -> p k n", k=k)

            def load_weight(key, dst3, src, r0, nrows, c0, ncols, wres):
                nk = nrows // 128
                cs = min(ncols, 2048)
                pieces = []
                for k0 in range(0, nk, 2):
                    for cc in range(0, ncols, cs):
                        pieces.append((k0, cc, min(cs, ncols - cc)))

                def fn(e, s):
                    for (k0, cc, cw_) in pieces:
                        srcv = src[r0 + k0 * 128: r0 + (k0 + 2) * 128, c0 + cc: c0 + cc + cw_]
                        e.dma_start(out=dst3[:, k0:k0 + 2, cc:cc + cw_],
                                    in_=srcv.rearrange("(k p) n -> p k n", p=128)).then_inc(s, 16)
                dma("pool", key, fn, len(pieces), writes=wres)

            def load_g(slot, row):
                def fn(e, s):
                    e.dma_start(out=gsl[:, slot, :], in_=gvec[row:row + 1, :].broadcast_to([128, D])).then_inc(s, 16)
                dma("sp", f"g{slot}", fn, 1, writes=[f"g{slot}"])

            def norm_tile(src, src_res, slot, col, xn_t, xn_res, pt_bank, dst, dst_res, extra_reads=()):
                norm_tile_a(src, src_res, slot, col, xn_t, xn_res, extra_reads)
                norm_tile_b(xn_t, xn_res, pt_bank, dst, dst_res)

            def norm_group(items):
                n = len(items)
                for i in range(n + 1):
                    if i < n:
                        norm_tile_a(*items[i][0])
                    if i >= 1:
                        norm_tile_b(*items[i - 1][1], evac=("dve" if (i - 1) % 2 == 1 else "act"))

            RSTD_MODE = ["pool"]

            def norm_tile_a(src, src_res, slot, col, xn_t, xn_res, extra_reads=()):
                S.add("act", lambda e: e.activation(out=junk[:], in_=src, func=AF.Square, accum_out=ss[:, col:col + 1]),
                      reads=[src_res] + list(extra_reads), writes=["junk", f"ss{col}"])
                if RSTD_MODE[0] == "pool":
                    S.add("pool", lambda e: e.tensor_scalar(out=rstd[:, col:col + 1], in0=ss[:, col:col + 1], scalar1=1.0 / D,
                                                            scalar2=EPS, op0=ALU.mult, op1=ALU.add),
                          reads=[f"ss{col}"], writes=[f"rs{col}"])
                    S.add("pool", lambda e: e.tensor_tensor(out=rstd[:, col:col + 1], in0=rstd[:, col:col + 1],
                                                            in1=neghalf[:, 0:1], op=ALU.pow),
                          reads=[f"rs{col}", "neghalf"], writes=[f"rs{col}"])
                else:
                    S.add("act", lambda e: e.activation(out=rstd[:, col:col + 1], in_=ss[:, col:col + 1], func=AF.Ln,
                                                        scale=1.0 / D, bias=EPS),
                          reads=[f"ss{col}"], writes=[f"rs{col}"])
                    S.add("act", lambda e: e.activation(out=rstd[:, col:col + 1], in_=rstd[:, col:col + 1], func=AF.Exp, scale=-0.5),
                          reads=[f"rs{col}"], writes=[f"rs{col}"])
                S.add("dve", lambda e: e.scalar_tensor_tensor(out=xn_t, in0=src, scalar=rstd[:, col:col + 1],
                                                              in1=gsl[:, slot, :], op0=ALU.mult, op1=ALU.mult),
                      reads=[src_res, f"rs{col}", f"g{slot}"], writes=[xn_res])

            def norm_tile_b(xn_t, xn_res, pt_bank, dst, dst_res, evac="act"):
                def tr(e):
                    for k in range(8):
                        ins = e.transpose(out=PT[:, pt_bank, k * 128:(k + 1) * 128], in_=xn_t[:, k * 128:(k + 1) * 128],
                                          identity=ident[:])
                    return ins
                S.add("pe", tr, reads=[xn_res, "ident"], writes=[f"PT{pt_bank}"])
                if evac == "act":
                    S.add("act", lambda e: e.activation(out=dst, in_=PT[:, pt_bank, :].rearrange("p (k n) -> p k n", k=8),
                                                        func=AF.Copy),
                          reads=[f"PT{pt_bank}"], writes=[dst_res])
                else:
                    S.add("dve", lambda e: e.tensor_copy(out=dst, in_=PT[:, pt_bank, :].rearrange("p (k n) -> p k n", k=8)),
                          reads=[f"PT{pt_bank}"], writes=[dst_res])

            def rotary(src3, cos_ap, sin_ap, nh, ta, tb, dst3, rres, wres, eng="dve", tag=""):
                cb = cos_ap.unsqueeze(1).broadcast_to([128, nh, 16])
                sb_ = sin_ap.unsqueeze(1).broadcast_to([128, nh, 16])
                x1 = src3[:, :, 0:16]
                x2 = src3[:, :, 16:32]
                S.add(eng, lambda e: e.tensor_tensor(out=ta, in0=x1, in1=cb, op=ALU.mult), reads=rres, writes=["rot_ta" + tag])
                S.add(eng, lambda e: e.tensor_tensor(out=tb, in0=x2, in1=sb_, op=ALU.mult), reads=rres, writes=["rot_tb" + tag])
                S.add(eng, lambda e: e.tensor_tensor(out=dst3[:, :, 0:16], in0=ta, in1=tb, op=ALU.subtract),
                      reads=["rot_ta" + tag, "rot_tb" + tag], writes=[wres + "a"])
                S.add(eng, lambda e: e.tensor_tensor(out=ta, in0=x1, in1=sb_, op=ALU.mult), reads=rres + [wres + "a"], writes=["rot_ta" + tag])
                S.add(eng, lambda e: e.tensor_tensor(out=tb, in0=x2, in1=cb, op=ALU.mult), reads=rres + [wres + "a"], writes=["rot_tb" + tag])
                S.add(eng, lambda e: e.tensor_tensor(out=dst3[:, :, 16:32], in0=ta, in1=tb, op=ALU.add),
                      reads=["rot_ta" + tag, "rot_tb" + tag], writes=[wres + "b"])

            with contextlib.ExitStack() as st:
                posi = sbuf(st, "posi", [128, 64], I32)
                posf = sbuf(st, "posf", [128, 64], F32)
                ang = sbuf(st, "ang", [128, 64, 16], F32)
                kf = sbuf(st, "kf", [128, 64, 16], F32)
                ki = sbuf(st, "ki", [128, 64, 16], I32)
                mm_ = sbuf(st, "mm_", [128, 64, 16], F32)
                cosp = sbuf(st, "cosp", [128, NPRE, 16], F32)
                sinp = sbuf(st, "sinp", [128, NPRE, 16], F32)
                wpre = sbuf(st, "wpre", [128, NPRE, H], F32)
                xs = sbuf(st, "xs", [128, 2, D], F32)
                xn2 = sbuf(st, "xn2", [128, 2, D], BF16)
                xnTp = sbuf(st, "xnTp", [128, 2, 8, 128], BF16)
                memT = sbuf(st, "memT", [128, 8, 256], BF16)
                krot = sbuf(st, "krot", [128, 2, 256], F32)
                rta = sbuf(st, "rta", [128, 16, 16], F32)
                rtb = sbuf(st, "rtb", [128, 16, 16], F32)
                kzp = sbuf(st, "kzp", [128, 2, 256], BF16)
                vbp = sbuf(st, "vbp", [128, 2, 512], BF16)
                thh = sbuf(st, "thh", [128, 32], F32)

                def cfn(e, s):
                    for (dst, src) in [(ident[:], ident_d), (onesm[:], onesm_d), (mask[:], mask_d), (xifm[:], xifm_d),
                                       (g128[:], g128_d), (zeta[:], zeta_d), (invf[:], invf_d), (convw[:], convw_d),
                                       (convb[:], convb_d), (lnw[:], lnw_d), (lnb[:], lnb_d), (wpre[:], wpre_d),
                                       (posi[:, 0:NPRE], posp), (posi[:, NPRE:64], posm)]:
                        e.dma_start(out=dst, in_=src).then_inc(s, 16)
                    e.dma_start(out=gnw[:], in_=gnw_d[0:1, :].broadcast_to([128, 512])).then_inc(s, 16)
                dma("sp", "const", cfn, 15, writes=["const"])
                load_g(0, 4)
                load_g(1, 0)
                S.add("pool", lambda e: e.memset(neghalf[:], -0.5), writes=["neghalf"])
                S.add("dve", lambda e: e.memset(Sst[:], 0.0), writes=["Sst"])
                S.add("pool", lambda e: e.memset(Sbf[:], 0.0), writes=["Sbf"])
                S.add("dve", lambda e: e.memset(uhalo[:], 0.0), writes=["uhalo"])
                S.barrier()
                RSTD_MODE[0] = "act"
                w_in_v = wview(0, 8, 2560)
                w_out_v = wview(24576, 8, 1024)
                xkst = sbuf(st, "xkst", [128, 8, 1024], BF16)
                xkvk = xkst
                xkvv = xkst
                load_weight("wk", xkvk, xkv_w, 0, D, 0, 1024, ["XK"])
                load_weight("k_in", w_in_v, w_in, 0, D, 0, 2560, ["R0a", "R0b", "R1a"])
                load_weight("k_out", w_out_v, w_out, 0, D, 0, 1024, ["R1b"])
                S.add("pool", lambda e: e.tensor_scalar(out=convw[:], in0=convw[:], scalar1=0.5, scalar2=None, op0=ALU.mult),
                      reads=["const"], writes=["convw"])
                S.add("dve", lambda e: e.tensor_copy(out=posf[:], in_=posi[:]), reads=["const"], writes=["posf"])
                S.add("dve", lambda e: e.tensor_tensor(out=ang[:], in0=posf[:].unsqueeze(2).broadcast_to([128, 64, 16]),
                                                       in1=invf[:].unsqueeze(1).broadcast_to([128, 64, 16]), op=ALU.mult),
                      reads=["posf", "const"], writes=["ang"])

                def reduce_ang(tag, shift):
                    src = ang
                    if shift != 0.0:
                        S.add("dve", lambda e: e.tensor_scalar(out=mm_[:], in0=ang[:], scalar1=shift, scalar2=None, op0=ALU.add),
                              reads=["ang", "mm_"], writes=["angs"])
                        src = mm_
                        sres = "angs"
                    else:
                        sres = "ang"
                    S.add("dve", lambda e: e.tensor_scalar(out=kf[:], in0=src[:], scalar1=float(1.0 / (2 * np.pi)), scalar2=None,
                                                           op0=ALU.mult), reads=[sres, "kfr"], writes=["kf"])
                    S.add("dve", lambda e: e.tensor_copy(out=ki[:], in_=kf[:]), reads=["kf"], writes=["ki"])
                    S.add("dve", lambda e: e.tensor_copy(out=kf[:], in_=ki[:]), reads=["ki"], writes=["kf"])
                    red = sbuf(st, "red" + tag, [128, 64, 16], F32)
                    S.add("dve", lambda e: e.scalar_tensor_tensor(out=red[:], in0=kf[:], scalar=-C1, in1=src[:], op0=ALU.mult,
                                                                  op1=ALU.add), reads=["kf", sres], writes=["red" + tag])
                    S.add("dve", lambda e: e.scalar_tensor_tensor(out=red[:], in0=kf[:], scalar=-C2, in1=red[:], op0=ALU.mult,
                                                                  op1=ALU.add), reads=["kf", "red" + tag], writes=["red" + tag])
                    S.add("dve", lambda e: e.scalar_tensor_tensor(out=red[:], in0=kf[:], scalar=-C3, in1=red[:], op0=ALU.mult,
                                                                  op1=ALU.add), reads=["kf", "red" + tag], writes=["red" + tag])
                    S.add("dve", lambda e: e.tensor_scalar(out=kf[:], in0=red[:], scalar1=PI, scalar2=-2 * PI, op0=ALU.is_gt,
                                                           op1=ALU.mult), reads=["red" + tag], writes=["kf"])
                    S.add("dve", lambda e: e.tensor_tensor(out=red[:], in0=red[:], in1=kf[:], op=ALU.add),
                          reads=["kf", "red" + tag], writes=["red" + tag])
                    S.add("dve", lambda e: e.tensor_scalar(out=kf[:], in0=red[:], scalar1=-PI, scalar2=2 * PI, op0=ALU.is_lt,
                                                           op1=ALU.mult), reads=["red" + tag], writes=["kf"])
                    S.add("dve", lambda e: e.tensor_tensor(out=red[:], in0=red[:], in1=kf[:], op=ALU.add),
                          reads=["kf", "red" + tag], writes=["red" + tag])
                    S.add("dve", lambda e: e.tensor_scalar(out=red[:], in0=red[:], scalar1=PI, scalar2=-PI, op0=ALU.min,
                                                           op1=ALU.max), reads=["red" + tag], writes=["red" + tag])
                    return red
                rs_ = reduce_ang("s", 0.0)
                S.add("act", lambda e: e.activation(out=sinp[:], in_=rs_[:, 0:NPRE, :], func=AF.Sin), reads=["reds"], writes=["sinp"])
                S.add("act", lambda e: e.activation(out=sinm[:], in_=rs_[:, NPRE:64, :], func=AF.Sin), reads=["reds"], writes=["sinm"])
                S.add("dve", lambda e: e.tensor_scalar(out=ang[:], in0=rs_[:], scalar1=PI / 2, scalar2=None, op0=ALU.add),
                      reads=["reds", "ang", "sinp", "sinm"], writes=["ang"])
                S.add("dve", lambda e: e.tensor_scalar(out=kf[:], in0=ang[:], scalar1=PI, scalar2=-2 * PI, op0=ALU.is_gt, op1=ALU.mult),
                      reads=["ang"], writes=["kf"])
                S.add("dve", lambda e: e.tensor_tensor(out=ang[:], in0=ang[:], in1=kf[:], op=ALU.add), reads=["kf", "ang"], writes=["ang"])
                S.add("dve", lambda e: e.tensor_scalar(out=ang[:], in0=ang[:], scalar1=PI, scalar2=-PI, op0=ALU.min, op1=ALU.max),
                      reads=["ang"], writes=["ang"])
                S.add("act", lambda e: e.activation(out=cosp[:], in_=ang[:, 0:NPRE, :], func=AF.Sin), reads=["ang"], writes=["cosp"])
                S.add("act", lambda e: e.activation(out=cosm[:], in_=ang[:, NPRE:64, :], func=AF.Sin), reads=["ang"], writes=["cosm"])

                checkpoint("tables", [("cosm", cosm[:], [128, 16, 16], F32, []), ("sinm", sinm[:], [128, 16, 16], F32, []),
                                      ("cosp", cosp[:], [128, NPRE, 16], F32, []), ("sinp", sinp[:], [128, NPRE, 16], F32, [])])
                checkpoint("memkv", [("kTm", kTm[:], [128, 8, 256], BF16, []), ("vm", vm[:], [128, 2, D], BF16, [])])
                for mt in range(2):
                    def ldm(e, s, mt=mt):
                        e.dma_start(out=xs[:, mt, :], in_=memx[mt * 128:(mt + 1) * 128, :]).then_inc(s, 16)
                    dma("sp", f"xs{mt}", ldm, 1, writes=[f"xs{mt}"])
                    norm_tile(xs[:, mt, :], f"xs{mt}", 0, mt, xn2[:, mt, :], f"xn{mt}", 0,
                              memT[:, :, mt * 128:(mt + 1) * 128], f"memT{mt}")
                for c in range(8):
                    def mk(e, c=c):
                        for k in range(8):
                            ins = e.matmul(PF[:, c % 2, 0:256], lhsT=xkvk[:, k, c * 128:(c + 1) * 128], rhs=memT[:, k, :],
                                           start=(k == 0), stop=(k == 7))
                        return ins
                    S.add("pe", mk, reads=["XK", "memT0", "memT1"], writes=[f"PF{c % 2}"])
                    S.add("act", lambda e, c=c: e.activation(out=kTm[:, c, :], in_=PF[:, c % 2, 0:256], func=AF.Copy),
                          reads=[f"PF{c % 2}"], writes=["kTm"])
                load_weight("wk", xkvv, xkv_w, 0, D, 1024, 1024, ["XK"])
                def pre_stage0(pt):
                    sl = pt % 2

                    def ldx(e, s, pt=pt, sl=sl):
                        e.dma_start(out=xs[:, sl, :], in_=xpre[pt * 128:(pt + 1) * 128, :]).then_inc(s, 16)
                    dma("sp", f"xs{sl}", ldx, 1, writes=[f"xs{sl}"])
                    col = 2 + (pt % 4)
                    norm_tile_a(xs[:, sl, :], f"xs{sl}", 1, col, xn2[:, sl, :], f"xn{sl}")

                def pre_stage1(pt):
                    sl = pt % 2
                    norm_tile_b(xn2[:, sl, :], f"xn{sl}", sl, xnTp[:, sl, :, :], f"xnTp{sl}")

                def pre_stage2(pt):
                    sl = pt % 2
                    bK, bV = (0, 1) if sl == 0 else (2, 3)

                    def mkv(e, sl=sl, bK=bK, bV=bV):
                        for k in range(8):
                            e.matmul(PF[:, bK, 0:256], lhsT=xnTp[:, sl, k, :], rhs=w_in_v[:, k, 1280:1536],
                                     start=(k == 0), stop=(k == 7))
                        for k in range(8):
                            ins = e.matmul(PF[:, bV, :], lhsT=xnTp[:, sl, k, :], rhs=w_in_v[:, k, 1536:2048],
                                           start=(k == 0), stop=(k == 7))
                        return ins
                    S.add("pe", mkv, reads=[f"xnTp{sl}", "R0a", "R0b", "R1a"], writes=[f"PF{bK}", f"PF{bV}"])
                    S.add("act", lambda e, sl=sl, bV=bV: e.activation(out=vbp[:, sl, :], in_=PF[:, bV, :], func=AF.Copy),
                          reads=[f"PF{bV}"], writes=[f"vbp{sl}"])
                    rotary(PF[:, bK, 0:256].rearrange("p (h d) -> p h d", h=8), cosp[:, pt, :], sinp[:, pt, :], 8,
                           rta[:, 8 * sl:8 * sl + 8, :], rtb[:, 8 * sl:8 * sl + 8, :],
                           krot[:, sl, :].rearrange("p (h d) -> p h d", h=8),
                           [f"PF{bK}", "cosp", "sinp"], f"krot{sl}", eng="dve", tag=f"p{sl}")
                    S.add("dve", lambda e, pt=pt, sl=sl: e.tensor_tensor(
                        out=kzp[:, sl, :].rearrange("p (h d) -> p h d", h=8),
                        in0=krot[:, sl, :].rearrange("p (h d) -> p h d", h=8),
                        in1=wpre[:, pt, :].unsqueeze(2).broadcast_to([128, 8, 32]), op=ALU.mult),
                        reads=[f"krot{sl}a", f"krot{sl}b", "const"], writes=[f"kzp{sl}"])
                    if pt == NPRE - 1:
                        for c in range(4):
                            def mab(e, c=c, sl=sl):
                                for k in range(8):
                                    e.matmul(PF[:, 4, 0:32], lhsT=w_in_v[:, k, c * 128:(c + 1) * 128], rhs=xnTp[:, sl, k, 96:128],
                                             start=(k == 0), stop=(k == 7))
                                for k in range(8):
                                    ins = e.matmul(PF[:, 4, 256:288], lhsT=w_in_v[:, k, 512 + c * 128:512 + (c + 1) * 128],
                                                   rhs=xnTp[:, sl, k, 96:128], start=(k == 0), stop=(k == 7), skip_group_check=True)
                                return ins
                            S.add("pe", mab, reads=[f"xnTp{sl}", "R0a", "R0b", "R1a"], writes=["PF4"])
                            S.add("act", lambda e: e.activation(out=thh[:], in_=PF[:, 4, 256:288], func=AF.Tanh, scale=0.5),
                                  reads=["PF4"], writes=["thh"])
                            S.add("dve", lambda e, c=c: e.scalar_tensor_tensor(out=uhalo[:, c, :], in0=thh[:], scalar=1.0,
                                                                              in1=PF[:, 4, 0:32], op0=ALU.add, op1=ALU.mult),
                                  reads=["thh", "PF4"], writes=["uhalo"])

                def pre_stage3(pt):
                    sl = pt % 2

                    def mst(e, pt=pt, sl=sl):
                        for hp in range(4):
                            ins = e.matmul(PF[0:64, 5, hp * 128:(hp + 1) * 128], lhsT=kzp[:, sl, hp * 64:(hp + 1) * 64],
                                           rhs=vbp[:, sl, hp * 128:(hp + 1) * 128], start=(pt == 0 and hp == 0),
                                           stop=(pt == NPRE - 1), skip_group_check=True)
                        return ins
                    S.add("pe", mst, reads=[f"kzp{sl}", f"vbp{sl}"], writes=["PF5"])

                for it in range(NPRE + 3):
                    if it < NPRE:
                        pre_stage0(it)
                    if 0 <= it - 1 < NPRE:
                        pre_stage1(it - 1)
                    if 0 <= it - 2 < NPRE:
                        pre_stage2(it - 2)
                    if 0 <= it - 3 < NPRE:
                        pre_stage3(it - 3)
                for a in range(2):
                    S.add("dve", lambda e, a=a: e.tensor_copy(
                        out=Sst[32 * a:32 * a + 32, :, :],
                        in_=PF[32 * a:32 * a + 32, 5, :].rearrange("p (hp x) -> p hp x", hp=4)[:, :, 64 * a:64 * a + 64]),
                        reads=["PF5"], writes=["Sst"])
                for a in range(2):
                    S.add("act", lambda e, a=a: e.activation(out=Sbf[32 * a:32 * a + 32, :, 64 * a:64 * a + 64],
                                                             in_=Sst[32 * a:32 * a + 32, :, :], func=AF.Copy),
                          reads=["Sst"], writes=["Sbf"])
                for mc in range(2):
                    for n in range(2):
                        bk = 2 + (mc * 2 + n) % 2

                        def mv(e, mc=mc, n=n, bk=bk):
                            for k in range(8):
                                ins = e.matmul(PF[:, bk, :], lhsT=memT[:, k, mc * 128:(mc + 1) * 128],
                                               rhs=xkvv[:, k, n * 512:(n + 1) * 512], start=(k == 0), stop=(k == 7))
                            return ins
                        S.add("pe", mv, reads=["XK", "memT0", "memT1"], writes=[f"PF{bk}"])
                        S.add("dve", lambda e, mc=mc, n=n, bk=bk: e.tensor_copy(out=vm[:, mc, n * 512:(n + 1) * 512], in_=PF[:, bk, :]),
                              reads=[f"PF{bk}"], writes=["vm"])

                checkpoint("prefix", [("Sst", Sst[:], [64, 4, 64], F32, []), ("uhalo", uhalo[:], [128, 4, 32], BF16, [])])
                S.barrier()

            def run_pass(ps_):
                T0 = ps_ * NTP
                RSTD_MODE[0] = "pool"
                with contextlib.ExitStack() as st:
                    xn = sbuf(st, "xn", [128, 2, D], BF16)
                    xnT = sbuf(st, "xnT", [128, 8, 512], BF16)
                    tht = sbuf(st, "tht", [128, 512], F32)
                    u = sbuf(st, "u", [128, 4, 544], BF16)
                    dg = sbuf(st, "dg", [128, 2, CW, 128], BF16)
                    tmpn = sbuf(st, "tmpn", [128, 2, 512], F32)
                    ybf = sbuf(st, "ybf", [128, 4, 512], BF16)
                    ysqb = sbuf(st, "ysqb", [128, 4, 512], BF16)
                    t_a = sbuf(st, "t_a", [128, 512], F32)
                    t_b = sbuf(st, "t_b", [128, 512], F32)
                    yconv = sbuf(st, "yconv", [128, 4, 512], BF16)
                    rot = sbuf(st, "rot", [128, 512], F32)
                    rtaA = sbuf(st, "rta2", [128, 16, 16], F32)
                    rtbA = sbuf(st, "rtb2", [128, 16, 16], F32)
                    qkb = sbuf(st, "qkb", [128, 2, 512], BF16)
                    kz = sbuf(st, "kz", [128, 2, 256], BF16)
                    vb = sbuf(st, "vb", [128, 2, 512], BF16)
                    gs = sbuf(st, "gs", [128, 2, 512], F32)
                    qkT = sbuf(st, "qkT", [64, 8, 128], BF16)
                    qx = sbuf(st, "qx", [64, 4, 128], BF16)
                    scm = sbuf(st, "scm", [128, 8, 128], BF16)
                    yret = sbuf(st, "yret", [128, 512], BF16)
                    yretT = sbuf(st, "yretT", [128, 4, 128], BF16)
                    gst = sbuf(st, "gst", [128, 32], F32)

                    if ps_ > 0:
                        load_g(1, 0)
                    for m in range(NM):
                        items = []
                        for j in range(4):
                            t = m * 4 + j
                            gt = T0 + t

                            def ldh(e, s, t=t, gt=gt):
                                e.dma_start(out=h[:, t, :], in_=xmain[gt * 128:(gt + 1) * 128, :]).then_inc(s, 16)
                            if ps_ == 0:
                                dma("sp", f"h{t}", ldh, 1, writes=[f"h{t}"])
                            items.append(((h[:, t, :], f"h{t}", 1, 8 + t, xn[:, t % 2, :], f"xn{t % 2}"),
                                          (xn[:, t % 2, :], f"xn{t % 2}", 0, xnT[:, :, j * 128:(j + 1) * 128], f"xnT{j}")))
                        norm_group(items)
                        xnT_res = [f"xnT{j}" for j in range(4)]
                        if ps_ == 0 and m == 0:
                            checkpoint("sA1", [("xnT", xnT[:], [128, 8, 512], BF16, [])])
                        ures = [f"u{c}" for c in range(4)]
                        if m == 0:
                            S.add("pool", lambda e: e.tensor_copy(out=u[:, :, 0:32], in_=uhalo[:]), reads=["uhalo"], writes=["uh"])
                        else:
                            S.add("pool", lambda e: e.tensor_copy(out=u[:, :, 0:32], in_=u[:, :, 512:544]), reads=ures, writes=["uh"])
                        def gen_diag(c):
                            S.add("dve", lambda e, c=c: e.tensor_tensor(
                                out=dg[:, c % 2, :, :], in0=ident[:].unsqueeze(1).broadcast_to([128, CW, 128]),
                                in1=convw[:, c, :].unsqueeze(2).broadcast_to([128, CW, 128]), op=ALU.mult),
                                reads=["ident", "convw", "const"], writes=[f"dg{c % 2}"])
                        gen_diag(0)
                        gen_diag(1)
                        for c in range(4):
                            ba, bb = (0, 1) if c % 2 == 0 else (2, 3)

                            def mab(e, c=c, ba=ba, bb=bb):
                                for k in range(8):
                                    e.matmul(PF[:, ba, :], lhsT=w_in_v[:, k, c * 128:(c + 1) * 128], rhs=xnT[:, k, :],
                                             start=(k == 0), stop=(k == 7))
                                for k in range(8):
                                    ins = e.matmul(PF[:, bb, :], lhsT=w_in_v[:, k, 512 + c * 128:512 + (c + 1) * 128],
                                                   rhs=xnT[:, k, :], start=(k == 0), stop=(k == 7))
                                return ins
                            S.add("pe", mab, reads=xnT_res + ["R0a", "R0b", "R1a"], writes=[f"PF{ba}", f"PF{bb}"])
                            S.add("act", lambda e, bb=bb: e.activation(out=tht[:], in_=PF[:, bb, :], func=AF.Tanh, scale=0.5),
                                  reads=[f"PF{bb}"], writes=["tht"])
                            S.add("dve", lambda e, c=c, ba=ba: e.scalar_tensor_tensor(out=u[:, c, 32:544], in0=tht[:], scalar=1.0,
                                                                                     in1=PF[:, ba, :], op0=ALU.add, op1=ALU.mult),
                                  reads=["tht", f"PF{ba}", "uh"], writes=[f"u{c}"])
                        if m == NM - 1:
                            S.add("pool", lambda e: e.tensor_copy(out=uhalo[:], in_=u[:, :, 512:544]), reads=ures, writes=["uhalo"])
                        if ps_ == 0 and m == 0:
                            checkpoint("sA2", [("u", u[:], [128, 4, 544], BF16, [])])
                        for c in range(4):
                            def mconv(e, c=c):
                                for tp in range(CW):
                                    ins = e.matmul(PF[:, c, :], lhsT=dg[:, c % 2, tp, :], rhs=u[:, c, 2 + tp:514 + tp],
                                                   start=(tp == 0), stop=(tp == CW - 1))
                                return ins
                            S.add("pe", mconv, reads=[f"u{c}", "uh", f"dg{c % 2}"], writes=[f"PF{c}"])
                            if c + 2 < 4:
                                gen_diag(c + 2)
                        if ps_ == 0 and m == 0:
                            checkpoint("sA3", [("u", u[:], [128, 4, 544], BF16, []), ("dg", dg[:], [128, 2, CW, 128], BF16, [])])
                        for c in range(4):
                            S.add("act", lambda e, c=c: e.activation(out=ysqb[:, c, :], in_=PF[:, c, :], func=AF.Square,
                                                                     bias=convb[:, c:c + 1]),
                                  reads=[f"PF{c}", "const"], writes=[f"ysqb{c}"])
                            S.add("dve", lambda e, c=c: e.tensor_scalar(out=ybf[:, c, :], in0=PF[:, c, :], scalar1=convb[:, c:c + 1],
                                                                        scalar2=None, op0=ALU.add),
                                  reads=[f"PF{c}", "const"], writes=[f"ybf{c}"])

                        def mstat(e):
                            for c in range(4):
                                e.matmul(PF[:, 4, :], lhsT=onesm[:], rhs=ybf[:, c, :], start=(c == 0), stop=(c == 3))
                            for c in range(4):
                                ins = e.matmul(PF[:, 5, :], lhsT=onesm[:], rhs=ysqb[:, c, :], start=(c == 0), stop=(c == 3))
                            return ins
                        S.add("pe", mstat, reads=[f"ybf{c}" for c in range(4)] + [f"ysqb{c}" for c in range(4)] + ["const"],
                              writes=["PF4", "PF5"])
                        S.add("act", lambda e: e.activation(out=t_a[:], in_=PF[:, 4, :], func=AF.Square), reads=["PF4"], writes=["t_a"])
                        S.add("act", lambda e: e.activation(out=tht[:], in_=PF[:, 4, :], func=AF.Copy), reads=["PF4"], writes=["tht"])
                        S.add("dve", lambda e: e.tensor_tensor(out=t_b[:], in0=PF[:, 5, :], in1=t_a[:], op=ALU.subtract),
                              reads=["PF5", "t_a"], writes=["t_b"])
                        S.add("dve", lambda e: e.tensor_scalar(out=t_b[:], in0=t_b[:], scalar1=0.0, scalar2=EPS, op0=ALU.max, op1=ALU.add),
                              reads=["t_b"], writes=["t_b"])
                        S.add("act", lambda e: e.activation(out=t_b[:], in_=t_b[:], func=AF.Ln), reads=["t_b"], writes=["t_b"])
                        S.add("act", lambda e: e.activation(out=t_b[:], in_=t_b[:], func=AF.Exp, scale=-0.5), reads=["t_b"], writes=["t_b"])
                        for c in range(4):
                            S.add("dve", lambda e, c=c: e.scalar_tensor_tensor(out=tmpn[:, c % 2, :], in0=PF[:, c, :],
                                                                               scalar=convb[:, c:c + 1], in1=tht[:],
                                                                               op0=ALU.add, op1=ALU.subtract),
                                  reads=[f"PF{c}", "tht", "const"], writes=[f"tmpn{c % 2}"])
                            S.add("dve", lambda e, c=c: e.tensor_tensor(out=tmpn[:, c % 2, :], in0=tmpn[:, c % 2, :], in1=t_b[:], op=ALU.mult),
                                  reads=[f"tmpn{c % 2}", "t_b"], writes=[f"tmpn{c % 2}"])
                            S.add("act", lambda e, c=c: e.activation(out=yconv[:, c, :], in_=tmpn[:, c % 2, :], func=AF.Silu,
                                                                     scale=lnw[:, c:c + 1], bias=lnb[:, c:c + 1]),
                                  reads=[f"tmpn{c % 2}", "const"], writes=[f"yconv{c}"])
                        if ps_ == 0 and m == 0:
                            checkpoint("sA4", [("yconv", yconv[:], [128, 4, 512], BF16, [])])
                        def x_pe(j):
                            tsl = slice(j * 128, (j + 1) * 128)

                            def mqkvg(e, tsl=tsl):
                                for (bk, c0) in ((0, 1024), (1, 1536), (2, 2048)):
                                    for k in range(8):
                                        ins = e.matmul(PF[:, bk, :], lhsT=xnT[:, k, tsl], rhs=w_in_v[:, k, c0:c0 + 512],
                                                       start=(k == 0), stop=(k == 7))
                                return ins
                            S.add("pe", mqkvg, reads=[f"xnT{j}", "R0a", "R0b", "R1a"], writes=["PF0", "PF1", "PF2"])

                        def x_ew(j):
                            sl = j % 2
                            gt = T0 + m * 4 + j
                            S.add("act", lambda e, sl=sl: e.activation(out=vb[:, sl, :], in_=PF[:, 1, :], func=AF.Copy),
                                  reads=["PF1"], writes=[f"vb{sl}"])
                            S.add("act", lambda e, sl=sl: e.activation(out=gs[:, sl, :], in_=PF[:, 2, :], func=AF.Silu),
                                  reads=["PF2"], writes=[f"gs{sl}"])
                            S.add("pool", lambda e, sl=sl: e.tensor_tensor(out=gs[:, sl, :], in0=gs[:, sl, :], in1=gnw[:], op=ALU.mult),
                                  reads=[f"gs{sl}", "const"], writes=[f"gs{sl}"])
                        def x_rot(j):
                            sl = j % 2
                            gt = T0 + m * 4 + j
                            rotary(PF[:, 0, :].rearrange("p (h d) -> p h d", h=16), cosm[:, gt, :], sinm[:, gt, :], 16,
                                   rtaA[:], rtbA[:], rot[:].rearrange("p (h d) -> p h d", h=16), ["PF0", "cosm", "sinm"], "rot")
                            S.add("act", lambda e, sl=sl: e.activation(out=qkb[:, sl, :], in_=rot[:], func=AF.Copy),
                                  reads=["rota", "rotb"], writes=[f"qkb{sl}"])
                            S.add("pool", lambda e, sl=sl: e.tensor_tensor(
                                out=kz[:, sl, :].rearrange("p (h d) -> p h d", h=8),
                                in0=rot[:, 256:512].rearrange("p (h d) -> p h d", h=8),
                                in1=zeta[:].unsqueeze(2).broadcast_to([128, 8, 32]), op=ALU.mult),
                                reads=["rota", "rotb", "const"], writes=[f"kz{sl}"])

                        def y_1a(j):
                            sl = j % 2

                            def trqk(e, sl=sl):
                                for i in range(8):
                                    ins = e.transpose(out=PT[0:64, 1, i * 128:(i + 1) * 128], in_=qkb[:, sl, i * 64:(i + 1) * 64],
                                                      identity=ident[:])
                                return ins
                            S.add("pe", trqk, reads=[f"qkb{sl}", "ident"], writes=["PT1"])
                            S.add("act", lambda e: e.activation(out=qkT[:], in_=PT[0:64, 1, :].rearrange("p (i n) -> p i n", i=8),
                                                                func=AF.Copy), reads=["PT1"], writes=["qkT"])
                            S.add("dve", lambda e: e.tensor_tensor(out=qx[:], in0=qkT[:, 0:4, :], in1=xifm[:], op=ALU.mult),
                                  reads=["qkT", "const"], writes=["qx"])

                        def y_1b(j):
                            def msc(e):
                                for hh in range(8):
                                    a, hp = hh % 2, hh // 2
                                    ins = e.matmul(PF[:, 3 + a, hp * 128:(hp + 1) * 128],
                                                   lhsT=qkT[32 * a:32 * a + 32, 4 + hp, :], rhs=qkT[32 * a:32 * a + 32, hp, :],
                                                   start=True, stop=True, skip_group_check=True)
                                return ins
                            S.add("pe", msc, reads=["qkT"], writes=["PF3", "PF4"])
                            for half in range(2):
                                S.add("dve", lambda e, half=half: e.tensor_tensor(
                                    out=scm[:].rearrange("p (hp a) n -> p hp a n", a=2)[:, :, half, :],
                                    in0=PF[:, 3 + half, :].rearrange("p (a b) -> p a b", a=4),
                                    in1=mask[:].rearrange("p (hp a) n -> p hp a n", a=2)[:, :, half, :], op=ALU.mult),
                                    reads=[f"PF{3 + half}", "const"], writes=[f"scm{half}"])

                        def y_1c(j):
                            sl = j % 2

                            def my(e, sl=sl):
                                for hp in range(4):
                                    e.matmul(PF[:, 5, hp * 128:(hp + 1) * 128], lhsT=qx[:, hp, :], rhs=Sbf[:, hp, :],
                                             start=True, stop=False, skip_group_check=True)
                                    for a in range(2):
                                        hh = 2 * hp + a
                                        ins = e.matmul(PF[:, 5, hh * 64:(hh + 1) * 64], lhsT=scm[:, hh, :],
                                                       rhs=vb[:, sl, hh * 64:(hh + 1) * 64], start=False, stop=(a == 1),
                                                       skip_group_check=True)
                                return ins
                            S.add("pe", my, reads=["scm0", "scm1", f"vb{sl}", "qx", "Sbf"], writes=["PF5"])

                            def msu(e, sl=sl):
                                for hp in range(4):
                                    ins = e.matmul(PF[0:64, 3, hp * 128:(hp + 1) * 128], lhsT=kz[:, sl, hp * 64:(hp + 1) * 64],
                                                   rhs=vb[:, sl, hp * 128:(hp + 1) * 128], start=True, stop=True, skip_group_check=True)
                                return ins
                            S.add("pe", msu, reads=[f"kz{sl}", f"vb{sl}"], writes=["PF3"])
                            S.add("dve", lambda e: e.tensor_tensor(out=Sst[:], in0=Sst[:], in1=g128[:], op=ALU.mult),
                                  reads=["Sst", "const"], writes=["Sst"])
                            for a in range(2):
                                S.add("dve", lambda e, a=a: e.tensor_tensor(
                                    out=Sst[32 * a:32 * a + 32, :, :], in0=Sst[32 * a:32 * a + 32, :, :],
                                    in1=PF[32 * a:32 * a + 32, 3, :].rearrange("p (hp x) -> p hp x", hp=4)[:, :, 64 * a:64 * a + 64],
                                    op=ALU.add), reads=["Sst", "PF3"], writes=["Sst"])
                            for a in range(2):
                                S.add("act", lambda e, a=a: e.activation(out=Sbf[32 * a:32 * a + 32, :, 64 * a:64 * a + 64],
                                                                         in_=Sst[32 * a:32 * a + 32, :, :], func=AF.Copy),
                                      reads=["Sst"], writes=["Sbf"])

                        def y_2(j):
                            sl = j % 2
                            Y3 = PF[:, 5, :].rearrange("p (h e) -> p h e", h=8)
                            S.add("dve", lambda e, Y3=Y3: e.tensor_reduce(out=gst[:, 0:8], in_=Y3, axis=AX.X, op=ALU.add),
                                  reads=["PF5"], writes=["gstA"])
                            S.add("act", lambda e: e.activation(out=t_a[:], in_=PF[:, 5, :], func=AF.Square), reads=["PF5"], writes=["t_a"])
                            S.add("dve", lambda e: e.tensor_reduce(out=gst[:, 8:16], in_=t_a[:].rearrange("p (h e) -> p h e", h=8),
                                                                   axis=AX.X, op=ALU.add), reads=["t_a"], writes=["gstB"])
                            S.add("dve", lambda e: e.tensor_scalar(out=gst[:, 0:16], in0=gst[:, 0:16], scalar1=1.0 / DV, scalar2=None,
                                                                   op0=ALU.mult), reads=["gstA", "gstB"], writes=["gstA", "gstB"])
                            S.add("dve", lambda e: e.tensor_tensor(out=gst[:, 16:24], in0=gst[:, 0:8], in1=gst[:, 0:8], op=ALU.mult),
                                  reads=["gstA"], writes=["gst2"])
                            S.add("dve", lambda e: e.scalar_tensor_tensor(out=gst[:, 16:24], in0=gst[:, 8:16], scalar=EPS, in1=gst[:, 16:24],
                                                                          op0=ALU.add, op1=ALU.subtract),
                                  reads=["gstB", "gst2"], writes=["gst2"])
                            S.add("pool", lambda e: e.tensor_tensor(out=gst[:, 16:24], in0=gst[:, 16:24], in1=neghalf[:, 0:8], op=ALU.pow),
                                  reads=["gst2", "neghalf"], writes=["gst2"])
                            yn3 = t_b[:].rearrange("p (h e) -> p h e", h=8)
                            S.add("dve", lambda e, Y3=Y3, yn3=yn3: e.tensor_tensor(
                                out=yn3, in0=Y3, in1=gst[:, 0:8].unsqueeze(2).broadcast_to([128, 8, 64]), op=ALU.subtract),
                                reads=["PF5", "gstA"], writes=["t_b"])
                        def y_2b(j):
                            sl = j % 2
                            yn3 = t_b[:].rearrange("p (h e) -> p h e", h=8)
                            S.add("dve", lambda e, yn3=yn3: e.tensor_tensor(
                                out=yn3, in0=yn3, in1=gst[:, 16:24].unsqueeze(2).broadcast_to([128, 8, 64]), op=ALU.mult),
                                reads=["t_b", "gst2"], writes=["t_b"])
                            S.add("dve", lambda e, sl=sl: e.tensor_tensor(out=yret[:], in0=t_b[:], in1=gs[:, sl, :], op=ALU.mult),
                                  reads=["t_b", f"gs{sl}"], writes=["yret"])

                            def tryr(e):
                                for i in range(4):
                                    ins = e.transpose(out=PT[:, 0, i * 128:(i + 1) * 128], in_=yret[:, i * 128:(i + 1) * 128],
                                                      identity=ident[:])
                                return ins
                            S.add("pe", tryr, reads=["yret", "ident"], writes=["PT0"])
                            S.add("act", lambda e: e.activation(out=yretT[:], in_=PT[:, 0, 0:512].rearrange("p (i n) -> p i n", i=4),
                                                                func=AF.Copy), reads=["PT0"], writes=["yretT"])

                        def y_3h(j, n):
                            tsl = slice(j * 128, (j + 1) * 128)

                            def mwo(e, tsl=tsl, n=n):
                                for kc in range(8):
                                    lt = yconv[:, kc, tsl] if kc < 4 else yretT[:, kc - 4, :]
                                    ins = e.matmul(PF[:, n, :], lhsT=lt, rhs=w_out_v[:, kc, n * 512:(n + 1) * 512],
                                                   start=(kc == 0), stop=(kc == 7))
                                return ins
                            S.add("pe", mwo, reads=[f"yconv{c}" for c in range(4)] + ["yretT", "R1b"], writes=[f"PF{n}"])

                        def y_3c(j):
                            t = m * 4 + j
                            S.add("dve", lambda e, t=t: e.tensor_tensor(out=h[:, t, :], in0=h[:, t, :],
                                                                        in1=PF[:, 0:2, :].rearrange("p a b -> p (a b)"), op=ALU.add),
                                  reads=[f"h{t}", "PF0", "PF1"], writes=[f"h{t}"])

                        x_pe(0)
                        x_ew(0)
                        x_rot(0)
                        for j in range(4):
                            y_1a(j)
                            if j > 0:
                                y_3h(j - 1, 0)
                            y_1b(j)
                            if j > 0:
                                y_3h(j - 1, 1)
                                y_3c(j - 1)
                            y_1c(j)
                            if j + 1 < 4:
                                x_pe(j + 1)
                            y_2(j)
                            if j + 1 < 4:
                                x_ew(j + 1)
                                x_rot(j + 1)
                            y_2b(j)
                        y_3h(3, 0)
                        y_3h(3, 1)
                        y_3c(3)
                    checkpoint(f"A{ps_}", [("h", h[:], [128, NTP, D], F32, [])])
                    S.barrier()

                with contextlib.ExitStack() as st:
                    RSTD_MODE[0] = "act"
                    W2 = sbuf(st, "W2", [128, 16384], BF16)
                    xq_v = W2[:, 0:8192].rearrange("p (k n) -> p k n", k=8)
                    xo_v = W2[:, 8192:16384].rearrange("p (k n) -> p k n", k=8)
                    load_g(0, 1)
                    load_weight("k_xq", xq_v, xq_w, 0, D, 0, 1024, ["R2a"])
                    load_weight("k_xo", xo_v, xo_w, 0, D, 0, 1024, ["R2b"])

                    def slot_views(s_):
                        if s_ < 2:
                            base = W[:, s_ * 16384:(s_ + 1) * 16384]
                        else:
                            base = W2[:, :]
                        return (base[:, 0:8192].rearrange("p (k n) -> p k n", k=8),
                                base[:, 8192:16384].rearrange("p (k n) -> p k n", k=8))

                    def load_quarter(qd, s_):
                        upv, dnv = slot_views(s_)
                        load_weight(f"k_u{s_}", upv, up_w, 0, D, qd * 1024, 1024, [f"R{s_}a"])
                        load_weight(f"k_d{s_}", dnv, down_w, qd * 1024, 1024, 0, 1024, [f"R{s_}b"])

                    with contextlib.ExitStack() as stb:
                        xnb = sbuf(stb, "xnB", [128, 2, D], BF16)
                        xnTb = sbuf(stb, "xnTB", [128, 8, 512], BF16)
                        qT = sbuf(stb, "qT", [128, 8, 512], BF16)
                        pn = sbuf(stb, "pn", [128, 2, 4, 256], BF16)
                        pTm = sbuf(stb, "pTm", [128, 8, 512], BF16)
                        oT = sbuf(stb, "oT", [128, 8, 512], BF16)
                        sst = sbuf(stb, "sst", [128, 2, 16], F32)
                        for m in range(NM):
                            items = []
                            for j in range(4):
                                t = m * 4 + j
                                items.append(((h[:, t, :], f"h{t}", 0, 16 + t, xnb[:, t % 2, :], f"xnb{t % 2}"),
                                              (xnb[:, t % 2, :], f"xnb{t % 2}", 0, xnTb[:, :, j * 128:(j + 1) * 128], f"xnTb{j}")))
                            norm_group(items)
                            xnT_res = [f"xnTb{j}" for j in range(4)]
                            load_quarter(m, m)
                            for c in range(8):
                                bk = c % 2

                                def mq(e, c=c, bk=bk):
                                    for k in range(8):
                                        ins = e.matmul(PF[:, bk, :], lhsT=xq_v[:, k, c * 128:(c + 1) * 128], rhs=xnTb[:, k, :],
                                                       start=(k == 0), stop=(k == 7))
                                    return ins
                                S.add("pe", mq, reads=xnT_res + ["R2a"], writes=[f"PF{bk}"])
                                if c % 2 == 0:
                                    S.add("act", lambda e, c=c, bk=bk: e.activation(out=qT[:, c, :], in_=PF[:, bk, :], func=AF.Copy),
                                          reads=[f"PF{bk}"], writes=[f"qT{c}"])
                                else:
                                    S.add("dve", lambda e, c=c, bk=bk: e.tensor_copy(out=qT[:, c, :], in_=PF[:, bk, :]),
                                          reads=[f"PF{bk}"], writes=[f"qT{c}"])
                            def b_msc(j):
                                tsl = slice(j * 128, (j + 1) * 128)
                                b0 = 2 + 2 * (j % 2)

                                def msc(e, tsl=tsl, b0=b0):
                                    for hd in range(4):
                                        for dd in range(2):
                                            ins = e.matmul(PF[:, b0 + hd // 2, (hd % 2) * 256:(hd % 2 + 1) * 256],
                                                           lhsT=qT[:, 2 * hd + dd, tsl], rhs=kTm[:, 2 * hd + dd, :],
                                                           start=(dd == 0), stop=(dd == 1), skip_group_check=True)
                                    return ins
                                S.add("pe", msc, reads=[f"qT{c}" for c in range(8)] + ["kTm"], writes=[f"PF{b0}", f"PF{b0 + 1}"])

                            def b_soft(j):
                                tsl = slice(j * 128, (j + 1) * 128)
                                par = j % 2
                                b0 = 2 + 2 * par
                                SC3 = PF[:, b0:b0 + 2, :].rearrange("p a (b m) -> p (a b) m", b=2)
                                S.add("dve", lambda e, SC3=SC3, par=par: e.tensor_reduce(out=sst[:, par, 0:4], in_=SC3, axis=AX.X, op=ALU.max),
                                      reads=[f"PF{b0}", f"PF{b0 + 1}"], writes=[f"sst0{par}"])
                                S.add("dve", lambda e, par=par: e.tensor_scalar(out=sst[:, par, 4:8], in0=sst[:, par, 0:4], scalar1=-1.0 / 16.0,
                                                                                 scalar2=None, op0=ALU.mult),
                                      reads=[f"sst0{par}"], writes=[f"sst1{par}"])
                                for hd in range(4):
                                    S.add("act", lambda e, hd=hd, par=par, b0=b0: e.activation(
                                        out=pn[:, par, hd, :], in_=PF[:, b0 + hd // 2, (hd % 2) * 256:(hd % 2 + 1) * 256], func=AF.Exp,
                                        scale=1.0 / 16.0, bias=sst[:, par, 4 + hd:5 + hd], accum_out=sst[:, par, 8 + hd:9 + hd]),
                                        reads=[f"PF{b0 + hd // 2}", f"sst1{par}"], writes=[f"pn{par}_{hd}", f"sst2{par}_{hd}"])
                                S.add("dve", lambda e, par=par: e.reciprocal(out=sst[:, par, 12:16], in_=sst[:, par, 8:12]),
                                      reads=[f"sst2{par}_{hd}" for hd in range(4)], writes=[f"sst3{par}"])
                                S.add("dve", lambda e, par=par: e.tensor_tensor(
                                    out=pn[:, par, :, :], in0=pn[:, par, :, :],
                                    in1=sst[:, par, 12:16].unsqueeze(2).broadcast_to([128, 4, 256]), op=ALU.mult),
                                    reads=[f"pn{par}_{hd}" for hd in range(4)] + [f"sst3{par}"], writes=[f"pnn{par}"])

                                def trp(e, par=par):
                                    for hd in range(4):
                                        for mc in range(2):
                                            i = hd * 2 + mc
                                            ins = e.transpose(out=PT[:, par, i * 128:(i + 1) * 128],
                                                              in_=pn[:, par, hd, mc * 128:(mc + 1) * 128], identity=ident[:])
                                    return ins
                                S.add("pe", trp, reads=[f"pnn{par}", "ident"] + [f"pn{par}_{hd}" for hd in range(4)], writes=[f"PT{par}"])
                                S.add("act", lambda e, tsl=tsl, par=par: e.activation(out=pTm[:, :, tsl],
                                                                                      in_=PT[:, par, :].rearrange("p (i n) -> p i n", i=8),
                                                                                      func=AF.Copy), reads=[f"PT{par}"], writes=[f"pTm{j}"])
                            for s_i in range(5):
                                if s_i < 4:
                                    b_msc(s_i)
                                if s_i >= 1:
                                    b_soft(s_i - 1)
                            for c in range(8):
                                hd = c // 2
                                bk = c % 2

                                def mpv(e, c=c, hd=hd, bk=bk):
                                    for mc in range(2):
                                        ins = e.matmul(PF[:, bk, :], lhsT=vm[:, mc, c * 128:(c + 1) * 128], rhs=pTm[:, hd * 2 + mc, :],
                                                       start=(mc == 0), stop=(mc == 1))
                                    return ins
                                S.add("pe", mpv, reads=[f"pTm{j}" for j in range(4)] + ["vm"], writes=[f"PF{bk}"])
                                if c % 2 == 0:
                                    S.add("act", lambda e, c=c, bk=bk: e.activation(out=oT[:, c, :], in_=PF[:, bk, :], func=AF.Copy),
                                          reads=[f"PF{bk}"], writes=[f"oT{c}"])
                                else:
                                    S.add("dve", lambda e, c=c, bk=bk: e.tensor_copy(out=oT[:, c, :], in_=PF[:, bk, :]),
                                          reads=[f"PF{bk}"], writes=[f"oT{c}"])
                            for j in range(4):
                                t = m * 4 + j
                                tsl = slice(j * 128, (j + 1) * 128)
                                b0 = 2 + 2 * (j % 2)

                                def mxo(e, tsl=tsl, b0=b0):
                                    for n in range(2):
                                        for kc in range(8):
                                            ins = e.matmul(PF[:, b0 + n, :], lhsT=oT[:, kc, tsl], rhs=xo_v[:, kc, n * 512:(n + 1) * 512],
                                                           start=(kc == 0), stop=(kc == 7))
                                    return ins
                                S.add("pe", mxo, reads=[f"oT{c}" for c in range(8)] + ["R2b"], writes=[f"PF{b0}", f"PF{b0 + 1}"])
                                S.add("dve", lambda e, t=t, b0=b0: e.tensor_tensor(out=h[:, t, :], in0=h[:, t, :],
                                                                                   in1=PF[:, b0:b0 + 2, :].rearrange("p a b -> p (a b)"),
                                                                                   op=ALU.add),
                                      reads=[f"h{t}", f"PF{b0}", f"PF{b0 + 1}"], writes=[f"h{t}"])
                        checkpoint(f"B{ps_}", [("h", h[:], [128, NTP, D], F32, [])])
                        S.barrier()

                    with contextlib.ExitStack() as stc:
                        xnc = sbuf(stc, "xnC", [128, 2, D], BF16)
                        xnTa = sbuf(stc, "xnTa", [128, 8, TH], BF16)
                        rl = sbuf(stc, "rl", [128, 2, 512], F32)
                        hT = sbuf(stc, "hT", [128, 8, 512], BF16)
                        ot = sbuf(stc, "ot", [128, 2, D], F32)
                        load_g(1, 2)
                        load_g(0, 3)
                        items = []
                        for t in range(NTP):
                            items.append(((h[:, t, :], f"h{t}", 1, 24 + t, xnc[:, t % 2, :], f"xnc{t % 2}"),
                                          (xnc[:, t % 2, :], f"xnc{t % 2}", t % 2, xnTa[:, :, t * 128:(t + 1) * 128], f"xnTa{t}")))
                        norm_group(items)
                        load_quarter(3, 2)
                        def final_tile(t):
                            gt = T0 + t
                            col = 32 + t
                            S.add("act", lambda e, t=t, col=col: e.activation(out=junk[:], in_=h[:, t, :], func=AF.Square,
                                                                              accum_out=ss[:, col:col + 1]),
                                  reads=[f"h{t}"], writes=["junk", f"ss{col}"])
                            S.add("act", lambda e, col=col: e.activation(out=rstd[:, col:col + 1], in_=ss[:, col:col + 1], func=AF.Ln,
                                                                         scale=1.0 / D, bias=EPS),
                                  reads=[f"ss{col}"], writes=[f"rs{col}"])
                            S.add("act", lambda e, col=col: e.activation(out=rstd[:, col:col + 1], in_=rstd[:, col:col + 1], func=AF.Exp,
                                                                         scale=-0.5),
                                  reads=[f"rs{col}"], writes=[f"rs{col}"])
                            S.add("dve", lambda e, t=t, col=col: e.scalar_tensor_tensor(
                                out=ot[:, t % 2, :], in0=h[:, t, :], scalar=rstd[:, col:col + 1], in1=gsl[:, 0, :],
                                op0=ALU.mult, op1=ALU.mult), reads=[f"h{t}", f"rs{col}", "g0"], writes=[f"ot{t % 2}"])

                            def sto(e, s, t=t, gt=gt):
                                e.dma_start(out=out_d[gt * 128:(gt + 1) * 128, :], in_=ot[:, t % 2, :]).then_inc(s, 16)
                            dma("sp", f"st{t % 2}", sto, 1, reads=[f"ot{t % 2}"], writes=[f"out{t % 2}"])

                        qslots = [0, 1, 0, 2]
                        for qd in range(4):
                            s_ = qslots[qd]
                            if qd == 1:
                                load_quarter(2, 0)
                            if ps_ + 1 < NPASS and qd == 2:
                                load_weight("k_out", w_out_v, w_out, 0, D, 0, 1024, ["R1b"])
                            if ps_ + 1 < NPASS and qd == 3:
                                load_weight("k_in", w_in_v, w_in, 0, D, 0, 2560, ["R0a", "R0b", "R1a"])
                            upv, dnv = slot_views(s_)
                            for m in range(NM):
                                for f in range(8):
                                    bk = f % 2

                                    def mup(e, f=f, bk=bk, m=m, upv=upv):
                                        for k in range(8):
                                            ins = e.matmul(PF[:, bk, :], lhsT=upv[:, k, f * 128:(f + 1) * 128],
                                                           rhs=xnTa[:, k, m * 512:(m + 1) * 512], start=(k == 0), stop=(k == 7))
                                        return ins
                                    S.add("pe", mup, reads=[f"xnTa{m * 4 + j}" for j in range(4)] + [f"R{s_}a"], writes=[f"PF{bk}"])
                                    S.add("act", lambda e, f=f, bk=bk: e.activation(out=rl[:, f % 2, :], in_=PF[:, bk, :], func=AF.Relu),
                                          reads=[f"PF{bk}"], writes=[f"rl{f % 2}"])
                                    if f % 2 == 0:
                                        S.add("act", lambda e, f=f: e.activation(out=hT[:, f, :], in_=rl[:, f % 2, :], func=AF.Square),
                                              reads=[f"rl{f % 2}"], writes=[f"hT{f}"])
                                    else:
                                        S.add("dve", lambda e, f=f: e.tensor_tensor(out=hT[:, f, :], in0=rl[:, f % 2, :], in1=rl[:, f % 2, :],
                                                                                    op=ALU.mult), reads=[f"rl{f % 2}"], writes=[f"hT{f}"])
                                for j in range(4):
                                    t = m * 4 + j
                                    tsl = slice(j * 128, (j + 1) * 128)

                                    db = 2 + 2 * (j % 2)

                                    def mdn(e, tsl=tsl, dnv=dnv, db=db):
                                        for n in range(2):
                                            for f in range(8):
                                                ins = e.matmul(PF[:, db + n, :], lhsT=hT[:, f, tsl], rhs=dnv[:, f, n * 512:(n + 1) * 512],
                                                               start=(f == 0), stop=(f == 7))
                                        return ins
                                    S.add("pe", mdn, reads=[f"hT{f}" for f in range(8)] + [f"R{s_}b"], writes=[f"PF{db}", f"PF{db + 1}"])
                                    S.add("dve", lambda e, t=t, db=db: e.tensor_tensor(out=h[:, t, :], in0=h[:, t, :],
                                                                                       in1=PF[:, db:db + 2, :].rearrange("p a b -> p (a b)"),
                                                                                       op=ALU.add),
                                          reads=[f"h{t}", f"PF{db}", f"PF{db + 1}"], writes=[f"h{t}"])
                                    if qd == 3:
                                        final_tile(t)
                                        if ps_ + 1 < NPASS:
                                            def ldh2(e, s, t=t, gt2=T0 + NTP + t):
                                                e.dma_start(out=h[:, t, :], in_=xmain[gt2 * 128:(gt2 + 1) * 128, :]).then_inc(s, 16)
                                            dma("sp", f"h{t}", ldh2, 1, reads=[f"ot{t % 2}"], writes=[f"h{t}"])
                        S.barrier()


            for ps_ in range(NPASS):
                run_pass(ps_)

      except _Stop:
          pass
      if True:
        S.add("sp", None, reads=["out0", "out1"])

        dsems = {k: es.enter_context(nc.semaphore("d_" + k)) for k in sorted(dma_key_names)}
        block = es.enter_context(nc.Block())
        S.emit({"pe": block.tensor, "act": block.scalar, "dve": block.vector, "pool": block.gpsimd, "sp": block.sync},
               sems, dsems)
    return nc, S


_CACHE = {}
STOP = None
DBG_OUT = []


class _Stop(Exception):
    pass


def kernel(x, mem, positions, norm_mix_w, w_in, conv_w, conv_b, conv_ln_w, conv_ln_b, ret_gn_w, w_out,
           norm_xattn_w, norm_mem_w, xq_w, xkv_w, xo_w, norm_mlp_w, mlp_up_w, mlp_down_w, norm_f_w):
    f32 = np.float32
    x = np.asarray(x, dtype=f32)
    mem = np.asarray(mem, dtype=f32)
    positions = np.asarray(positions).astype(np.int32)
    B = x.shape[0]
    consts = _host_consts()
    shared = {
        "gvec": np.stack([np.asarray(v, dtype=f32) for v in (norm_mix_w, norm_xattn_w, norm_mlp_w, norm_f_w, norm_mem_w)], 0),
        "w_in": np.ascontiguousarray(np.asarray(w_in, dtype=f32)),
        "w_out": np.ascontiguousarray(np.asarray(w_out, dtype=f32)),
        "xq_w": np.ascontiguousarray(np.asarray(xq_w, dtype=f32)),
        "xkv_w": np.ascontiguousarray(np.asarray(xkv_w, dtype=f32)),
        "xo_w": np.ascontiguousarray(np.asarray(xo_w, dtype=f32)),
        "up_w": np.ascontiguousarray(np.asarray(mlp_up_w, dtype=f32)),
        "down_w": np.ascontiguousarray(np.asarray(mlp_down_w, dtype=f32)),
        "convw": np.ascontiguousarray(np.asarray(conv_w, dtype=f32).reshape(CW, 4, 128).transpose(2, 1, 0)),
        "convb": np.ascontiguousarray(np.asarray(conv_b, dtype=f32).reshape(4, 128).T),
        "lnw": np.ascontiguousarray(np.asarray(conv_ln_w, dtype=f32).reshape(4, 128).T),
        "lnb": np.ascontiguousarray(np.asarray(conv_ln_b, dtype=f32).reshape(4, 128).T),
        "gnw": np.asarray(ret_gn_w, dtype=f32).reshape(1, 512),
    }
    shared.update(consts)
    in_maps = []
    npre_tok = NPRE * 128
    for c in range(NCORE):
        b, q = c // 4, c % 4
        t0 = q * TPC
        xpre = np.zeros((npre_tok, D), dtype=f32)
        ppre = np.zeros((npre_tok,), dtype=np.int32)
        if t0 > 0:
            xpre[npre_tok - t0:] = x[b, 0:t0]
            ppre[npre_tok - t0:] = positions[b, 0:t0]
        m = dict(shared)
        m["xmain"] = np.ascontiguousarray(x[b, t0:t0 + TPC])
        m["xpre"] = xpre
        m["posm"] = np.ascontiguousarray(positions[b, t0:t0 + TPC].reshape(TPC // 128, 128).T)
        m["posp"] = np.ascontiguousarray(ppre.reshape(NPRE, 128).T)
        m["memx"] = np.ascontiguousarray(mem[b])
        in_maps.append(m)
    if "nc" not in _CACHE:
        _CACHE["nc"] = build_nc()[0]
    nc = _CACHE["nc"]
    res = run_bass_kernel_spmd(nc, in_maps, core_ids=list(range(NCORE)))
    out = np.zeros((B, SEQ, D), dtype=f32)
    for c in range(NCORE):
        b, q = c // 4, c % 4
        out[b, q * TPC:(q + 1) * TPC] = res.results[c]["out"]
    return out
```

```python
import contextlib
import struct
import numpy as np
import ml_dtypes
import concourse.bass as bass
import concourse.mybir as mybir
from concourse.bass_utils import run_bass_kernel_spmd

F32 = mybir.dt.float32
BF16 = mybir.dt.bfloat16
I32 = mybir.dt.int32
AF = mybir.ActivationFunctionType
ALU = mybir.AluOpType
AX = mybir.AxisListType

D = 1024
SEQ = 8192
NCORE = 8
TPC = 2048
TH = 1024
NPASS = TPC // TH
NTP = TH // 128
NM = TH // 512
NPRE = (SEQ - TPC) // 128
H = 8
DK = 32
DV = 64
CW = 31
DFF = 4096
EPS = 1e-6
PI = float(np.pi)

ENGS = ("pe", "act", "dve", "pool", "sp")


class Op:
    __slots__ = ("eng", "fn", "deps", "needed", "sem", "val", "is_dma", "ndma", "name")

    def __init__(self, eng, fn, name=""):
        self.eng = eng
        self.fn = fn
        self.deps = []
        self.needed = False
        self.sem = None
        self.val = None
        self.is_dma = False
        self.ndma = 0
        self.name = name


class Sched:
    def __init__(self):
        self.ops = {e: [] for e in ENGS}
        self.last_w = {}
        self.readers = {}
        self.dma_keys = {}
        self.frozen = False

    def _dep(self, op, other):
        if other is None or other is op:
            return
        if other.is_dma:
            other = self.dma_keys[other.sem][-1]
            if other is op:
                return
        op.deps.append(other)
        other.needed = True

    def add(self, eng, fn, reads=(), writes=(), name="", dma_key=None, ndma=0):
        if self.frozen:
            return None
        op = Op(eng, fn, name)
        if dma_key is not None:
            op.is_dma = True
            op.ndma = ndma
            op.sem = dma_key
        for r in reads:
            w = self.last_w.get(r)
            if w is not None:
                if not (w.eng == eng and eng == "pe" and not w.is_dma and not op.is_dma):
                    self._dep(op, w)
            if r[:2] in ("PF", "PT"):
                for rd in self.readers.get(r, []):
                    if rd.eng != eng:
                        self._dep(op, rd)
            self.readers.setdefault(r, []).append(op)
        for r in writes:
            w = self.last_w.get(r)
            if w is not None:
                same = (w.eng == eng and eng == "pe" and not w.is_dma and not op.is_dma)
                if not same:
                    self._dep(op, w)
            for rd in self.readers.get(r, []):
                same = (rd.eng == eng and eng == "pe" and not rd.is_dma and not op.is_dma)
                if not same:
                    self._dep(op, rd)
            self.last_w[r] = op
            self.readers[r] = []
        if dma_key is not None:
            self.dma_keys.setdefault(dma_key, []).append(op)
        self.ops[eng].append(op)
        return op

    def barrier(self):
        if self.frozen:
            return
        lasts = []
        for e in ENGS:
            for op in reversed(self.ops[e]):
                if not op.is_dma and op.fn is not None:
                    lasts.append(op)
                    break
        dl = [lst[-1] for lst in self.dma_keys.values() if lst]
        for e in ENGS:
            op = Op(e, None, "barrier")
            for o in lasts:
                if o.eng != e or e != "pe":
                    op.deps.append(o)
                    o.needed = True
            for o in dl:
                op.deps.append(o)
            self.ops[e].append(op)

    def emit(self, block_engines, sems, dma_sems):
        for e in ENGS:
            cnt = 0
            for op in self.ops[e]:
                if op.is_dma:
                    continue
                if op.needed:
                    cnt += 1
                    op.val = cnt
                op.sem = ("eng", e)
        for key, lst in self.dma_keys.items():
            cnt = 0
            for op in lst:
                cnt += 16 * op.ndma
                op.val = cnt
        self.stats = {}

        def semh(op):
            if op.is_dma:
                return dma_sems[op.sem]
            return sems[op.sem[1]]

        def run(e, eng):
            seen = {}
            nwait = 0
            for op in self.ops[e]:
                need = {}
                for d in op.deps:
                    if d.val is None:
                        raise RuntimeError(f"dep {d.name} has no val")
                    if d.sem not in need or need[d.sem][0] < d.val:
                        need[d.sem] = (d.val, semh(d))
                for k, (v, hh) in need.items():
                    if seen.get(k, 0) >= v:
                        continue
                    eng.wait_ge(hh, v)
                    nwait += 1
                    seen[k] = v
                if op.fn is None:
                    continue
                if op.is_dma:
                    op.fn(eng, dma_sems[op.sem])
                else:
                    ins = op.fn(eng)
                    if op.needed:
                        ins.then_inc(sems[e], 1)
            self.stats[e] = (len(self.ops[e]), nwait)

        for e in ENGS:
            if not self.ops[e]:
                continue

            def section(eng, e=e):
                run(e, eng)
            block_engines[e](section)


def _split2pi():
    def trunc(v, bits):
        i = struct.unpack("<I", struct.pack("<f", np.float32(v)))[0]
        i &= ~((1 << (23 - bits)) - 1) & 0xFFFFFFFF
        return struct.unpack("<f", struct.pack("<I", i))[0]
    tp = 2 * np.pi
    c1 = trunc(tp, 8)
    c2 = trunc(tp - c1, 10)
    c3 = float(np.float32(tp - c1 - c2))
    return float(c1), float(c2), c3


C1, C2, C3 = _split2pi()


def _host_consts():
    lg = np.log1p(-np.exp2(-5.0 - np.arange(H, dtype=np.float64)))
    scale = DK ** -0.5
    c = {}
    c["ident"] = np.eye(128, dtype=np.float32).astype(ml_dtypes.bfloat16)
    c["onesm"] = np.full((128, 128), 1.0 / 512.0, dtype=np.float32).astype(ml_dtypes.bfloat16)
    j = np.arange(128)[:, None]
    i = np.arange(128)[None, :]
    cj, ci = j // 64, i // 64
    mask = np.zeros((128, H, 128), dtype=np.float64)
    for h in range(H):
        m = np.where(cj == ci, np.exp(lg[h] * np.abs(i - j)),
                     np.where(cj < ci, np.exp(lg[h] * (i - j)), 0.0))
        mask[:, h, :] = m * scale
    c["mask"] = mask.astype(np.float32)
    xi = np.zeros((64, 4, 128), dtype=np.float64)
    g128 = np.zeros((64, 4, 64), dtype=np.float64)
    for hp in range(4):
        for a in range(2):
            h = 2 * hp + a
            xi[32 * a:32 * a + 32, hp, :] = scale * np.exp(lg[h] * (np.arange(128) + 1.0))[None, :]
            g128[32 * a:32 * a + 32, hp, :] = np.exp(lg[h] * 128.0)
    c["xifm"] = xi.astype(np.float32)
    c["g128"] = g128.astype(np.float32)
    c["zeta"] = np.exp(lg[None, :] * (127.0 - np.arange(128))[:, None]).astype(np.float32)
    dist = (NPRE * 128 - 1) - (np.arange(NPRE)[None, :, None] * 128 + np.arange(128)[:, None, None])
    c["wpre"] = np.exp(lg[None, None, :] * dist).astype(np.float32)
    half = DK // 2
    invf = (np.float32(10000.0) ** (-(np.arange(half, dtype=np.float32)) / np.float32(half))).astype(np.float32)
    c["invf"] = np.broadcast_to(invf[None, :], (128, half)).copy()
    return c


def build_nc():
    nc = bass.Bass("TRN2", target_bir_lowering=False)

    def din(name, shape, dt=F32):
        return nc.dram_tensor(name, list(shape), dt, kind="ExternalInput").ap()

    xmain = din("xmain", [TPC, D])
    xpre = din("xpre", [NPRE * 128, D])
    posm = din("posm", [128, TPC // 128], I32)
    posp = din("posp", [128, NPRE], I32)
    memx = din("memx", [256, D])
    gvec = din("gvec", [5, D])
    w_in = din("w_in", [D, 2560])
    w_out = din("w_out", [D, D])
    xq_w = din("xq_w", [D, D])
    xkv_w = din("xkv_w", [D, 2 * D])
    xo_w = din("xo_w", [D, D])
    up_w = din("up_w", [D, DFF])
    down_w = din("down_w", [DFF, D])
    convw_d = din("convw", [128, 4, CW])
    convb_d = din("convb", [128, 4])
    lnw_d = din("lnw", [128, 4])
    lnb_d = din("lnb", [128, 4])
    gnw_d = din("gnw", [1, 512])
    ident_d = din("ident", [128, 128], BF16)
    onesm_d = din("onesm", [128, 128], BF16)
    mask_d = din("mask", [128, H, 128])
    xifm_d = din("xifm", [64, 4, 128])
    g128_d = din("g128", [64, 4, 64])
    zeta_d = din("zeta", [128, H])
    wpre_d = din("wpre", [128, NPRE, H])
    invf_d = din("invf", [128, 16])
    out_d = nc.dram_tensor("out", [TPC, D], F32, kind="ExternalOutput").ap()

    S = Sched()
    dma_key_names = set()
    del DBG_OUT[:]

    def checkpoint(tag, dumps=()):
        if STOP != tag:
            return
        S.barrier()
        for (name, ap, shape, dt, rres) in dumps:
            dd = nc.dram_tensor("dbg_" + name, list(shape), dt, kind="ExternalOutput").ap()
            DBG_OUT.append("dbg_" + name)

            def fn(e, s, dd=dd, ap=ap):
                e.dma_start(out=dd, in_=ap).then_inc(s, 16)
            dma_key_names.add("dbg")
            S.add("sp", fn, reads=list(rres), writes=["dbgout"], dma_key="dbg", ndma=1)
        S.barrier()
        S.frozen = True

    with contextlib.ExitStack() as es:
      try:
            _cnt = [0]

            def sbuf(stack, name, shape, dt):
                _cnt[0] += 1
                return stack.enter_context(nc.sbuf_tensor(f"sb{_cnt[0]}_{name}", list(shape), dt))

            h = sbuf(es, "h", [128, NTP, D], F32)
            W = sbuf(es, "W", [128, 2 * 16384], BF16)
            gsl = sbuf(es, "gsl", [128, 2, D], F32)
            ident = sbuf(es, "ident", [128, 128], BF16)
            onesm = sbuf(es, "onesm", [128, 128], BF16)
            mask = sbuf(es, "mask", [128, H, 128], F32)
            xifm = sbuf(es, "xifm", [64, 4, 128], F32)
            g128 = sbuf(es, "g128", [64, 4, 64], F32)
            zeta = sbuf(es, "zeta", [128, H], F32)
            invf = sbuf(es, "invf", [128, 16], F32)
            convw = sbuf(es, "convw", [128, 4, CW], F32)
            convb = sbuf(es, "convb", [128, 4], F32)
            lnw = sbuf(es, "lnw", [128, 4], F32)
            lnb = sbuf(es, "lnb", [128, 4], F32)
            gnw = sbuf(es, "gnw", [128, 512], F32)
            neghalf = sbuf(es, "neghalf", [128, 8], F32)
            cosm = sbuf(es, "cosm", [128, TPC // 128, 16], F32)
            sinm = sbuf(es, "sinm", [128, TPC // 128, 16], F32)
            kTm = sbuf(es, "kTm", [128, 8, 256], BF16)
            vm = sbuf(es, "vm", [128, 2, D], BF16)
            Sst = sbuf(es, "Sst", [64, 4, 64], F32)
            Sbf = sbuf(es, "Sbf", [64, 4, 128], BF16)
            uhalo = sbuf(es, "uhalo", [128, 4, 32], BF16)
            ss = sbuf(es, "ss", [128, 64], F32)
            rstd = sbuf(es, "rstd", [128, 64], F32)
            junk = sbuf(es, "junk", [128, D], BF16)

            PT = es.enter_context(nc.psum_tensor("PT", [128, 2, 1024], BF16))
            PF = es.enter_context(nc.psum_tensor("PF", [128, 6, 512], F32))

            sems = {e: es.enter_context(nc.semaphore("s_" + e)) for e in ENGS}

            def dma(eng, key, fn, n, reads=(), writes=()):
                dma_key_names.add(key)
                S.add(eng, fn, reads=reads, writes=writes, dma_key=key, ndma=n)

            def wview(off, k, n):
                return W[:, off:off + k * n].rearrange("p (k n) -> p k n", k=k)

            def load_weight(key, dst3, src, r0, nrows, c0, ncols, wres):
                nk = nrows // 128
                cs = min(ncols, 2048)
                pieces = []
                for k0 in range(0, nk, 2):
                    for cc in range(0, ncols, cs):
                        pieces.append((k0, cc, min(cs, ncols - cc)))

                def fn(e, s):
                    for (k0, cc, cw_) in pieces:
                        srcv = src[r0 + k0 * 128: r0 + (k0 + 2) * 128, c0 + cc: c0 + cc + cw_]
                        e.dma_start(out=dst3[:, k0:k0 + 2, cc:cc + cw_],
                                    in_=srcv.rearrange("(k p) n -> p k n", p=128)).then_inc(s, 16)
                dma("pool", key, fn, len(pieces), writes=wres)

            def load_g(slot, row):
                def fn(e, s):
                    e.dma_start(out=gsl[:, slot, :], in_=gvec[row:row + 1, :].broadcast_to([128, D])).then_inc(s, 16)
                dma("sp", f"g{slot}", fn, 1, writes=[f"g{slot}"])

            def norm_tile(src, src_res, slot, col, xn_t, xn_res, pt_bank, dst, dst_res, extra_reads=()):
                norm_tile_a(src, src_res, slot, col, xn_t, xn_res, extra_reads)
                norm_tile_b(xn_t, xn_res, pt_bank, dst, dst_res)

            def norm_group(items):
                n = len(items)
                for i in range(n + 1):
                    if i < n:
                        norm_tile_a(*items[i][0])
                    if i >= 1:
                        norm_tile_b(*items[i - 1][1])

            RSTD_MODE = ["pool"]

            def norm_tile_a(src, src_res, slot, col, xn_t, xn_res, extra_reads=()):
                S.add("act", lambda e: e.activation(out=junk[:], in_=src, func=AF.Square, accum_out=ss[:, col:col + 1]),
                      reads=[src_res] + list(extra_reads), writes=["junk", f"ss{col}"])
                if RSTD_MODE[0] == "pool":
                    S.add("pool", lambda e: e.tensor_scalar(out=rstd[:, col:col + 1], in0=ss[:, col:col + 1], scalar1=1.0 / D,
                                                            scalar2=EPS, op0=ALU.mult, op1=ALU.add),
                          reads=[f"ss{col}"], writes=[f"rs{col}"])
                    S.add("pool", lambda e: e.tensor_tensor(out=rstd[:, col:col + 1], in0=rstd[:, col:col + 1],
                                                            in1=neghalf[:, 0:1], op=ALU.pow),
                          reads=[f"rs{col}", "neghalf"], writes=[f"rs{col}"])
                else:
                    S.add("act", lambda e: e.activation(out=rstd[:, col:col + 1], in_=ss[:, col:col + 1], func=AF.Ln,
                                                        scale=1.0 / D, bias=EPS),
                          reads=[f"ss{col}"], writes=[f"rs{col}"])
                    S.add("act", lambda e: e.activation(out=rstd[:, col:col + 1], in_=rstd[:, col:col + 1], func=AF.Exp, scale=-0.5),
                          reads=[f"rs{col}"], writes=[f"rs{col}"])
                S.add("dve", lambda e: e.scalar_tensor_tensor(out=xn_t, in0=src, scalar=rstd[:, col:col + 1],
                                                              in1=gsl[:, slot, :], op0=ALU.mult, op1=ALU.mult),
                      reads=[src_res, f"rs{col}", f"g{slot}"], writes=[xn_res])

            def norm_tile_b(xn_t, xn_res, pt_bank, dst, dst_res):
                def tr(e):
                    for k in range(8):
                        ins = e.transpose(out=PT[:, pt_bank, k * 128:(k + 1) * 128], in_=xn_t[:, k * 128:(k + 1) * 128],
                                          identity=ident[:])
                    return ins
                S.add("pe", tr, reads=[xn_res, "ident"], writes=[f"PT{pt_bank}"])
                S.add("act", lambda e: e.activation(out=dst, in_=PT[:, pt_bank, :].rearrange("p (k n) -> p k n", k=8),
                                                    func=AF.Copy),
                      reads=[f"PT{pt_bank}"], writes=[dst_res])

            def rotary(src3, cos_ap, sin_ap, nh, ta, tb, dst3, rres, wres, eng="dve", tag=""):
                cb = cos_ap.unsqueeze(1).broadcast_to([128, nh, 16])
                sb_ = sin_ap.unsqueeze(1).broadcast_to([128, nh, 16])
                x1 = src3[:, :, 0:16]
                x2 = src3[:, :, 16:32]
                S.add(eng, lambda e: e.tensor_tensor(out=ta, in0=x1, in1=cb, op=ALU.mult), reads=rres, writes=["rot_ta" + tag])
                S.add(eng, lambda e: e.tensor_tensor(out=tb, in0=x2, in1=sb_, op=ALU.mult), reads=rres, writes=["rot_tb" + tag])
                S.add(eng, lambda e: e.tensor_tensor(out=dst3[:, :, 0:16], in0=ta, in1=tb, op=ALU.subtract),
                      reads=["rot_ta" + tag, "rot_tb" + tag], writes=[wres + "a"])
                S.add(eng, lambda e: e.tensor_tensor(out=ta, in0=x1, in1=sb_, op=ALU.mult), reads=rres + [wres + "a"], writes=["rot_ta" + tag])
                S.add(eng, lambda e: e.tensor_tensor(out=tb, in0=x2, in1=cb, op=ALU.mult), reads=rres + [wres + "a"], writes=["rot_tb" + tag])
                S.add(eng, lambda e: e.tensor_tensor(out=dst3[:, :, 16:32], in0=ta, in1=tb, op=ALU.add),
                      reads=["rot_ta" + tag, "rot_tb" + tag], writes=[wres + "b"])

            with contextlib.ExitStack() as st:
                posi = sbuf(st, "posi", [128, 64], I32)
                posf = sbuf(st, "posf", [128, 64], F32)
                ang = sbuf(st, "ang", [128, 64, 16], F32)
                kf = sbuf(st, "kf", [128, 64, 16], F32)
                ki = sbuf(st, "ki", [128, 64, 16], I32)
                mm_ = sbuf(st, "mm_", [128, 64, 16], F32)
                cosp = sbuf(st, "cosp", [128, NPRE, 16], F32)
                sinp = sbuf(st, "sinp", [128, NPRE, 16], F32)
                wpre = sbuf(st, "wpre", [128, NPRE, H], F32)
                xs = sbuf(st, "xs", [128, 2, D], F32)
                xn2 = sbuf(st, "xn2", [128, 2, D], BF16)
                xnTp = sbuf(st, "xnTp", [128, 2, 8, 128], BF16)
                memT = sbuf(st, "memT", [128, 8, 256], BF16)
                krot = sbuf(st, "krot", [128, 2, 256], F32)
                rta = sbuf(st, "rta", [128, 16, 16], F32)
                rtb = sbuf(st, "rtb", [128, 16, 16], F32)
                kzp = sbuf(st, "kzp", [128, 2, 256], BF16)
                vbp = sbuf(st, "vbp", [128, 2, 512], BF16)
                thh = sbuf(st, "thh", [128, 32], F32)

                def cfn(e, s):
                    for (dst, src) in [(ident[:], ident_d), (onesm[:], onesm_d), (mask[:], mask_d), (xifm[:], xifm_d),
                                       (g128[:], g128_d), (zeta[:], zeta_d), (invf[:], invf_d), (convw[:], convw_d),
                                       (convb[:], convb_d), (lnw[:], lnw_d), (lnb[:], lnb_d), (wpre[:], wpre_d),
                                       (posi[:, 0:NPRE], posp), (posi[:, NPRE:64], posm)]:
                        e.dma_start(out=dst, in_=src).then_inc(s, 16)
                    e.dma_start(out=gnw[:], in_=gnw_d[0:1, :].broadcast_to([128, 512])).then_inc(s, 16)
                dma("sp", "const", cfn, 15, writes=["const"])
                load_g(0, 4)
                load_g(1, 0)
                S.add("pool", lambda e: e.memset(neghalf[:], -0.5), writes=["neghalf"])
                S.add("dve", lambda e: e.memset(Sst[:], 0.0), writes=["Sst"])
                S.add("pool", lambda e: e.memset(Sbf[:], 0.0), writes=["Sbf"])
                S.add("dve", lambda e: e.memset(uhalo[:], 0.0), writes=["uhalo"])
                S.barrier()
                RSTD_MODE[0] = "act"
                w_in_v = wview(0, 8, 2560)
                w_out_v = wview(24576, 8, 1024)
                xkst = sbuf(st, "xkst", [128, 8, 1024], BF16)
                xkvk = xkst
                xkvv = xkst
                load_weight("wk", xkvk, xkv_w, 0, D, 0, 1024, ["XK"])
                load_weight("k_in", w_in_v, w_in, 0, D, 0, 2560, ["R0a", "R0b", "R1a"])
                load_weight("k_out", w_out_v, w_out, 0, D, 0, 1024, ["R1b"])
                S.add("pool", lambda e: e.tensor_scalar(out=convw[:], in0=convw[:], scalar1=0.5, scalar2=None, op0=ALU.mult),
                      reads=["const"], writes=["convw"])
                S.add("dve", lambda e: e.tensor_copy(out=posf[:], in_=posi[:]), reads=["const"], writes=["posf"])
                S.add("dve", lambda e: e.tensor_tensor(out=ang[:], in0=posf[:].unsqueeze(2).broadcast_to([128, 64, 16]),
                                                       in1=invf[:].unsqueeze(1).broadcast_to([128, 64, 16]), op=ALU.mult),
                      reads=["posf", "const"], writes=["ang"])

                def reduce_ang(tag, shift):
                    src = ang
                    if shift != 0.0:
                        S.add("dve", lambda e: e.tensor_scalar(out=mm_[:], in0=ang[:], scalar1=shift, scalar2=None, op0=ALU.add),
                              reads=["ang", "mm_"], writes=["angs"])
                        src = mm_
                        sres = "angs"
                    else:
                        sres = "ang"
                    S.add("dve", lambda e: e.tensor_scalar(out=kf[:], in0=src[:], scalar1=float(1.0 / (2 * np.pi)), scalar2=None,
                                                           op0=ALU.mult), reads=[sres, "kfr"], writes=["kf"])
                    S.add("dve", lambda e: e.tensor_copy(out=ki[:], in_=kf[:]), reads=["kf"], writes=["ki"])
                    S.add("dve", lambda e: e.tensor_copy(out=kf[:], in_=ki[:]), reads=["ki"], writes=["kf"])
                    red = sbuf(st, "red" + tag, [128, 64, 16], F32)
                    S.add("dve", lambda e: e.scalar_tensor_tensor(out=red[:], in0=kf[:], scalar=-C1, in1=src[:], op0=ALU.mult,
                                                                  op1=ALU.add), reads=["kf", sres], writes=["red" + tag])
                    S.add("dve", lambda e: e.scalar_tensor_tensor(out=red[:], in0=kf[:], scalar=-C2, in1=red[:], op0=ALU.mult,
                                                                  op1=ALU.add), reads=["kf", "red" + tag], writes=["red" + tag])
                    S.add("dve", lambda e: e.scalar_tensor_tensor(out=red[:], in0=kf[:], scalar=-C3, in1=red[:], op0=ALU.mult,
                                                                  op1=ALU.add), reads=["kf", "red" + tag], writes=["red" + tag])
                    S.add("dve", lambda e: e.tensor_scalar(out=kf[:], in0=red[:], scalar1=PI, scalar2=-2 * PI, op0=ALU.is_gt,
                                                           op1=ALU.mult), reads=["red" + tag], writes=["kf"])
                    S.add("dve", lambda e: e.tensor_tensor(out=red[:], in0=red[:], in1=kf[:], op=ALU.add),
                          reads=["kf", "red" + tag], writes=["red" + tag])
                    S.add("dve", lambda e: e.tensor_scalar(out=kf[:], in0=red[:], scalar1=-PI, scalar2=2 * PI, op0=ALU.is_lt,
                                                           op1=ALU.mult), reads=["red" + tag], writes=["kf"])
                    S.add("dve", lambda e: e.tensor_tensor(out=red[:], in0=red[:], in1=kf[:], op=ALU.add),
                          reads=["kf", "red" + tag], writes=["red" + tag])
                    S.add("dve", lambda e: e.tensor_scalar(out=red[:], in0=red[:], scalar1=PI, scalar2=-PI, op0=ALU.min,
                                                           op1=ALU.max), reads=["red" + tag], writes=["red" + tag])
                    return red
                rs_ = reduce_ang("s", 0.0)
                S.add("act", lambda e: e.activation(out=sinp[:], in_=rs_[:, 0:NPRE, :], func=AF.Sin), reads=["reds"], writes=["sinp"])
                S.add("act", lambda e: e.activation(out=sinm[:], in_=rs_[:, NPRE:64, :], func=AF.Sin), reads=["reds"], writes=["sinm"])
                S.add("dve", lambda e: e.tensor_scalar(out=ang[:], in0=rs_[:], scalar1=PI / 2, scalar2=None, op0=ALU.add),
                      reads=["reds", "ang", "sinp", "sinm"], writes=["ang"])
                S.add("dve", lambda e: e.tensor_scalar(out=kf[:], in0=ang[:], scalar1=PI, scalar2=-2 * PI, op0=ALU.is_gt, op1=ALU.mult),
                      reads=["ang"], writes=["kf"])
                S.add("dve", lambda e: e.tensor_tensor(out=ang[:], in0=ang[:], in1=kf[:], op=ALU.add), reads=["kf", "ang"], writes=["ang"])
                S.add("dve", lambda e: e.tensor_scalar(out=ang[:], in0=ang[:], scalar1=PI, scalar2=-PI, op0=ALU.min, op1=ALU.max),
                      reads=["ang"], writes=["ang"])
                S.add("act", lambda e: e.activation(out=cosp[:], in_=ang[:, 0:NPRE, :], func=AF.Sin), reads=["ang"], writes=["cosp"])
                S.add("act", lambda e: e.activation(out=cosm[:], in_=ang[:, NPRE:64, :], func=AF.Sin), reads=["ang"], writes=["cosm"])

                checkpoint("tables", [("cosm", cosm[:], [128, 16, 16], F32, []), ("sinm", sinm[:], [128, 16, 16], F32, []),
                                      ("cosp", cosp[:], [128, NPRE, 16], F32, []), ("sinp", sinp[:], [128, NPRE, 16], F32, [])])
                checkpoint("memkv", [("kTm", kTm[:], [128, 8, 256], BF16, []), ("vm", vm[:], [128, 2, D], BF16, [])])
                for mt in range(2):
                    def ldm(e, s, mt=mt):
                        e.dma_start(out=xs[:, mt, :], in_=memx[mt * 128:(mt + 1) * 128, :]).then_inc(s, 16)
                    dma("sp", f"xs{mt}", ldm, 1, writes=[f"xs{mt}"])
                    norm_tile(xs[:, mt, :], f"xs{mt}", 0, mt, xn2[:, mt, :], f"xn{mt}", 0,
                              memT[:, :, mt * 128:(mt + 1) * 128], f"memT{mt}")
                for c in range(8):
                    def mk(e, c=c):
                        for k in range(8):
                            ins = e.matmul(PF[:, c % 2, 0:256], lhsT=xkvk[:, k, c * 128:(c + 1) * 128], rhs=memT[:, k, :],
                                           start=(k == 0), stop=(k == 7))
                        return ins
                    S.add("pe", mk, reads=["XK", "memT0", "memT1"], writes=[f"PF{c % 2}"])
                    S.add("act", lambda e, c=c: e.activation(out=kTm[:, c, :], in_=PF[:, c % 2, 0:256], func=AF.Copy),
                          reads=[f"PF{c % 2}"], writes=["kTm"])
                load_weight("wk", xkvv, xkv_w, 0, D, 1024, 1024, ["XK"])
                def pre_stage0(pt):
                    sl = pt % 2

                    def ldx(e, s, pt=pt, sl=sl):
                        e.dma_start(out=xs[:, sl, :], in_=xpre[pt * 128:(pt + 1) * 128, :]).then_inc(s, 16)
                    dma("sp", f"xs{sl}", ldx, 1, writes=[f"xs{sl}"])
                    col = 2 + (pt % 4)
                    norm_tile_a(xs[:, sl, :], f"xs{sl}", 1, col, xn2[:, sl, :], f"xn{sl}")

                def pre_stage1(pt):
                    sl = pt % 2
                    norm_tile_b(xn2[:, sl, :], f"xn{sl}", sl, xnTp[:, sl, :, :], f"xnTp{sl}")

                def pre_stage2(pt):
                    sl = pt % 2
                    bK, bV = (0, 1) if sl == 0 else (2, 3)

                    def mkv(e, sl=sl, bK=bK, bV=bV):
                        for k in range(8):
                            e.matmul(PF[:, bK, 0:256], lhsT=xnTp[:, sl, k, :], rhs=w_in_v[:, k, 1280:1536],
                                     start=(k == 0), stop=(k == 7))
                        for k in range(8):
                            ins = e.matmul(PF[:, bV, :], lhsT=xnTp[:, sl, k, :], rhs=w_in_v[:, k, 1536:2048],
                                           start=(k == 0), stop=(k == 7))
                        return ins
                    S.add("pe", mkv, reads=[f"xnTp{sl}", "R0a", "R0b", "R1a"], writes=[f"PF{bK}", f"PF{bV}"])
                    S.add("act", lambda e, sl=sl, bV=bV: e.activation(out=vbp[:, sl, :], in_=PF[:, bV, :], func=AF.Copy),
                          reads=[f"PF{bV}"], writes=[f"vbp{sl}"])
                    rotary(PF[:, bK, 0:256].rearrange("p (h d) -> p h d", h=8), cosp[:, pt, :], sinp[:, pt, :], 8,
                           rta[:, 8 * sl:8 * sl + 8, :], rtb[:, 8 * sl:8 * sl + 8, :],
                           krot[:, sl, :].rearrange("p (h d) -> p h d", h=8),
                           [f"PF{bK}", "cosp", "sinp"], f"krot{sl}", eng="dve", tag=f"p{sl}")
                    S.add("dve", lambda e, pt=pt, sl=sl: e.tensor_tensor(
                        out=kzp[:, sl, :].rearrange("p (h d) -> p h d", h=8),
                        in0=krot[:, sl, :].rearrange("p (h d) -> p h d", h=8),
                        in1=wpre[:, pt, :].unsqueeze(2).broadcast_to([128, 8, 32]), op=ALU.mult),
                        reads=[f"krot{sl}a", f"krot{sl}b", "const"], writes=[f"kzp{sl}"])
                    if pt == NPRE - 1:
                        for c in range(4):
                            def mab(e, c=c, sl=sl):
                                for k in range(8):
                                    e.matmul(PF[:, 4, 0:32], lhsT=w_in_v[:, k, c * 128:(c + 1) * 128], rhs=xnTp[:, sl, k, 96:128],
                                             start=(k == 0), stop=(k == 7))
                                for k in range(8):
                                    ins = e.matmul(PF[:, 4, 256:288], lhsT=w_in_v[:, k, 512 + c * 128:512 + (c + 1) * 128],
                                                   rhs=xnTp[:, sl, k, 96:128], start=(k == 0), stop=(k == 7), skip_group_check=True)
                                return ins
                            S.add("pe", mab, reads=[f"xnTp{sl}", "R0a", "R0b", "R1a"], writes=["PF4"])
                            S.add("act", lambda e: e.activation(out=thh[:], in_=PF[:, 4, 256:288], func=AF.Tanh, scale=0.5),
                                  reads=["PF4"], writes=["thh"])
                            S.add("dve", lambda e, c=c: e.scalar_tensor_tensor(out=uhalo[:, c, :], in0=thh[:], scalar=1.0,
                                                                              in1=PF[:, 4, 0:32], op0=ALU.add, op1=ALU.mult),
                                  reads=["thh", "PF4"], writes=["uhalo"])

                def pre_stage3(pt):
                    sl = pt % 2

                    def mst(e, pt=pt, sl=sl):
                        for hp in range(4):
                            ins = e.matmul(PF[0:64, 5, hp * 128:(hp + 1) * 128], lhsT=kzp[:, sl, hp * 64:(hp + 1) * 64],
                                           rhs=vbp[:, sl, hp * 128:(hp + 1) * 128], start=(pt == 0 and hp == 0),
                                           stop=(pt == NPRE - 1), skip_group_check=True)
                        return ins
                    S.add("pe", mst, reads=[f"kzp{sl}", f"vbp{sl}"], writes=["PF5"])

                for it in range(NPRE + 3):
                    if it < NPRE:
                        pre_stage0(it)
                    if 0 <= it - 1 < NPRE:
                        pre_stage1(it - 1)
                    if 0 <= it - 2 < NPRE:
                        pre_stage2(it - 2)
                    if 0 <= it - 3 < NPRE:
                        pre_stage3(it - 3)
                for a in range(2):
                    S.add("dve", lambda e, a=a: e.tensor_copy(
                        out=Sst[32 * a:32 * a + 32, :, :],
                        in_=PF[32 * a:32 * a + 32, 5, :].rearrange("p (hp x) -> p hp x", hp=4)[:, :, 64 * a:64 * a + 64]),
                        reads=["PF5"], writes=["Sst"])
                for a in range(2):
                    S.add("act", lambda e, a=a: e.activation(out=Sbf[32 * a:32 * a + 32, :, 64 * a:64 * a + 64],
                                                             in_=Sst[32 * a:32 * a + 32, :, :], func=AF.Copy),
                          reads=["Sst"], writes=["Sbf"])
                for mc in range(2):
                    for n in range(2):
                        bk = 2 + (mc * 2 + n) % 2

                        def mv(e, mc=mc, n=n, bk=bk):
                            for k in range(8):
                                ins = e.matmul(PF[:, bk, :], lhsT=memT[:, k, mc * 128:(mc + 1) * 128],
                                               rhs=xkvv[:, k, n * 512:(n + 1) * 512], start=(k == 0), stop=(k == 7))
                            return ins
                        S.add("pe", mv, reads=["XK", "memT0", "memT1"], writes=[f"PF{bk}"])
                        S.add("dve", lambda e, mc=mc, n=n, bk=bk: e.tensor_copy(out=vm[:, mc, n * 512:(n + 1) * 512], in_=PF[:, bk, :]),
                              reads=[f"PF{bk}"], writes=["vm"])

                checkpoint("prefix", [("Sst", Sst[:], [64, 4, 64], F32, []), ("uhalo", uhalo[:], [128, 4, 32], BF16, [])])
                S.barrier()

            def run_pass(ps_):
                T0 = ps_ * NTP
                RSTD_MODE[0] = "pool"
                with contextlib.ExitStack() as st:
                    xn = sbuf(st, "xn", [128, 2, D], BF16)
                    xnT = sbuf(st, "xnT", [128, 8, 512], BF16)
                    tht = sbuf(st, "tht", [128, 512], F32)
                    u = sbuf(st, "u", [128, 4, 544], BF16)
                    dg = sbuf(st, "dg", [128, 2, CW, 128], BF16)
                    tmpn = sbuf(st, "tmpn", [128, 2, 512], F32)
                    ybf = sbuf(st, "ybf", [128, 4, 512], BF16)
                    ysqb = sbuf(st, "ysqb", [128, 4, 512], BF16)
                    t_a = sbuf(st, "t_a", [128, 512], F32)
                    t_b = sbuf(st, "t_b", [128, 512], F32)
                    yconv = sbuf(st, "yconv", [128, 4, 512], BF16)
                    rot = sbuf(st, "rot", [128, 512], F32)
                    rtaA = sbuf(st, "rta2", [128, 16, 16], F32)
                    rtbA = sbuf(st, "rtb2", [128, 16, 16], F32)
                    qkb = sbuf(st, "qkb", [128, 2, 512], BF16)
                    kz = sbuf(st, "kz", [128, 2, 256], BF16)
                    vb = sbuf(st, "vb", [128, 2, 512], BF16)
                    gs = sbuf(st, "gs", [128, 2, 512], F32)
                    qkT = sbuf(st, "qkT", [64, 8, 128], BF16)
                    qx = sbuf(st, "qx", [64, 4, 128], BF16)
                    scm = sbuf(st, "scm", [128, 8, 128], BF16)
                    yret = sbuf(st, "yret", [128, 512], BF16)
                    yretT = sbuf(st, "yretT", [128, 4, 128], BF16)
                    gst = sbuf(st, "gst", [128, 32], F32)

                    if ps_ > 0:
                        load_g(1, 0)
                    for m in range(NM):
                        items = []
                        for j in range(4):
                            t = m * 4 + j
                            gt = T0 + t

                            def ldh(e, s, t=t, gt=gt):
                                e.dma_start(out=h[:, t, :], in_=xmain[gt * 128:(gt + 1) * 128, :]).then_inc(s, 16)
                            if ps_ == 0:
                                dma("sp", f"h{t}", ldh, 1, writes=[f"h{t}"])
                            items.append(((h[:, t, :], f"h{t}", 1, 8 + t, xn[:, t % 2, :], f"xn{t % 2}"),
                                          (xn[:, t % 2, :], f"xn{t % 2}", 0, xnT[:, :, j * 128:(j + 1) * 128], f"xnT{j}")))
                        norm_group(items)
                        xnT_res = [f"xnT{j}" for j in range(4)]
                        if ps_ == 0 and m == 0:
                            checkpoint("sA1", [("xnT", xnT[:], [128, 8, 512], BF16, [])])
                        ures = [f"u{c}" for c in range(4)]
                        if m == 0:
                            S.add("pool", lambda e: e.tensor_copy(out=u[:, :, 0:32], in_=uhalo[:]), reads=["uhalo"], writes=["uh"])
                        else:
                            S.add("pool", lambda e: e.tensor_copy(out=u[:, :, 0:32], in_=u[:, :, 512:544]), reads=ures, writes=["uh"])
                        def gen_diag(c):
                            S.add("dve", lambda e, c=c: e.tensor_tensor(
                                out=dg[:, c % 2, :, :], in0=ident[:].unsqueeze(1).broadcast_to([128, CW, 128]),
                                in1=convw[:, c, :].unsqueeze(2).broadcast_to([128, CW, 128]), op=ALU.mult),
                                reads=["ident", "convw", "const"], writes=[f"dg{c % 2}"])
                        gen_diag(0)
                        gen_diag(1)
                        for c in range(4):
                            ba, bb = (0, 1) if c % 2 == 0 else (2, 3)

                            def mab(e, c=c, ba=ba, bb=bb):
                                for k in range(8):
                                    e.matmul(PF[:, ba, :], lhsT=w_in_v[:, k, c * 128:(c + 1) * 128], rhs=xnT[:, k, :],
                                             start=(k == 0), stop=(k == 7))
                                for k in range(8):
                                    ins = e.matmul(PF[:, bb, :], lhsT=w_in_v[:, k, 512 + c * 128:512 + (c + 1) * 128],
                                                   rhs=xnT[:, k, :], start=(k == 0), stop=(k == 7))
                                return ins
                            S.add("pe", mab, reads=xnT_res + ["R0a", "R0b", "R1a"], writes=[f"PF{ba}", f"PF{bb}"])
                            S.add("act", lambda e, bb=bb: e.activation(out=tht[:], in_=PF[:, bb, :], func=AF.Tanh, scale=0.5),
                                  reads=[f"PF{bb}"], writes=["tht"])
                            S.add("dve", lambda e, c=c, ba=ba: e.scalar_tensor_tensor(out=u[:, c, 32:544], in0=tht[:], scalar=1.0,
                                                                                     in1=PF[:, ba, :], op0=ALU.add, op1=ALU.mult),
                                  reads=["tht", f"PF{ba}", "uh"], writes=[f"u{c}"])
                        if m == NM - 1:
                            S.add("pool", lambda e: e.tensor_copy(out=uhalo[:], in_=u[:, :, 512:544]), reads=ures, writes=["uhalo"])
                        if ps_ == 0 and m == 0:
                            checkpoint("sA2", [("u", u[:], [128, 4, 544], BF16, [])])
                        for c in range(4):
                            def mconv(e, c=c):
                                for tp in range(CW):
                                    ins = e.matmul(PF[:, c, :], lhsT=dg[:, c % 2, tp, :], rhs=u[:, c, 2 + tp:514 + tp],
                                                   start=(tp == 0), stop=(tp == CW - 1))
                                return ins
                            S.add("pe", mconv, reads=[f"u{c}", "uh", f"dg{c % 2}"], writes=[f"PF{c}"])
                            if c + 2 < 4:
                                gen_diag(c + 2)
                        if ps_ == 0 and m == 0:
                            checkpoint("sA3", [("u", u[:], [128, 4, 544], BF16, []), ("dg", dg[:], [128, 2, CW, 128], BF16, [])])
                        for c in range(4):
                            S.add("act", lambda e, c=c: e.activation(out=ysqb[:, c, :], in_=PF[:, c, :], func=AF.Square,
                                                                     bias=convb[:, c:c + 1]),
                                  reads=[f"PF{c}", "const"], writes=[f"ysqb{c}"])
                            S.add("dve", lambda e, c=c: e.tensor_scalar(out=ybf[:, c, :], in0=PF[:, c, :], scalar1=convb[:, c:c + 1],
                                                                        scalar2=None, op0=ALU.add),
                                  reads=[f"PF{c}", "const"], writes=[f"ybf{c}"])

                        def mstat(e):
                            for c in range(4):
                                e.matmul(PF[:, 4, :], lhsT=onesm[:], rhs=ybf[:, c, :], start=(c == 0), stop=(c == 3))
                            for c in range(4):
                                ins = e.matmul(PF[:, 5, :], lhsT=onesm[:], rhs=ysqb[:, c, :], start=(c == 0), stop=(c == 3))
                            return ins
                        S.add("pe", mstat, reads=[f"ybf{c}" for c in range(4)] + [f"ysqb{c}" for c in range(4)] + ["const"],
                              writes=["PF4", "PF5"])
                        S.add("act", lambda e: e.activation(out=t_a[:], in_=PF[:, 4, :], func=AF.Square), reads=["PF4"], writes=["t_a"])
                        S.add("act", lambda e: e.activation(out=tht[:], in_=PF[:, 4, :], func=AF.Copy), reads=["PF4"], writes=["tht"])
                        S.add("dve", lambda e: e.tensor_tensor(out=t_b[:], in0=PF[:, 5, :], in1=t_a[:], op=ALU.subtract),
                              reads=["PF5", "t_a"], writes=["t_b"])
                        S.add("dve", lambda e: e.tensor_scalar(out=t_b[:], in0=t_b[:], scalar1=0.0, scalar2=EPS, op0=ALU.max, op1=ALU.add),
                              reads=["t_b"], writes=["t_b"])
                        S.add("act", lambda e: e.activation(out=t_b[:], in_=t_b[:], func=AF.Ln), reads=["t_b"], writes=["t_b"])
                        S.add("act", lambda e: e.activation(out=t_b[:], in_=t_b[:], func=AF.Exp, scale=-0.5), reads=["t_b"], writes=["t_b"])
                        for c in range(4):
                            S.add("dve", lambda e, c=c: e.scalar_tensor_tensor(out=tmpn[:, c % 2, :], in0=PF[:, c, :],
                                                                               scalar=convb[:, c:c + 1], in1=tht[:],
                                                                               op0=ALU.add, op1=ALU.subtract),
                                  reads=[f"PF{c}", "tht", "const"], writes=[f"tmpn{c % 2}"])
                            S.add("dve", lambda e, c=c: e.tensor_tensor(out=tmpn[:, c % 2, :], in0=tmpn[:, c % 2, :], in1=t_b[:], op=ALU.mult),
                                  reads=[f"tmpn{c % 2}", "t_b"], writes=[f"tmpn{c % 2}"])
                            S.add("act", lambda e, c=c: e.activation(out=yconv[:, c, :], in_=tmpn[:, c % 2, :], func=AF.Silu,
                                                                     scale=lnw[:, c:c + 1], bias=lnb[:, c:c + 1]),
                                  reads=[f"tmpn{c % 2}", "const"], writes=[f"yconv{c}"])
                        if ps_ == 0 and m == 0:
                            checkpoint("sA4", [("yconv", yconv[:], [128, 4, 512], BF16, [])])
                        def x_pe(j):
                            tsl = slice(j * 128, (j + 1) * 128)

                            def mqkvg(e, tsl=tsl):
                                for (bk, c0) in ((0, 1024), (1, 1536), (2, 2048)):
                                    for k in range(8):
                                        ins = e.matmul(PF[:, bk, :], lhsT=xnT[:, k, tsl], rhs=w_in_v[:, k, c0:c0 + 512],
                                                       start=(k == 0), stop=(k == 7))
                                return ins
                            S.add("pe", mqkvg, reads=[f"xnT{j}", "R0a", "R0b", "R1a"], writes=["PF0", "PF1", "PF2"])

                        def x_ew(j):
                            sl = j % 2
                            gt = T0 + m * 4 + j
                            S.add("act", lambda e, sl=sl: e.activation(out=vb[:, sl, :], in_=PF[:, 1, :], func=AF.Copy),
                                  reads=["PF1"], writes=[f"vb{sl}"])
                            S.add("act", lambda e, sl=sl: e.activation(out=gs[:, sl, :], in_=PF[:, 2, :], func=AF.Silu),
                                  reads=["PF2"], writes=[f"gs{sl}"])
                            S.add("pool", lambda e, sl=sl: e.tensor_tensor(out=gs[:, sl, :], in0=gs[:, sl, :], in1=gnw[:], op=ALU.mult),
                                  reads=[f"gs{sl}", "const"], writes=[f"gs{sl}"])
                        def x_rot(j):
                            sl = j % 2
                            gt = T0 + m * 4 + j
                            rotary(PF[:, 0, :].rearrange("p (h d) -> p h d", h=16), cosm[:, gt, :], sinm[:, gt, :], 16,
                                   rtaA[:], rtbA[:], rot[:].rearrange("p (h d) -> p h d", h=16), ["PF0", "cosm", "sinm"], "rot")
                            S.add("act", lambda e, sl=sl: e.activation(out=qkb[:, sl, :], in_=rot[:], func=AF.Copy),
                                  reads=["rota", "rotb"], writes=[f"qkb{sl}"])
                            S.add("pool", lambda e, sl=sl: e.tensor_tensor(
                                out=kz[:, sl, :].rearrange("p (h d) -> p h d", h=8),
                                in0=rot[:, 256:512].rearrange("p (h d) -> p h d", h=8),
                                in1=zeta[:].unsqueeze(2).broadcast_to([128, 8, 32]), op=ALU.mult),
                                reads=["rota", "rotb", "const"], writes=[f"kz{sl}"])

                        def y_1a(j):
                            sl = j % 2

                            def trqk(e, sl=sl):
                                for i in range(8):
                                    ins = e.transpose(out=PT[0:64, 1, i * 128:(i + 1) * 128], in_=qkb[:, sl, i * 64:(i + 1) * 64],
                                                      identity=ident[:])
                                return ins
                            S.add("pe", trqk, reads=[f"qkb{sl}", "ident"], writes=["PT1"])
                            S.add("act", lambda e: e.activation(out=qkT[:], in_=PT[0:64, 1, :].rearrange("p (i n) -> p i n", i=8),
                                                                func=AF.Copy), reads=["PT1"], writes=["qkT"])
                            S.add("dve", lambda e: e.tensor_tensor(out=qx[:], in0=qkT[:, 0:4, :], in1=xifm[:], op=ALU.mult),
                                  reads=["qkT", "const"], writes=["qx"])

                        def y_1b(j):
                            def msc(e):
                                for hh in range(8):
                                    a, hp = hh % 2, hh // 2
                                    ins = e.matmul(PF[:, 3 + a, hp * 128:(hp + 1) * 128],
                                                   lhsT=qkT[32 * a:32 * a + 32, 4 + hp, :], rhs=qkT[32 * a:32 * a + 32, hp, :],
                                                   start=True, stop=True, skip_group_check=True)
                                return ins
                            S.add("pe", msc, reads=["qkT"], writes=["PF3", "PF4"])
                            for half in range(2):
                                S.add("dve", lambda e, half=half: e.tensor_tensor(
                                    out=scm[:].rearrange("p (hp a) n -> p hp a n", a=2)[:, :, half, :],
                                    in0=PF[:, 3 + half, :].rearrange("p (a b) -> p a b", a=4),
                                    in1=mask[:].rearrange("p (hp a) n -> p hp a n", a=2)[:, :, half, :], op=ALU.mult),
                                    reads=[f"PF{3 + half}", "const"], writes=[f"scm{half}"])

                        def y_1c(j):
                            sl = j % 2

                            def my(e, sl=sl):
                                for hp in range(4):
                                    e.matmul(PF[:, 5, hp * 128:(hp + 1) * 128], lhsT=qx[:, hp, :], rhs=Sbf[:, hp, :],
                                             start=True, stop=False, skip_group_check=True)
                                    for a in range(2):
                                        hh = 2 * hp + a
                                        ins = e.matmul(PF[:, 5, hh * 64:(hh + 1) * 64], lhsT=scm[:, hh, :],
                                                       rhs=vb[:, sl, hh * 64:(hh + 1) * 64], start=False, stop=(a == 1),
                                                       skip_group_check=True)
                                return ins
                            S.add("pe", my, reads=["scm0", "scm1", f"vb{sl}", "qx", "Sbf"], writes=["PF5"])

                            def msu(e, sl=sl):
                                for hp in range(4):
                                    ins = e.matmul(PF[0:64, 3, hp * 128:(hp + 1) * 128], lhsT=kz[:, sl, hp * 64:(hp + 1) * 64],
                                                   rhs=vb[:, sl, hp * 128:(hp + 1) * 128], start=True, stop=True, skip_group_check=True)
                                return ins
                            S.add("pe", msu, reads=[f"kz{sl}", f"vb{sl}"], writes=["PF3"])
                            S.add("dve", lambda e: e.tensor_tensor(out=Sst[:], in0=Sst[:], in1=g128[:], op=ALU.mult),
                                  reads=["Sst", "const"], writes=["Sst"])
                            for a in range(2):
                                S.add("dve", lambda e, a=a: e.tensor_tensor(
                                    out=Sst[32 * a:32 * a + 32, :, :], in0=Sst[32 * a:32 * a + 32, :, :],
                                    in1=PF[32 * a:32 * a + 32, 3, :].rearrange("p (hp x) -> p hp x", hp=4)[:, :, 64 * a:64 * a + 64],
                                    op=ALU.add), reads=["Sst", "PF3"], writes=["Sst"])
                            for a in range(2):
                                S.add("act", lambda e, a=a: e.activation(out=Sbf[32 * a:32 * a + 32, :, 64 * a:64 * a + 64],
                                                                         in_=Sst[32 * a:32 * a + 32, :, :], func=AF.Copy),
                                      reads=["Sst"], writes=["Sbf"])

                        def y_2(j):
                            sl = j % 2
                            Y3 = PF[:, 5, :].rearrange("p (h e) -> p h e", h=8)
                            S.add("dve", lambda e, Y3=Y3: e.tensor_reduce(out=gst[:, 0:8], in_=Y3, axis=AX.X, op=ALU.add),
                                  reads=["PF5"], writes=["gstA"])
                            S.add("act", lambda e: e.activation(out=t_a[:], in_=PF[:, 5, :], func=AF.Square), reads=["PF5"], writes=["t_a"])
                            S.add("dve", lambda e: e.tensor_reduce(out=gst[:, 8:16], in_=t_a[:].rearrange("p (h e) -> p h e", h=8),
                                                                   axis=AX.X, op=ALU.add), reads=["t_a"], writes=["gstB"])
                            S.add("dve", lambda e: e.tensor_scalar(out=gst[:, 0:16], in0=gst[:, 0:16], scalar1=1.0 / DV, scalar2=None,
                                                                   op0=ALU.mult), reads=["gstA", "gstB"], writes=["gstA", "gstB"])
                            S.add("dve", lambda e: e.tensor_tensor(out=gst[:, 16:24], in0=gst[:, 0:8], in1=gst[:, 0:8], op=ALU.mult),
                                  reads=["gstA"], writes=["gst2"])
                            S.add("dve", lambda e: e.scalar_tensor_tensor(out=gst[:, 16:24], in0=gst[:, 8:16], scalar=EPS, in1=gst[:, 16:24],
                                                                          op0=ALU.add, op1=ALU.subtract),
                                  reads=["gstB", "gst2"], writes=["gst2"])
                            S.add("pool", lambda e: e.tensor_tensor(out=gst[:, 16:24], in0=gst[:, 16:24], in1=neghalf[:, 0:8], op=ALU.pow),
                                  reads=["gst2", "neghalf"], writes=["gst2"])
                            yn3 = t_b[:].rearrange("p (h e) -> p h e", h=8)
                            S.add("dve", lambda e, Y3=Y3, yn3=yn3: e.tensor_tensor(
                                out=yn3, in0=Y3, in1=gst[:, 0:8].unsqueeze(2).broadcast_to([128, 8, 64]), op=ALU.subtract),
                                reads=["PF5", "gstA"], writes=["t_b"])
                        def y_2b(j):
                            sl = j % 2
                            yn3 = t_b[:].rearrange("p (h e) -> p h e", h=8)
                            S.add("dve", lambda e, yn3=yn3: e.tensor_tensor(
                                out=yn3, in0=yn3, in1=gst[:, 16:24].unsqueeze(2).broadcast_to([128, 8, 64]), op=ALU.mult),
                                reads=["t_b", "gst2"], writes=["t_b"])
                            S.add("dve", lambda e, sl=sl: e.tensor_tensor(out=yret[:], in0=t_b[:], in1=gs[:, sl, :], op=ALU.mult),
                                  reads=["t_b", f"gs{sl}"], writes=["yret"])

                            def tryr(e):
                                for i in range(4):
                                    ins = e.transpose(out=PT[:, 0, i * 128:(i + 1) * 128], in_=yret[:, i * 128:(i + 1) * 128],
                                                      identity=ident[:])
                                return ins
                            S.add("pe", tryr, reads=["yret", "ident"], writes=["PT0"])
                            S.add("act", lambda e: e.activation(out=yretT[:], in_=PT[:, 0, 0:512].rearrange("p (i n) -> p i n", i=4),
                                                                func=AF.Copy), reads=["PT0"], writes=["yretT"])

                        def y_3h(j, n):
                            tsl = slice(j * 128, (j + 1) * 128)

                            def mwo(e, tsl=tsl, n=n):
                                for kc in range(8):
                                    lt = yconv[:, kc, tsl] if kc < 4 else yretT[:, kc - 4, :]
                                    ins = e.matmul(PF[:, n, :], lhsT=lt, rhs=w_out_v[:, kc, n * 512:(n + 1) * 512],
                                                   start=(kc == 0), stop=(kc == 7))
                                return ins
                            S.add("pe", mwo, reads=[f"yconv{c}" for c in range(4)] + ["yretT", "R1b"], writes=[f"PF{n}"])

                        def y_3c(j):
                            t = m * 4 + j
                            S.add("dve", lambda e, t=t: e.tensor_tensor(out=h[:, t, :], in0=h[:, t, :],
                                                                        in1=PF[:, 0:2, :].rearrange("p a b -> p (a b)"), op=ALU.add),
                                  reads=[f"h{t}", "PF0", "PF1"], writes=[f"h{t}"])

                        x_pe(0)
                        x_ew(0)
                        x_rot(0)
                        for j in range(4):
                            y_1a(j)
                            if j > 0:
                                y_3h(j - 1, 0)
                            y_1b(j)
                            if j > 0:
                                y_3h(j - 1, 1)
                                y_3c(j - 1)
                            y_1c(j)
                            if j + 1 < 4:
                                x_pe(j + 1)
                            y_2(j)
                            if j + 1 < 4:
                                x_ew(j + 1)
                                x_rot(j + 1)
                            y_2b(j)
                        y_3h(3, 0)
                        y_3h(3, 1)
                        y_3c(3)
                    checkpoint(f"A{ps_}", [("h", h[:], [128, NTP, D], F32, [])])
                    S.barrier()

                with contextlib.ExitStack() as st:
                    RSTD_MODE[0] = "act"
                    W2 = sbuf(st, "W2", [128, 16384], BF16)
                    xq_v = W2[:, 0:8192].rearrange("p (k n) -> p k n", k=8)
                    xo_v = W2[:, 8192:16384].rearrange("p (k n) -> p k n", k=8)
                    load_g(0, 1)
                    load_weight("k_xq", xq_v, xq_w, 0, D, 0, 1024, ["R2a"])
                    load_weight("k_xo", xo_v, xo_w, 0, D, 0, 1024, ["R2b"])

                    def slot_views(s_):
                        if s_ < 2:
                            base = W[:, s_ * 16384:(s_ + 1) * 16384]
                        else:
                            base = W2[:, :]
                        return (base[:, 0:8192].rearrange("p (k n) -> p k n", k=8),
                                base[:, 8192:16384].rearrange("p (k n) -> p k n", k=8))

                    def load_quarter(qd, s_):
                        upv, dnv = slot_views(s_)
                        load_weight(f"k_u{s_}", upv, up_w, 0, D, qd * 1024, 1024, [f"R{s_}a"])
                        load_weight(f"k_d{s_}", dnv, down_w, qd * 1024, 1024, 0, 1024, [f"R{s_}b"])

                    with contextlib.ExitStack() as stb:
                        xnb = sbuf(stb, "xnB", [128, 2, D], BF16)
                        xnTb = sbuf(stb, "xnTB", [128, 8, 512], BF16)
                        qT = sbuf(stb, "qT", [128, 8, 512], BF16)
                        pn = sbuf(stb, "pn", [128, 3, 4, 256], BF16)
                        pTm = sbuf(stb, "pTm", [128, 8, 512], BF16)
                        oT = sbuf(stb, "oT", [128, 8, 512], BF16)
                        sst = sbuf(stb, "sst", [128, 3, 16], F32)
                        for m in range(NM):
                            items = []
                            for j in range(4):
                                t = m * 4 + j
                                items.append(((h[:, t, :], f"h{t}", 0, 16 + t, xnb[:, t % 2, :], f"xnb{t % 2}"),
                                              (xnb[:, t % 2, :], f"xnb{t % 2}", 0, xnTb[:, :, j * 128:(j + 1) * 128], f"xnTb{j}")))
                            norm_group(items)
                            xnT_res = [f"xnTb{j}" for j in range(4)]
                            load_quarter(m, m)
                            for c in range(8):
                                bk = c % 2

                                def mq(e, c=c, bk=bk):
                                    for k in range(8):
                                        ins = e.matmul(PF[:, bk, :], lhsT=xq_v[:, k, c * 128:(c + 1) * 128], rhs=xnTb[:, k, :],
                                                       start=(k == 0), stop=(k == 7))
                                    return ins
                                S.add("pe", mq, reads=xnT_res + ["R2a"], writes=[f"PF{bk}"])
                                if c % 2 == 0:
                                    S.add("act", lambda e, c=c, bk=bk: e.activation(out=qT[:, c, :], in_=PF[:, bk, :], func=AF.Copy),
                                          reads=[f"PF{bk}"], writes=[f"qT{c}"])
                                else:
                                    S.add("dve", lambda e, c=c, bk=bk: e.tensor_copy(out=qT[:, c, :], in_=PF[:, bk, :]),
                                          reads=[f"PF{bk}"], writes=[f"qT{c}"])
                            def b_msc(j):
                                tsl = slice(j * 128, (j + 1) * 128)
                                b0 = 2 * (j % 3)

                                def msc(e, tsl=tsl, b0=b0):
                                    for hd in range(4):
                                        for dd in range(2):
                                            ins = e.matmul(PF[:, b0 + hd // 2, (hd % 2) * 256:(hd % 2 + 1) * 256],
                                                           lhsT=qT[:, 2 * hd + dd, tsl], rhs=kTm[:, 2 * hd + dd, :],
                                                           start=(dd == 0), stop=(dd == 1), skip_group_check=True)
                                    return ins
                                S.add("pe", msc, reads=[f"qT{c}" for c in range(8)] + ["kTm"], writes=[f"PF{b0}", f"PF{b0 + 1}"])

                            def b_soft(j):
                                tsl = slice(j * 128, (j + 1) * 128)
                                par = j % 3
                                b0 = 2 * par
                                ptb = j % 2
                                SC3 = PF[:, b0:b0 + 2, :].rearrange("p a (b m) -> p (a b) m", b=2)
                                S.add("dve", lambda e, SC3=SC3, par=par: e.tensor_reduce(out=sst[:, par, 0:4], in_=SC3, axis=AX.X, op=ALU.max),
                                      reads=[f"PF{b0}", f"PF{b0 + 1}"], writes=[f"sst0{par}"])
                                S.add("dve", lambda e, par=par: e.tensor_scalar(out=sst[:, par, 4:8], in0=sst[:, par, 0:4], scalar1=-1.0 / 16.0,
                                                                                 scalar2=None, op0=ALU.mult),
                                      reads=[f"sst0{par}"], writes=[f"sst1{par}"])
                                for hd in range(4):
                                    S.add("act", lambda e, hd=hd, par=par, b0=b0: e.activation(
                                        out=pn[:, par, hd, :], in_=PF[:, b0 + hd // 2, (hd % 2) * 256:(hd % 2 + 1) * 256], func=AF.Exp,
                                        scale=1.0 / 16.0, bias=sst[:, par, 4 + hd:5 + hd], accum_out=sst[:, par, 8 + hd:9 + hd]),
                                        reads=[f"PF{b0 + hd // 2}", f"sst1{par}"], writes=[f"pn{par}_{hd}", f"sst2{par}_{hd}"])
                                S.add("dve", lambda e, par=par: e.reciprocal(out=sst[:, par, 12:16], in_=sst[:, par, 8:12]),
                                      reads=[f"sst2{par}_{hd}" for hd in range(4)], writes=[f"sst3{par}"])
                                S.add("dve", lambda e, par=par: e.tensor_tensor(
                                    out=pn[:, par, :, :], in0=pn[:, par, :, :],
                                    in1=sst[:, par, 12:16].unsqueeze(2).broadcast_to([128, 4, 256]), op=ALU.mult),
                                    reads=[f"pn{par}_{hd}" for hd in range(4)] + [f"sst3{par}"], writes=[f"pnn{par}"])

                                def trp(e, par=par, ptb=ptb):
                                    for hd in range(4):
                                        for mc in range(2):
                                            i = hd * 2 + mc
                                            ins = e.transpose(out=PT[:, ptb, i * 128:(i + 1) * 128],
                                                              in_=pn[:, par, hd, mc * 128:(mc + 1) * 128], identity=ident[:])
                                    return ins
                                S.add("pe", trp, reads=[f"pnn{par}", "ident"] + [f"pn{par}_{hd}" for hd in range(4)], writes=[f"PT{ptb}"])
                                S.add("act", lambda e, tsl=tsl, ptb=ptb: e.activation(out=pTm[:, :, tsl],
                                                                                      in_=PT[:, ptb, :].rearrange("p (i n) -> p i n", i=8),
                                                                                      func=AF.Copy), reads=[f"PT{ptb}"], writes=[f"pTm{j}"])
                            for s_i in range(6):
                                if s_i < 4:
                                    b_msc(s_i)
                                if s_i >= 2:
                                    b_soft(s_i - 2)
                            for c in range(8):
                                hd = c // 2
                                bk = c % 2

                                def mpv(e, c=c, hd=hd, bk=bk):
                                    for mc in range(2):
                                        ins = e.matmul(PF[:, bk, :], lhsT=vm[:, mc, c * 128:(c + 1) * 128], rhs=pTm[:, hd * 2 + mc, :],
                                                       start=(mc == 0), stop=(mc == 1))
                                    return ins
                                S.add("pe", mpv, reads=[f"pTm{j}" for j in range(4)] + ["vm"], writes=[f"PF{bk}"])
                                if c % 2 == 0:
                                    S.add("act", lambda e, c=c, bk=bk: e.activation(out=oT[:, c, :], in_=PF[:, bk, :], func=AF.Copy),
                                          reads=[f"PF{bk}"], writes=[f"oT{c}"])
                                else:
                                    S.add("dve", lambda e, c=c, bk=bk: e.tensor_copy(out=oT[:, c, :], in_=PF[:, bk, :]),
                                          reads=[f"PF{bk}"], writes=[f"oT{c}"])
                            for j in range(4):
                                t = m * 4 + j
                                tsl = slice(j * 128, (j + 1) * 128)
                                b0 = 2 + 2 * (j % 2)

                                def mxo(e, tsl=tsl, b0=b0):
                                    for n in range(2):
                                        for kc in range(8):
                                            ins = e.matmul(PF[:, b0 + n, :], lhsT=oT[:, kc, tsl], rhs=xo_v[:, kc, n * 512:(n + 1) * 512],
                                                           start=(kc == 0), stop=(kc == 7))
                                    return ins
                                S.add("pe", mxo, reads=[f"oT{c}" for c in range(8)] + ["R2b"], writes=[f"PF{b0}", f"PF{b0 + 1}"])
                                S.add("dve", lambda e, t=t, b0=b0: e.tensor_tensor(out=h[:, t, :], in0=h[:, t, :],
                                                                                   in1=PF[:, b0:b0 + 2, :].rearrange("p a b -> p (a b)"),
                                                                                   op=ALU.add),
                                      reads=[f"h{t}", f"PF{b0}", f"PF{b0 + 1}"], writes=[f"h{t}"])
                        checkpoint(f"B{ps_}", [("h", h[:], [128, NTP, D], F32, [])])
                        S.barrier()

                    with contextlib.ExitStack() as stc:
                        xnc = sbuf(stc, "xnC", [128, 2, D], BF16)
                        xnTa = sbuf(stc, "xnTa", [128, 8, TH], BF16)
                        rl = sbuf(stc, "rl", [128, 2, 512], F32)
                        hT = sbuf(stc, "hT", [128, 8, 512], BF16)
                        ot = sbuf(stc, "ot", [128, 2, D], F32)
                        load_g(1, 2)
                        load_g(0, 3)
                        items = []
                        for t in range(NTP):
                            items.append(((h[:, t, :], f"h{t}", 1, 24 + t, xnc[:, t % 2, :], f"xnc{t % 2}"),
                                          (xnc[:, t % 2, :], f"xnc{t % 2}", t % 2, xnTa[:, :, t * 128:(t + 1) * 128], f"xnTa{t}")))
                        norm_group(items[:4])
                        load_quarter(3, 2)
                        def final_tile(t):
                            gt = T0 + t
                            col = 32 + t
                            S.add("act", lambda e, t=t, col=col: e.activation(out=junk[:], in_=h[:, t, :], func=AF.Square,
                                                                              accum_out=ss[:, col:col + 1]),
                                  reads=[f"h{t}"], writes=["junk", f"ss{col}"])
                            S.add("act", lambda e, col=col: e.activation(out=rstd[:, col:col + 1], in_=ss[:, col:col + 1], func=AF.Ln,
                                                                         scale=1.0 / D, bias=EPS),
                                  reads=[f"ss{col}"], writes=[f"rs{col}"])
                            S.add("act", lambda e, col=col: e.activation(out=rstd[:, col:col + 1], in_=rstd[:, col:col + 1], func=AF.Exp,
                                                                         scale=-0.5),
                                  reads=[f"rs{col}"], writes=[f"rs{col}"])
                            S.add("dve", lambda e, t=t, col=col: e.scalar_tensor_tensor(
                                out=ot[:, t % 2, :], in0=h[:, t, :], scalar=rstd[:, col:col + 1], in1=gsl[:, 0, :],
                                op0=ALU.mult, op1=ALU.mult), reads=[f"h{t}", f"rs{col}", "g0"], writes=[f"ot{t % 2}"])

                            def sto(e, s, t=t, gt=gt):
                                e.dma_start(out=out_d[gt * 128:(gt + 1) * 128, :], in_=ot[:, t % 2, :]).then_inc(s, 16)
                            dma("sp", f"st{t % 2}", sto, 1, reads=[f"ot{t % 2}"], writes=[f"out{t % 2}"])

                        qslots = [0, 1, 0, 2]
                        for qd in range(4):
                            s_ = qslots[qd]
                            if qd == 1:
                                load_quarter(2, 0)
                            if ps_ + 1 < NPASS and qd == 2:
                                load_weight("k_out", w_out_v, w_out, 0, D, 0, 1024, ["R1b"])
                            if ps_ + 1 < NPASS and qd == 3:
                                load_weight("k_in", w_in_v, w_in, 0, D, 0, 2560, ["R0a", "R0b", "R1a"])
                            upv, dnv = slot_views(s_)
                            for m in range(NM):
                                for f in range(8):
                                    bk = f % 2

                                    def mup(e, f=f, bk=bk, m=m, upv=upv):
                                        for k in range(8):
                                            ins = e.matmul(PF[:, bk, :], lhsT=upv[:, k, f * 128:(f + 1) * 128],
                                                           rhs=xnTa[:, k, m * 512:(m + 1) * 512], start=(k == 0), stop=(k == 7))
                                        return ins
                                    S.add("pe", mup, reads=[f"xnTa{m * 4 + j}" for j in range(4)] + [f"R{s_}a"], writes=[f"PF{bk}"])
                                    S.add("act", lambda e, f=f, bk=bk: e.activation(out=rl[:, f % 2, :], in_=PF[:, bk, :], func=AF.Relu),
                                          reads=[f"PF{bk}"], writes=[f"rl{f % 2}"])
                                    if f % 2 == 0:
                                        S.add("act", lambda e, f=f: e.activation(out=hT[:, f, :], in_=rl[:, f % 2, :], func=AF.Square),
                                              reads=[f"rl{f % 2}"], writes=[f"hT{f}"])
                                    else:
                                        S.add("dve", lambda e, f=f: e.tensor_tensor(out=hT[:, f, :], in0=rl[:, f % 2, :], in1=rl[:, f % 2, :],
                                                                                    op=ALU.mult), reads=[f"rl{f % 2}"], writes=[f"hT{f}"])
                                if qd == 0 and m == 0:
                                    norm_group(items[4:])
                                for j in range(4):
                                    t = m * 4 + j
                                    tsl = slice(j * 128, (j + 1) * 128)

                                    db = 2 + 2 * (j % 2)

                                    def mdn(e, tsl=tsl, dnv=dnv, db=db):
                                        for n in range(2):
                                            for f in range(8):
                                                ins = e.matmul(PF[:, db + n, :], lhsT=hT[:, f, tsl], rhs=dnv[:, f, n * 512:(n + 1) * 512],
                                                               start=(f == 0), stop=(f == 7))
                                        return ins
                                    S.add("pe", mdn, reads=[f"hT{f}" for f in range(8)] + [f"R{s_}b"], writes=[f"PF{db}", f"PF{db + 1}"])
                                    S.add("dve", lambda e, t=t, db=db: e.tensor_tensor(out=h[:, t, :], in0=h[:, t, :],
                                                                                       in1=PF[:, db:db + 2, :].rearrange("p a b -> p (a b)"),
                                                                                       op=ALU.add),
                                          reads=[f"h{t}", f"PF{db}", f"PF{db + 1}"], writes=[f"h{t}"])
                                    if qd == 3:
                                        final_tile(t)
                                        if ps_ + 1 < NPASS:
                                            def ldh2(e, s, t=t, gt2=T0 + NTP + t):
                                                e.dma_start(out=h[:, t, :], in_=xmain[gt2 * 128:(gt2 + 1) * 128, :]).then_inc(s, 16)
                                            dma("sp", f"h{t}", ldh2, 1, reads=[f"ot{t % 2}"], writes=[f"h{t}"])
                        S.barrier()


            for ps_ in range(NPASS):
                run_pass(ps_)

      except _Stop:
          pass
      if True:
        S.add("sp", None, reads=["out0", "out1"])

        dsems = {k: es.enter_context(nc.semaphore("d_" + k)) for k in sorted(dma_key_names)}
        block = es.enter_context(nc.Block())
        S.emit({"pe": block.tensor, "act": block.scalar, "dve": block.vector, "pool": block.gpsimd, "sp": block.sync},
               sems, dsems)
    return nc, S


_CACHE = {}
STOP = None
DBG_OUT = []


class _Stop(Exception):
    pass


def kernel(x, mem, positions, norm_mix_w, w_in, conv_w, conv_b, conv_ln_w, conv_ln_b, ret_gn_w, w_out,
           norm_xattn_w, norm_mem_w, xq_w, xkv_w, xo_w, norm_mlp_w, mlp_up_w, mlp_down_w, norm_f_w):
    f32 = np.float32
    x = np.asarray(x, dtype=f32)
    mem = np.asarray(mem, dtype=f32)
    positions = np.asarray(positions).astype(np.int32)
    B = x.shape[0]
    consts = _host_consts()
    shared = {
        "gvec": np.stack([np.asarray(v, dtype=f32) for v in (norm_mix_w, norm_xattn_w, norm_mlp_w, norm_f_w, norm_mem_w)], 0),
        "w_in": np.ascontiguousarray(np.asarray(w_in, dtype=f32)),
        "w_out": np.ascontiguousarray(np.asarray(w_out, dtype=f32)),
        "xq_w": np.ascontiguousarray(np.asarray(xq_w, dtype=f32)),
        "xkv_w": np.ascontiguousarray(np.asarray(xkv_w, dtype=f32)),
        "xo_w": np.ascontiguousarray(np.asarray(xo_w, dtype=f32)),
        "up_w": np.ascontiguousarray(np.asarray(mlp_up_w, dtype=f32)),
        "down_w": np.ascontiguousarray(np.asarray(mlp_down_w, dtype=f32)),
        "convw": np.ascontiguousarray(np.asarray(conv_w, dtype=f32).reshape(CW, 4, 128).transpose(2, 1, 0)),
        "convb": np.ascontiguousarray(np.asarray(conv_b, dtype=f32).reshape(4, 128).T),
        "lnw": np.ascontiguousarray(np.asarray(conv_ln_w, dtype=f32).reshape(4, 128).T),
        "lnb": np.ascontiguousarray(np.asarray(conv_ln_b, dtype=f32).reshape(4, 128).T),
        "gnw": np.asarray(ret_gn_w, dtype=f32).reshape(1, 512),
    }
    shared.update(consts)
    in_maps = []
    npre_tok = NPRE * 128
    for c in range(NCORE):
        b, q = c // 4, c % 4
        t0 = q * TPC
        xpre = np.zeros((npre_tok, D), dtype=f32)
        ppre = np.zeros((npre_tok,), dtype=np.int32)
        if t0 > 0:
            xpre[npre_tok - t0:] = x[b, 0:t0]
            ppre[npre_tok - t0:] = positions[b, 0:t0]
        m = dict(shared)
        m["xmain"] = np.ascontiguousarray(x[b, t0:t0 + TPC])
        m["xpre"] = xpre
        m["posm"] = np.ascontiguousarray(positions[b, t0:t0 + TPC].reshape(TPC // 128, 128).T)
        m["posp"] = np.ascontiguousarray(ppre.reshape(NPRE, 128).T)
        m["memx"] = np.ascontiguousarray(mem[b])
        in_maps.append(m)
    if "nc" not in _CACHE:
        _CACHE["nc"] = build_nc()[0]
    nc = _CACHE["nc"]
    res = run_bass_kernel_spmd(nc, in_maps, core_ids=list(range(NCORE)))
    out = np.zeros((B, SEQ, D), dtype=f32)
    for c in range(NCORE):
        b, q = c // 4, c % 4
        out[b, q * TPC:(q + 1) * TPC] = res.results[c]["out"]
    return out
```

```python
import contextlib
import struct
import numpy as np
import ml_dtypes
import concourse.bass as bass
import concourse.mybir as mybir
from concourse.bass_utils import run_bass_kernel_spmd

F32 = mybir.dt.float32
BF16 = mybir.dt.bfloat16
I32 = mybir.dt.int32
AF = mybir.ActivationFunctionType
ALU = mybir.AluOpType
AX = mybir.AxisListType

D = 1024
SEQ = 8192
NCORE = 8
TPC = 2048
TH = 1024
NPASS = TPC // TH
NTP = TH // 128
NM = TH // 512
NPRE = (SEQ - TPC) // 128
H = 8
DK = 32
DV = 64
CW = 31
DFF = 4096
EPS = 1e-6
PI = float(np.pi)

ENGS = ("pe", "act", "dve", "pool", "sp")


class Op:
    __slots__ = ("eng", "fn", "deps", "needed", "sem", "val", "is_dma", "ndma", "name")

    def __init__(self, eng, fn, name=""):
        self.eng = eng
        self.fn = fn
        self.deps = []
        self.needed = False
        self.sem = None
        self.val = None
        self.is_dma = False
        self.ndma = 0
        self.name = name


class Sched:
    def __init__(self):
        self.ops = {e: [] for e in ENGS}
        self.last_w = {}
        self.readers = {}
        self.dma_keys = {}
        self.frozen = False

    def _dep(self, op, other):
        if other is None or other is op:
            return
        if other.is_dma:
            other = self.dma_keys[other.sem][-1]
            if other is op:
                return
        op.deps.append(other)
        other.needed = True

    def add(self, eng, fn, reads=(), writes=(), name="", dma_key=None, ndma=0):
        if self.frozen:
            return None
        op = Op(eng, fn, name)
        if dma_key is not None:
            op.is_dma = True
            op.ndma = ndma
            op.sem = dma_key
        for r in reads:
            w = self.last_w.get(r)
            if w is not None:
                if not (w.eng == eng and eng == "pe" and not w.is_dma and not op.is_dma):
                    self._dep(op, w)
            if r[:2] in ("PF", "PT"):
                for rd in self.readers.get(r, []):
                    if rd.eng != eng:
                        self._dep(op, rd)
            self.readers.setdefault(r, []).append(op)
        for r in writes:
            w = self.last_w.get(r)
            if w is not None:
                same = (w.eng == eng and eng == "pe" and not w.is_dma and not op.is_dma)
                if not same:
                    self._dep(op, w)
            for rd in self.readers.get(r, []):
                same = (rd.eng == eng and eng == "pe" and not rd.is_dma and not op.is_dma)
                if not same:
                    self._dep(op, rd)
            self.last_w[r] = op
            self.readers[r] = []
        if dma_key is not None:
            self.dma_keys.setdefault(dma_key, []).append(op)
        self.ops[eng].append(op)
        return op

    def barrier(self):
        if self.frozen:
            return
        lasts = []
        for e in ENGS:
            for op in reversed(self.ops[e]):
                if not op.is_dma and op.fn is not None:
                    lasts.append(op)
                    break
        dl = [lst[-1] for lst in self.dma_keys.values() if lst]
        for e in ENGS:
            op = Op(e, None, "barrier")
            for o in lasts:
                if o.eng != e or e != "pe":
                    op.deps.append(o)
                    o.needed = True
            for o in dl:
                op.deps.append(o)
            self.ops[e].append(op)

    def emit(self, block_engines, sems, dma_sems):
        for e in ENGS:
            cnt = 0
            for op in self.ops[e]:
                if op.is_dma:
                    continue
                if op.needed:
                    cnt += 1
                    op.val = cnt
                op.sem = ("eng", e)
        for key, lst in self.dma_keys.items():
            cnt = 0
            for op in lst:
                cnt += 16 * op.ndma
                op.val = cnt
        self.stats = {}

        def semh(op):
            if op.is_dma:
                return dma_sems[op.sem]
            return sems[op.sem[1]]

        def run(e, eng):
            seen = {}
            nwait = 0
            for op in self.ops[e]:
                need = {}
                for d in op.deps:
                    if d.val is None:
                        raise RuntimeError(f"dep {d.name} has no val")
                    if d.sem not in need or need[d.sem][0] < d.val:
                        need[d.sem] = (d.val, semh(d))
                for k, (v, hh) in need.items():
                    if seen.get(k, 0) >= v:
                        continue
                    eng.wait_ge(hh, v)
                    nwait += 1
                    seen[k] = v
                if op.fn is None:
                    continue
                if op.is_dma:
                    op.fn(eng, dma_sems[op.sem])
                else:
                    ins = op.fn(eng)
                    if op.needed:
                        ins.then_inc(sems[e], 1)
            self.stats[e] = (len(self.ops[e]), nwait)

        for e in ENGS:
            if not self.ops[e]:
                continue

            def section(eng, e=e):
                run(e, eng)
            block_engines[e](section)


def _split2pi():
    def trunc(v, bits):
        i = struct.unpack("<I", struct.pack("<f", np.float32(v)))[0]
        i &= ~((1 << (23 - bits)) - 1) & 0xFFFFFFFF
        return struct.unpack("<f", struct.pack("<I", i))[0]
    tp = 2 * np.pi
    c1 = trunc(tp, 8)
    c2 = trunc(tp - c1, 10)
    c3 = float(np.float32(tp - c1 - c2))
    return float(c1), float(c2), c3


C1, C2, C3 = _split2pi()


def _host_consts():
    lg = np.log1p(-np.exp2(-5.0 - np.arange(H, dtype=np.float64)))
    scale = DK ** -0.5
    c = {}
    c["ident"] = np.eye(128, dtype=np.float32).astype(ml_dtypes.bfloat16)
    c["onesm"] = np.full((128, 128), 1.0 / 512.0, dtype=np.float32).astype(ml_dtypes.bfloat16)
    j = np.arange(128)[:, None]
    i = np.arange(128)[None, :]
    cj, ci = j // 64, i // 64
    mask = np.zeros((128, H, 128), dtype=np.float64)
    for h in range(H):
        m = np.where(cj == ci, np.exp(lg[h] * np.abs(i - j)),
                     np.where(cj < ci, np.exp(lg[h] * (i - j)), 0.0))
        mask[:, h, :] = m * scale
    c["mask"] = mask.astype(np.float32)
    xi = np.zeros((64, 4, 128), dtype=np.float64)
    g128 = np.zeros((64, 4, 64), dtype=np.float64)
    for hp in range(4):
        for a in range(2):
            h = 2 * hp + a
            xi[32 * a:32 * a + 32, hp, :] = scale * np.exp(lg[h] * (np.arange(128) + 1.0))[None, :]
            g128[32 * a:32 * a + 32, hp, :] = np.exp(lg[h] * 128.0)
    c["xifm"] = xi.astype(np.float32)
    c["g128"] = g128.astype(np.float32)
    c["zeta"] = np.exp(lg[None, :] * (127.0 - np.arange(128))[:, None]).astype(np.float32)
    dist = (NPRE * 128 - 1) - (np.arange(NPRE)[None, :, None] * 128 + np.arange(128)[:, None, None])
    c["wpre"] = np.exp(lg[None, None, :] * dist).astype(np.float32)
    half = DK // 2
    invf = (np.float32(10000.0) ** (-(np.arange(half, dtype=np.float32)) / np.float32(half))).astype(np.float32)
    c["invf"] = np.broadcast_to(invf[None, :], (128, half)).copy()
    return c


def build_nc():
    nc = bass.Bass("TRN2", target_bir_lowering=False)

    def din(name, shape, dt=F32):
        return nc.dram_tensor(name, list(shape), dt, kind="ExternalInput").ap()

    xmain = din("xmain", [TPC, D])
    xpre = din("xpre", [NPRE * 128, D])
    posm = din("posm", [128, TPC // 128], I32)
    posp = din("posp", [128, NPRE], I32)
    memx = din("memx", [256, D])
    gvec = din("gvec", [5, D])
    w_in = din("w_in", [D, 2560])
    w_out = din("w_out", [D, D])
    xq_w = din("xq_w", [D, D])
    xkv_w = din("xkv_w", [D, 2 * D])
    xo_w = din("xo_w", [D, D])
    up_w = din("up_w", [D, DFF])
    down_w = din("down_w", [DFF, D])
    convw_d = din("convw", [128, 4, CW])
    convb_d = din("convb", [128, 4])
    lnw_d = din("lnw", [128, 4])
    lnb_d = din("lnb", [128, 4])
    gnw_d = din("gnw", [1, 512])
    ident_d = din("ident", [128, 128], BF16)
    onesm_d = din("onesm", [128, 128], BF16)
    mask_d = din("mask", [128, H, 128])
    xifm_d = din("xifm", [64, 4, 128])
    g128_d = din("g128", [64, 4, 64])
    zeta_d = din("zeta", [128, H])
    wpre_d = din("wpre", [128, NPRE, H])
    invf_d = din("invf", [128, 16])
    out_d = nc.dram_tensor("out", [TPC, D], F32, kind="ExternalOutput").ap()

    S = Sched()
    dma_key_names = set()
    del DBG_OUT[:]

    def checkpoint(tag, dumps=()):
        if STOP != tag:
            return
        S.barrier()
        for (name, ap, shape, dt, rres) in dumps:
            dd = nc.dram_tensor("dbg_" + name, list(shape), dt, kind="ExternalOutput").ap()
            DBG_OUT.append("dbg_" + name)

            def fn(e, s, dd=dd, ap=ap):
                e.dma_start(out=dd, in_=ap).then_inc(s, 16)
            dma_key_names.add("dbg")
            S.add("sp", fn, reads=list(rres), writes=["dbgout"], dma_key="dbg", ndma=1)
        S.barrier()
        S.frozen = True

    with contextlib.ExitStack() as es:
      try:
            _cnt = [0]

            def sbuf(stack, name, shape, dt):
                _cnt[0] += 1
                return stack.enter_context(nc.sbuf_tensor(f"sb{_cnt[0]}_{name}", list(shape), dt))

            h = sbuf(es, "h", [128, NTP, D], F32)
            W = sbuf(es, "W", [128, 2 * 16384], BF16)
            gsl = sbuf(es, "gsl", [128, 2, D], F32)
            ident = sbuf(es, "ident", [128, 128], BF16)
            onesm = sbuf(es, "onesm", [128, 128], BF16)
            mask = sbuf(es, "mask", [128, H, 128], F32)
            xifm = sbuf(es, "xifm", [64, 4, 128], F32)
            g128 = sbuf(es, "g128", [64, 4, 64], F32)
            zeta = sbuf(es, "zeta", [128, H], F32)
            invf = sbuf(es, "invf", [128, 16], F32)
            convw = sbuf(es, "convw", [128, 4, CW], F32)
            convb = sbuf(es, "convb", [128, 4], F32)
            lnw = sbuf(es, "lnw", [128, 4], F32)
            lnb = sbuf(es, "lnb", [128, 4], F32)
            gnw = sbuf(es, "gnw", [128, 512], F32)
            neghalf = sbuf(es, "neghalf", [128, 8], F32)
            cosm = sbuf(es, "cosm", [128, TPC // 128, 16], F32)
            sinm = sbuf(es, "sinm", [128, TPC // 128, 16], F32)
            kTm = sbuf(es, "kTm", [128, 8, 256], BF16)
            vm = sbuf(es, "vm", [128, 2, D], BF16)
            Sst = sbuf(es, "Sst", [64, 4, 64], F32)
            Sbf = sbuf(es, "Sbf", [64, 4, 128], BF16)
            uhalo = sbuf(es, "uhalo", [128, 4, 32], BF16)
            ss = sbuf(es, "ss", [128, 64], F32)
            rstd = sbuf(es, "rstd", [128, 64], F32)
            junk = sbuf(es, "junk", [128, D], BF16)

            PT = es.enter_context(nc.psum_tensor("PT", [128, 2, 1024], BF16))
            PF = es.enter_context(nc.psum_tensor("PF", [128, 6, 512], F32))

            sems = {e: es.enter_context(nc.semaphore("s_" + e)) for e in ENGS}

            def dma(eng, key, fn, n, reads=(), writes=()):
                dma_key_names.add(key)
                S.add(eng, fn, reads=reads, writes=writes, dma_key=key, ndma=n)

            def wview(off, k, n):
                return W[:, off:off + k * n].rearrange("p (k n) -> p k n", k=k)

            def load_weight(key, dst3, src, r0, nrows, c0, ncols, wres):
                nk = nrows // 128
                cs = min(ncols, 2048)
                pieces = []
                for k0 in range(0, nk, 2):
                    for cc in range(0, ncols, cs):
                        pieces.append((k0, cc, min(cs, ncols - cc)))

                def fn(e, s):
                    for (k0, cc, cw_) in pieces:
                        srcv = src[r0 + k0 * 128: r0 + (k0 + 2) * 128, c0 + cc: c0 + cc + cw_]
                        e.dma_start(out=dst3[:, k0:k0 + 2, cc:cc + cw_],
                                    in_=srcv.rearrange("(k p) n -> p k n", p=128)).then_inc(s, 16)
                dma("pool", key, fn, len(pieces), writes=wres)

            def load_g(slot, row):
                def fn(e, s):
                    e.dma_start(out=gsl[:, slot, :], in_=gvec[row:row + 1, :].broadcast_to([128, D])).then_inc(s, 16)
                dma("sp", f"g{slot}", fn, 1, writes=[f"g{slot}"])

            def norm_tile(src, src_res, slot, col, xn_t, xn_res, pt_bank, dst, dst_res, extra_reads=()):
                norm_tile_a(src, src_res, slot, col, xn_t, xn_res, extra_reads)
                norm_tile_b(xn_t, xn_res, pt_bank, dst, dst_res)

            def norm_group(items):
                n = len(items)
                for i in range(n + 1):
                    if i < n:
                        norm_tile_a(*items[i][0])
                    if i >= 1:
                        norm_tile_b(*items[i - 1][1])

            RSTD_MODE = ["pool"]

            def norm_tile_a(src, src_res, slot, col, xn_t, xn_res, extra_reads=()):
                S.add("act", lambda e: e.activation(out=junk[:], in_=src, func=AF.Square, accum_out=ss[:, col:col + 1]),
                      reads=[src_res] + list(extra_reads), writes=["junk", f"ss{col}"])
                if RSTD_MODE[0] == "pool":
                    S.add("pool", lambda e: e.tensor_scalar(out=rstd[:, col:col + 1], in0=ss[:, col:col + 1], scalar1=1.0 / D,
                                                            scalar2=EPS, op0=ALU.mult, op1=ALU.add),
                          reads=[f"ss{col}"], writes=[f"rs{col}"])
                    S.add("pool", lambda e: e.tensor_tensor(out=rstd[:, col:col + 1], in0=rstd[:, col:col + 1],
                                                            in1=neghalf[:, 0:1], op=ALU.pow),
                          reads=[f"rs{col}", "neghalf"], writes=[f"rs{col}"])
                else:
                    S.add("act", lambda e: e.activation(out=rstd[:, col:col + 1], in_=ss[:, col:col + 1], func=AF.Ln,
                                                        scale=1.0 / D, bias=EPS),
                          reads=[f"ss{col}"], writes=[f"rs{col}"])
                    S.add("act", lambda e: e.activation(out=rstd[:, col:col + 1], in_=rstd[:, col:col + 1], func=AF.Exp, scale=-0.5),
                          reads=[f"rs{col}"], writes=[f"rs{col}"])
                S.add("dve", lambda e: e.scalar_tensor_tensor(out=xn_t, in0=src, scalar=rstd[:, col:col + 1],
                                                              in1=gsl[:, slot, :], op0=ALU.mult, op1=ALU.mult),
                      reads=[src_res, f"rs{col}", f"g{slot}"], writes=[xn_res])

            def norm_tile_b(xn_t, xn_res, pt_bank, dst, dst_res):
                def tr(e):
                    for k in range(8):
                        ins = e.transpose(out=PT[:, pt_bank, k * 128:(k + 1) * 128], in_=xn_t[:, k * 128:(k + 1) * 128],
                                          identity=ident[:])
                    return ins
                S.add("pe", tr, reads=[xn_res, "ident"], writes=[f"PT{pt_bank}"])
                S.add("act", lambda e: e.activation(out=dst, in_=PT[:, pt_bank, :].rearrange("p (k n) -> p k n", k=8),
                                                    func=AF.Copy),
                      reads=[f"PT{pt_bank}"], writes=[dst_res])

            def rotary(src3, cos_ap, sin_ap, nh, ta, tb, dst3, rres, wres, eng="dve", tag=""):
                cb = cos_ap.unsqueeze(1).broadcast_to([128, nh, 16])
                sb_ = sin_ap.unsqueeze(1).broadcast_to([128, nh, 16])
                x1 = src3[:, :, 0:16]
                x2 = src3[:, :, 16:32]
                S.add(eng, lambda e: e.tensor_tensor(out=ta, in0=x1, in1=cb, op=ALU.mult), reads=rres, writes=["rot_ta" + tag])
                S.add(eng, lambda e: e.tensor_tensor(out=tb, in0=x2, in1=sb_, op=ALU.mult), reads=rres, writes=["rot_tb" + tag])
                S.add(eng, lambda e: e.tensor_tensor(out=dst3[:, :, 0:16], in0=ta, in1=tb, op=ALU.subtract),
                      reads=["rot_ta" + tag, "rot_tb" + tag], writes=[wres + "a"])
                S.add(eng, lambda e: e.tensor_tensor(out=ta, in0=x1, in1=sb_, op=ALU.mult), reads=rres + [wres + "a"], writes=["rot_ta" + tag])
                S.add(eng, lambda e: e.tensor_tensor(out=tb, in0=x2, in1=cb, op=ALU.mult), reads=rres + [wres + "a"], writes=["rot_tb" + tag])
                S.add(eng, lambda e: e.tensor_tensor(out=dst3[:, :, 16:32], in0=ta, in1=tb, op=ALU.add),
                      reads=["rot_ta" + tag, "rot_tb" + tag], writes=[wres + "b"])

            with contextlib.ExitStack() as st:
                posi = sbuf(st, "posi", [128, 64], I32)
                posf = sbuf(st, "posf", [128, 64], F32)
                ang = sbuf(st, "ang", [128, 64, 16], F32)
                kf = sbuf(st, "kf", [128, 64, 16], F32)
                ki = sbuf(st, "ki", [128, 64, 16], I32)
                mm_ = sbuf(st, "mm_", [128, 64, 16], F32)
                cosp = sbuf(st, "cosp", [128, NPRE, 16], F32)
                sinp = sbuf(st, "sinp", [128, NPRE, 16], F32)
                wpre = sbuf(st, "wpre", [128, NPRE, H], F32)
                xs = sbuf(st, "xs", [128, 2, D], F32)
                xn2 = sbuf(st, "xn2", [128, 2, D], BF16)
                xnTp = sbuf(st, "xnTp", [128, 2, 8, 128], BF16)
                memT = sbuf(st, "memT", [128, 8, 256], BF16)
                krot = sbuf(st, "krot", [128, 2, 256], F32)
                rta = sbuf(st, "rta", [128, 16, 16], F32)
                rtb = sbuf(st, "rtb", [128, 16, 16], F32)
                kzp = sbuf(st, "kzp", [128, 2, 256], BF16)
                vbp = sbuf(st, "vbp", [128, 2, 512], BF16)
                thh = sbuf(st, "thh", [128, 32], F32)

                def cfn(e, s):
                    for (dst, src) in [(ident[:], ident_d), (onesm[:], onesm_d), (mask[:], mask_d), (xifm[:], xifm_d),
                                       (g128[:], g128_d), (zeta[:], zeta_d), (invf[:], invf_d), (convw[:], convw_d),
                                       (convb[:], convb_d), (lnw[:], lnw_d), (lnb[:], lnb_d), (wpre[:], wpre_d),
                                       (posi[:, 0:NPRE], posp), (posi[:, NPRE:64], posm)]:
                        e.dma_start(out=dst, in_=src).then_inc(s, 16)
                    e.dma_start(out=gnw[:], in_=gnw_d[0:1, :].broadcast_to([128, 512])).then_inc(s, 16)
                dma("sp", "const", cfn, 15, writes=["const"])
                load_g(0, 4)
                load_g(1, 0)
                S.add("pool", lambda e: e.memset(neghalf[:], -0.5), writes=["neghalf"])
                S.add("dve", lambda e: e.memset(Sst[:], 0.0), writes=["Sst"])
                S.add("pool", lambda e: e.memset(Sbf[:], 0.0), writes=["Sbf"])
                S.add("dve", lambda e: e.memset(uhalo[:], 0.0), writes=["uhalo"])
                S.barrier()
                RSTD_MODE[0] = "act"
                w_in_v = wview(0, 8, 2560)
                w_out_v = wview(24576, 8, 1024)
                xkst = sbuf(st, "xkst", [128, 8, 1024], BF16)
                xkvk = xkst
                xkvv = xkst
                load_weight("wk", xkvk, xkv_w, 0, D, 0, 1024, ["XK"])
                load_weight("k_in", w_in_v, w_in, 0, D, 0, 2560, ["R0a", "R0b", "R1a"])
                load_weight("k_out", w_out_v, w_out, 0, D, 0, 1024, ["R1b"])
                S.add("pool", lambda e: e.tensor_scalar(out=convw[:], in0=convw[:], scalar1=0.5, scalar2=None, op0=ALU.mult),
                      reads=["const"], writes=["convw"])
                S.add("dve", lambda e: e.tensor_copy(out=posf[:], in_=posi[:]), reads=["const"], writes=["posf"])
                S.add("dve", lambda e: e.tensor_tensor(out=ang[:], in0=posf[:].unsqueeze(2).broadcast_to([128, 64, 16]),
                                                       in1=invf[:].unsqueeze(1).broadcast_to([128, 64, 16]), op=ALU.mult),
                      reads=["posf", "const"], writes=["ang"])

                def reduce_ang(tag, shift):
                    src = ang
                    if shift != 0.0:
                        S.add("dve", lambda e: e.tensor_scalar(out=mm_[:], in0=ang[:], scalar1=shift, scalar2=None, op0=ALU.add),
                              reads=["ang", "mm_"], writes=["angs"])
                        src = mm_
                        sres = "angs"
                    else:
                        sres = "ang"
                    S.add("dve", lambda e: e.tensor_scalar(out=kf[:], in0=src[:], scalar1=float(1.0 / (2 * np.pi)), scalar2=None,
                                                           op0=ALU.mult), reads=[sres, "kfr"], writes=["kf"])
                    S.add("dve", lambda e: e.tensor_copy(out=ki[:], in_=kf[:]), reads=["kf"], writes=["ki"])
                    S.add("dve", lambda e: e.tensor_copy(out=kf[:], in_=ki[:]), reads=["ki"], writes=["kf"])
                    red = sbuf(st, "red" + tag, [128, 64, 16], F32)
                    S.add("dve", lambda e: e.scalar_tensor_tensor(out=red[:], in0=kf[:], scalar=-C1, in1=src[:], op0=ALU.mult,
                                                                  op1=ALU.add), reads=["kf", sres], writes=["red" + tag])
                    S.add("dve", lambda e: e.scalar_tensor_tensor(out=red[:], in0=kf[:], scalar=-C2, in1=red[:], op0=ALU.mult,
                                                                  op1=ALU.add), reads=["kf", "red" + tag], writes=["red" + tag])
                    S.add("dve", lambda e: e.scalar_tensor_tensor(out=red[:], in0=kf[:], scalar=-C3, in1=red[:], op0=ALU.mult,
                                                                  op1=ALU.add), reads=["kf", "red" + tag], writes=["red" + tag])
                    S.add("dve", lambda e: e.tensor_scalar(out=kf[:], in0=red[:], scalar1=PI, scalar2=-2 * PI, op0=ALU.is_gt,
                                                           op1=ALU.mult), reads=["red" + tag], writes=["kf"])
                    S.add("dve", lambda e: e.tensor_tensor(out=red[:], in0=red[:], in1=kf[:], op=ALU.add),
                          reads=["kf", "red" + tag], writes=["red" + tag])
                    S.add("dve", lambda e: e.tensor_scalar(out=kf[:], in0=red[:], scalar1=-PI, scalar2=2 * PI, op0=ALU.is_lt,
                                                           op1=ALU.mult), reads=["red" + tag], writes=["kf"])
                    S.add("dve", lambda e: e.tensor_tensor(out=red[:], in0=red[:], in1=kf[:], op=ALU.add),
                          reads=["kf", "red" + tag], writes=["red" + tag])
                    S.add("dve", lambda e: e.tensor_scalar(out=red[:], in0=red[:], scalar1=PI, scalar2=-PI, op0=ALU.min,
                                                           op1=ALU.max), reads=["red" + tag], writes=["red" + tag])
                    return red
                rs_ = reduce_ang("s", 0.0)
                S.add("act", lambda e: e.activation(out=sinp[:], in_=rs_[:, 0:NPRE, :], func=AF.Sin), reads=["reds"], writes=["sinp"])
                S.add("act", lambda e: e.activation(out=sinm[:], in_=rs_[:, NPRE:64, :], func=AF.Sin), reads=["reds"], writes=["sinm"])
                S.add("dve", lambda e: e.tensor_scalar(out=ang[:], in0=rs_[:], scalar1=PI / 2, scalar2=None, op0=ALU.add),
                      reads=["reds", "ang", "sinp", "sinm"], writes=["ang"])
                S.add("dve", lambda e: e.tensor_scalar(out=kf[:], in0=ang[:], scalar1=PI, scalar2=-2 * PI, op0=ALU.is_gt, op1=ALU.mult),
                      reads=["ang"], writes=["kf"])
                S.add("dve", lambda e: e.tensor_tensor(out=ang[:], in0=ang[:], in1=kf[:], op=ALU.add), reads=["kf", "ang"], writes=["ang"])
                S.add("dve", lambda e: e.tensor_scalar(out=ang[:], in0=ang[:], scalar1=PI, scalar2=-PI, op0=ALU.min, op1=ALU.max),
                      reads=["ang"], writes=["ang"])
                S.add("act", lambda e: e.activation(out=cosp[:], in_=ang[:, 0:NPRE, :], func=AF.Sin), reads=["ang"], writes=["cosp"])
                S.add("act", lambda e: e.activation(out=cosm[:], in_=ang[:, NPRE:64, :], func=AF.Sin), reads=["ang"], writes=["cosm"])

                checkpoint("tables", [("cosm", cosm[:], [128, 16, 16], F32, []), ("sinm", sinm[:], [128, 16, 16], F32, []),
                                      ("cosp", cosp[:], [128, NPRE, 16], F32, []), ("sinp", sinp[:], [128, NPRE, 16], F32, [])])
                checkpoint("memkv", [("kTm", kTm[:], [128, 8, 256], BF16, []), ("vm", vm[:], [128, 2, D], BF16, [])])
                for mt in range(2):
                    def ldm(e, s, mt=mt):
                        e.dma_start(out=xs[:, mt, :], in_=memx[mt * 128:(mt + 1) * 128, :]).then_inc(s, 16)
                    dma("sp", f"xs{mt}", ldm, 1, writes=[f"xs{mt}"])
                    norm_tile(xs[:, mt, :], f"xs{mt}", 0, mt, xn2[:, mt, :], f"xn{mt}", 0,
                              memT[:, :, mt * 128:(mt + 1) * 128], f"memT{mt}")
                for c in range(8):
                    def mk(e, c=c):
                        for k in range(8):
                            ins = e.matmul(PF[:, c % 2, 0:256], lhsT=xkvk[:, k, c * 128:(c + 1) * 128], rhs=memT[:, k, :],
                                           start=(k == 0), stop=(k == 7))
                        return ins
                    S.add("pe", mk, reads=["XK", "memT0", "memT1"], writes=[f"PF{c % 2}"])
                    S.add("act", lambda e, c=c: e.activation(out=kTm[:, c, :], in_=PF[:, c % 2, 0:256], func=AF.Copy),
                          reads=[f"PF{c % 2}"], writes=["kTm"])
                load_weight("wk", xkvv, xkv_w, 0, D, 1024, 1024, ["XK"])
                def pre_stage0(pt):
                    sl = pt % 2

                    def ldx(e, s, pt=pt, sl=sl):
                        e.dma_start(out=xs[:, sl, :], in_=xpre[pt * 128:(pt + 1) * 128, :]).then_inc(s, 16)
                    dma("sp", f"xs{sl}", ldx, 1, writes=[f"xs{sl}"])
                    col = 2 + (pt % 4)
                    norm_tile_a(xs[:, sl, :], f"xs{sl}", 1, col, xn2[:, sl, :], f"xn{sl}")

                def pre_stage1(pt):
                    sl = pt % 2
                    norm_tile_b(xn2[:, sl, :], f"xn{sl}", sl, xnTp[:, sl, :, :], f"xnTp{sl}")

                def pre_stage2(pt):
                    sl = pt % 2
                    bK, bV = (0, 1) if sl == 0 else (2, 3)

                    def mkv(e, sl=sl, bK=bK, bV=bV):
                        for k in range(8):
                            e.matmul(PF[:, bK, 0:256], lhsT=xnTp[:, sl, k, :], rhs=w_in_v[:, k, 1280:1536],
                                     start=(k == 0), stop=(k == 7))
                        for k in range(8):
                            ins = e.matmul(PF[:, bV, :], lhsT=xnTp[:, sl, k, :], rhs=w_in_v[:, k, 1536:2048],
                                           start=(k == 0), stop=(k == 7))
                        return ins
                    S.add("pe", mkv, reads=[f"xnTp{sl}", "R0a", "R0b", "R1a"], writes=[f"PF{bK}", f"PF{bV}"])
                    S.add("act", lambda e, sl=sl, bV=bV: e.activation(out=vbp[:, sl, :], in_=PF[:, bV, :], func=AF.Copy),
                          reads=[f"PF{bV}"], writes=[f"vbp{sl}"])
                    rotary(PF[:, bK, 0:256].rearrange("p (h d) -> p h d", h=8), cosp[:, pt, :], sinp[:, pt, :], 8,
                           rta[:, 8 * sl:8 * sl + 8, :], rtb[:, 8 * sl:8 * sl + 8, :],
                           krot[:, sl, :].rearrange("p (h d) -> p h d", h=8),
                           [f"PF{bK}", "cosp", "sinp"], f"krot{sl}", eng="dve", tag=f"p{sl}")
                    S.add("dve", lambda e, pt=pt, sl=sl: e.tensor_tensor(
                        out=kzp[:, sl, :].rearrange("p (h d) -> p h d", h=8),
                        in0=krot[:, sl, :].rearrange("p (h d) -> p h d", h=8),
                        in1=wpre[:, pt, :].unsqueeze(2).broadcast_to([128, 8, 32]), op=ALU.mult),
                        reads=[f"krot{sl}a", f"krot{sl}b", "const"], writes=[f"kzp{sl}"])
                    if pt == NPRE - 1:
                        for c in range(4):
                            def mab(e, c=c, sl=sl):
                                for k in range(8):
                                    e.matmul(PF[:, 4, 0:32], lhsT=w_in_v[:, k, c * 128:(c + 1) * 128], rhs=xnTp[:, sl, k, 96:128],
                                             start=(k == 0), stop=(k == 7))
                                for k in range(8):
                                    ins = e.matmul(PF[:, 4, 256:288], lhsT=w_in_v[:, k, 512 + c * 128:512 + (c + 1) * 128],
                                                   rhs=xnTp[:, sl, k, 96:128], start=(k == 0), stop=(k == 7), skip_group_check=True)
                                return ins
                            S.add("pe", mab, reads=[f"xnTp{sl}", "R0a", "R0b", "R1a"], writes=["PF4"])
                            S.add("act", lambda e: e.activation(out=thh[:], in_=PF[:, 4, 256:288], func=AF.Tanh, scale=0.5),
                                  reads=["PF4"], writes=["thh"])
                            S.add("dve", lambda e, c=c: e.scalar_tensor_tensor(out=uhalo[:, c, :], in0=thh[:], scalar=1.0,
                                                                              in1=PF[:, 4, 0:32], op0=ALU.add, op1=ALU.mult),
                                  reads=["thh", "PF4"], writes=["uhalo"])

                def pre_stage3(pt):
                    sl = pt % 2

                    def mst(e, pt=pt, sl=sl):
                        for hp in range(4):
                            ins = e.matmul(PF[0:64, 5, hp * 128:(hp + 1) * 128], lhsT=kzp[:, sl, hp * 64:(hp + 1) * 64],
                                           rhs=vbp[:, sl, hp * 128:(hp + 1) * 128], start=(pt == 0 and hp == 0),
                                           stop=(pt == NPRE - 1), skip_group_check=True)
                        return ins
                    S.add("pe", mst, reads=[f"kzp{sl}", f"vbp{sl}"], writes=["PF5"])

                for it in range(NPRE + 3):
                    if it < NPRE:
                        pre_stage0(it)
                    if 0 <= it - 1 < NPRE:
                        pre_stage1(it - 1)
                    if 0 <= it - 2 < NPRE:
                        pre_stage2(it - 2)
                    if 0 <= it - 3 < NPRE:
                        pre_stage3(it - 3)
                for a in range(2):
                    S.add("dve", lambda e, a=a: e.tensor_copy(
                        out=Sst[32 * a:32 * a + 32, :, :],
                        in_=PF[32 * a:32 * a + 32, 5, :].rearrange("p (hp x) -> p hp x", hp=4)[:, :, 64 * a:64 * a + 64]),
                        reads=["PF5"], writes=["Sst"])
                for a in range(2):
                    S.add("act", lambda e, a=a: e.activation(out=Sbf[32 * a:32 * a + 32, :, 64 * a:64 * a + 64],
                                                             in_=Sst[32 * a:32 * a + 32, :, :], func=AF.Copy),
                          reads=["Sst"], writes=["Sbf"])
                for mc in range(2):
                    for n in range(2):
                        bk = 2 + (mc * 2 + n) % 2

                        def mv(e, mc=mc, n=n, bk=bk):
                            for k in range(8):
                                ins = e.matmul(PF[:, bk, :], lhsT=memT[:, k, mc * 128:(mc + 1) * 128],
                                               rhs=xkvv[:, k, n * 512:(n + 1) * 512], start=(k == 0), stop=(k == 7))
                            return ins
                        S.add("pe", mv, reads=["XK", "memT0", "memT1"], writes=[f"PF{bk}"])
                        S.add("dve", lambda e, mc=mc, n=n, bk=bk: e.tensor_copy(out=vm[:, mc, n * 512:(n + 1) * 512], in_=PF[:, bk, :]),
                              reads=[f"PF{bk}"], writes=["vm"])

                checkpoint("prefix", [("Sst", Sst[:], [64, 4, 64], F32, []), ("uhalo", uhalo[:], [128, 4, 32], BF16, [])])
                S.barrier()

            def run_pass(ps_):
                T0 = ps_ * NTP
                RSTD_MODE[0] = "pool"
                with contextlib.ExitStack() as st:
                    xn = sbuf(st, "xn", [128, 2, D], BF16)
                    xnT = sbuf(st, "xnT", [128, 8, 512], BF16)
                    tht = sbuf(st, "tht", [128, 512], F32)
                    u = sbuf(st, "u", [128, 4, 544], BF16)
                    dg = sbuf(st, "dg", [128, 2, CW, 128], BF16)
                    tmpn = sbuf(st, "tmpn", [128, 2, 512], F32)
                    ybf = sbuf(st, "ybf", [128, 4, 512], BF16)
                    ysqb = sbuf(st, "ysqb", [128, 4, 512], BF16)
                    t_a = sbuf(st, "t_a", [128, 512], F32)
                    t_b = sbuf(st, "t_b", [128, 512], F32)
                    yconv = sbuf(st, "yconv", [128, 4, 512], BF16)
                    rot = sbuf(st, "rot", [128, 512], F32)
                    rtaA = sbuf(st, "rta2", [128, 16, 16], F32)
                    rtbA = sbuf(st, "rtb2", [128, 16, 16], F32)
                    qkb = sbuf(st, "qkb", [128, 2, 512], BF16)
                    kz = sbuf(st, "kz", [128, 2, 256], BF16)
                    vb = sbuf(st, "vb", [128, 2, 512], BF16)
                    gs = sbuf(st, "gs", [128, 2, 512], F32)
                    qkT = sbuf(st, "qkT", [64, 8, 128], BF16)
                    qx = sbuf(st, "qx", [64, 4, 128], BF16)
                    scm = sbuf(st, "scm", [128, 8, 128], BF16)
                    yret = sbuf(st, "yret", [128, 512], BF16)
                    yretT = sbuf(st, "yretT", [128, 4, 128], BF16)
                    gst = sbuf(st, "gst", [128, 32], F32)

                    if ps_ > 0:
                        load_g(1, 0)
                    for m in range(NM):
                        items = []
                        for j in range(4):
                            t = m * 4 + j
                            gt = T0 + t

                            def ldh(e, s, t=t, gt=gt):
                                e.dma_start(out=h[:, t, :], in_=xmain[gt * 128:(gt + 1) * 128, :]).then_inc(s, 16)
                            if ps_ == 0:
                                dma("sp", f"h{t}", ldh, 1, writes=[f"h{t}"])
                            items.append(((h[:, t, :], f"h{t}", 1, 8 + t, xn[:, t % 2, :], f"xn{t % 2}"),
                                          (xn[:, t % 2, :], f"xn{t % 2}", 0, xnT[:, :, j * 128:(j + 1) * 128], f"xnT{j}")))
                        norm_group(items)
                        xnT_res = [f"xnT{j}" for j in range(4)]
                        if ps_ == 0 and m == 0:
                            checkpoint("sA1", [("xnT", xnT[:], [128, 8, 512], BF16, [])])
                        ures = [f"u{c}" for c in range(4)]
                        if m == 0:
                            S.add("pool", lambda e: e.tensor_copy(out=u[:, :, 0:32], in_=uhalo[:]), reads=["uhalo"], writes=["uh"])
                        else:
                            S.add("pool", lambda e: e.tensor_copy(out=u[:, :, 0:32], in_=u[:, :, 512:544]), reads=ures, writes=["uh"])
                        def gen_diag(c):
                            S.add("dve", lambda e, c=c: e.tensor_tensor(
                                out=dg[:, c % 2, :, :], in0=ident[:].unsqueeze(1).broadcast_to([128, CW, 128]),
                                in1=convw[:, c, :].unsqueeze(2).broadcast_to([128, CW, 128]), op=ALU.mult),
                                reads=["ident", "convw", "const"], writes=[f"dg{c % 2}"])
                        gen_diag(0)
                        gen_diag(1)
                        for c in range(4):
                            ba, bb = (0, 1) if c % 2 == 0 else (2, 3)

                            def mab(e, c=c, ba=ba, bb=bb):
                                for k in range(8):
                                    e.matmul(PF[:, ba, :], lhsT=w_in_v[:, k, c * 128:(c + 1) * 128], rhs=xnT[:, k, :],
                                             start=(k == 0), stop=(k == 7))
                                for k in range(8):
                                    ins = e.matmul(PF[:, bb, :], lhsT=w_in_v[:, k, 512 + c * 128:512 + (c + 1) * 128],
                                                   rhs=xnT[:, k, :], start=(k == 0), stop=(k == 7))
                                return ins
                            S.add("pe", mab, reads=xnT_res + ["R0a", "R0b", "R1a"], writes=[f"PF{ba}", f"PF{bb}"])
                            S.add("act", lambda e, bb=bb: e.activation(out=tht[:], in_=PF[:, bb, :], func=AF.Tanh, scale=0.5),
                                  reads=[f"PF{bb}"], writes=["tht"])
                            S.add("dve", lambda e, c=c, ba=ba: e.scalar_tensor_tensor(out=u[:, c, 32:544], in0=tht[:], scalar=1.0,
                                                                                     in1=PF[:, ba, :], op0=ALU.add, op1=ALU.mult),
                                  reads=["tht", f"PF{ba}", "uh"], writes=[f"u{c}"])
                        if m == NM - 1:
                            S.add("pool", lambda e: e.tensor_copy(out=uhalo[:], in_=u[:, :, 512:544]), reads=ures, writes=["uhalo"])
                        if ps_ == 0 and m == 0:
                            checkpoint("sA2", [("u", u[:], [128, 4, 544], BF16, [])])
                        for c in range(4):
                            def mconv(e, c=c):
                                for tp in range(CW):
                                    ins = e.matmul(PF[:, c, :], lhsT=dg[:, c % 2, tp, :], rhs=u[:, c, 2 + tp:514 + tp],
                                                   start=(tp == 0), stop=(tp == CW - 1))
                                return ins
                            S.add("pe", mconv, reads=[f"u{c}", "uh", f"dg{c % 2}"], writes=[f"PF{c}"])
                            if c + 2 < 4:
                                gen_diag(c + 2)
                        if ps_ == 0 and m == 0:
                            checkpoint("sA3", [("u", u[:], [128, 4, 544], BF16, []), ("dg", dg[:], [128, 2, CW, 128], BF16, [])])
                        for c in range(4):
                            S.add("act", lambda e, c=c: e.activation(out=ysqb[:, c, :], in_=PF[:, c, :], func=AF.Square,
                                                                     bias=convb[:, c:c + 1]),
                                  reads=[f"PF{c}", "const"], writes=[f"ysqb{c}"])
                            S.add("dve", lambda e, c=c: e.tensor_scalar(out=ybf[:, c, :], in0=PF[:, c, :], scalar1=convb[:, c:c + 1],
                                                                        scalar2=None, op0=ALU.add),
                                  reads=[f"PF{c}", "const"], writes=[f"ybf{c}"])

                        def mstat(e):
                            for c in range(4):
                                e.matmul(PF[:, 4, :], lhsT=onesm[:], rhs=ybf[:, c, :], start=(c == 0), stop=(c == 3))
                            for c in range(4):
                                ins = e.matmul(PF[:, 5, :], lhsT=onesm[:], rhs=ysqb[:, c, :], start=(c == 0), stop=(c == 3))
                            return ins
                        S.add("pe", mstat, reads=[f"ybf{c}" for c in range(4)] + [f"ysqb{c}" for c in range(4)] + ["const"],
                              writes=["PF4", "PF5"])
                        S.add("act", lambda e: e.activation(out=t_a[:], in_=PF[:, 4, :], func=AF.Square), reads=["PF4"], writes=["t_a"])
                        S.add("act", lambda e: e.activation(out=tht[:], in_=PF[:, 4, :], func=AF.Copy), reads=["PF4"], writes=["tht"])
                        S.add("dve", lambda e: e.tensor_tensor(out=t_b[:], in0=PF[:, 5, :], in1=t_a[:], op=ALU.subtract),
                              reads=["PF5", "t_a"], writes=["t_b"])
                        S.add("dve", lambda e: e.tensor_scalar(out=t_b[:], in0=t_b[:], scalar1=0.0, scalar2=EPS, op0=ALU.max, op1=ALU.add),
                              reads=["t_b"], writes=["t_b"])
                        S.add("act", lambda e: e.activation(out=t_b[:], in_=t_b[:], func=AF.Ln), reads=["t_b"], writes=["t_b"])
                        S.add("act", lambda e: e.activation(out=t_b[:], in_=t_b[:], func=AF.Exp, scale=-0.5), reads=["t_b"], writes=["t_b"])
                        for c in range(4):
                            S.add("dve", lambda e, c=c: e.scalar_tensor_tensor(out=tmpn[:, c % 2, :], in0=PF[:, c, :],
                                                                               scalar=convb[:, c:c + 1], in1=tht[:],
                                                                               op0=ALU.add, op1=ALU.subtract),
                                  reads=[f"PF{c}", "tht", "const"], writes=[f"tmpn{c % 2}"])
                            S.add("dve", lambda e, c=c: e.tensor_tensor(out=tmpn[:, c % 2, :], in0=tmpn[:, c % 2, :], in1=t_b[:], op=ALU.mult),
                                  reads=[f"tmpn{c % 2}", "t_b"], writes=[f"tmpn{c % 2}"])
                            S.add("act", lambda e, c=c: e.activation(out=yconv[:, c, :], in_=tmpn[:, c % 2, :], func=AF.Silu,
                                                                     scale=lnw[:, c:c + 1], bias=lnb[:, c:c + 1]),
                                  reads=[f"tmpn{c % 2}", "const"], writes=[f"yconv{c}"])
                        if ps_ == 0 and m == 0:
                            checkpoint("sA4", [("yconv", yconv[:], [128, 4, 512], BF16, [])])
                        def x_pe(j):
                            tsl = slice(j * 128, (j + 1) * 128)

                            def mqkvg(e, tsl=tsl):
                                for (bk, c0) in ((0, 1024), (1, 1536), (2, 2048)):
                                    for k in range(8):
                                        ins = e.matmul(PF[:, bk, :], lhsT=xnT[:, k, tsl], rhs=w_in_v[:, k, c0:c0 + 512],
                                                       start=(k == 0), stop=(k == 7))
                                return ins
                            S.add("pe", mqkvg, reads=[f"xnT{j}", "R0a", "R0b", "R1a"], writes=["PF0", "PF1", "PF2"])

                        def x_ew(j):
                            sl = j % 2
                            gt = T0 + m * 4 + j
                            S.add("act", lambda e, sl=sl: e.activation(out=vb[:, sl, :], in_=PF[:, 1, :], func=AF.Copy),
                                  reads=["PF1"], writes=[f"vb{sl}"])
                            S.add("act", lambda e, sl=sl: e.activation(out=gs[:, sl, :], in_=PF[:, 2, :], func=AF.Silu),
                                  reads=["PF2"], writes=[f"gs{sl}"])
                            S.add("pool", lambda e, sl=sl: e.tensor_tensor(out=gs[:, sl, :], in0=gs[:, sl, :], in1=gnw[:], op=ALU.mult),
                                  reads=[f"gs{sl}", "const"], writes=[f"gs{sl}"])
                        def x_rot(j):
                            sl = j % 2
                            gt = T0 + m * 4 + j
                            rotary(PF[:, 0, :].rearrange("p (h d) -> p h d", h=16), cosm[:, gt, :], sinm[:, gt, :], 16,
                                   rtaA[:], rtbA[:], rot[:].rearrange("p (h d) -> p h d", h=16), ["PF0", "cosm", "sinm"], "rot")
                            S.add("act", lambda e, sl=sl: e.activation(out=qkb[:, sl, :], in_=rot[:], func=AF.Copy),
                                  reads=["rota", "rotb"], writes=[f"qkb{sl}"])
                            S.add("pool", lambda e, sl=sl: e.tensor_tensor(
                                out=kz[:, sl, :].rearrange("p (h d) -> p h d", h=8),
                                in0=rot[:, 256:512].rearrange("p (h d) -> p h d", h=8),
                                in1=zeta[:].unsqueeze(2).broadcast_to([128, 8, 32]), op=ALU.mult),
                                reads=["rota", "rotb", "const"], writes=[f"kz{sl}"])

                        def y_1a(j):
                            sl = j % 2

                            def trqk(e, sl=sl):
                                for i in range(8):
                                    ins = e.transpose(out=PT[0:64, 1, i * 128:(i + 1) * 128], in_=qkb[:, sl, i * 64:(i + 1) * 64],
                                                      identity=ident[:])
                                return ins
                            S.add("pe", trqk, reads=[f"qkb{sl}", "ident"], writes=["PT1"])
                            S.add("act", lambda e: e.activation(out=qkT[:], in_=PT[0:64, 1, :].rearrange("p (i n) -> p i n", i=8),
                                                                func=AF.Copy), reads=["PT1"], writes=["qkT"])
                            S.add("dve", lambda e: e.tensor_tensor(out=qx[:], in0=qkT[:, 0:4, :], in1=xifm[:], op=ALU.mult),
                                  reads=["qkT", "const"], writes=["qx"])

                        def y_1b(j):
                            def msc(e):
                                for hh in range(8):
                                    a, hp = hh % 2, hh // 2
                                    ins = e.matmul(PF[:, 3 + a, hp * 128:(hp + 1) * 128],
                                                   lhsT=qkT[32 * a:32 * a + 32, 4 + hp, :], rhs=qkT[32 * a:32 * a + 32, hp, :],
                                                   start=True, stop=True, skip_group_check=True)
                                return ins
                            S.add("pe", msc, reads=["qkT"], writes=["PF3", "PF4"])
                            for half in range(2):
                                S.add("dve", lambda e, half=half: e.tensor_tensor(
                                    out=scm[:].rearrange("p (hp a) n -> p hp a n", a=2)[:, :, half, :],
                                    in0=PF[:, 3 + half, :].rearrange("p (a b) -> p a b", a=4),
                                    in1=mask[:].rearrange("p (hp a) n -> p hp a n", a=2)[:, :, half, :], op=ALU.mult),
                                    reads=[f"PF{3 + half}", "const"], writes=[f"scm{half}"])

                        def y_1c(j):
                            sl = j % 2

                            def my(e, sl=sl):
                                for hp in range(4):
                                    e.matmul(PF[:, 5, hp * 128:(hp + 1) * 128], lhsT=qx[:, hp, :], rhs=Sbf[:, hp, :],
                                             start=True, stop=False, skip_group_check=True)
                                    for a in range(2):
                                        hh = 2 * hp + a
                                        ins = e.matmul(PF[:, 5, hh * 64:(hh + 1) * 64], lhsT=scm[:, hh, :],
                                                       rhs=vb[:, sl, hh * 64:(hh + 1) * 64], start=False, stop=(a == 1),
                                                       skip_group_check=True)
                                return ins
                            S.add("pe", my, reads=["scm0", "scm1", f"vb{sl}", "qx", "Sbf"], writes=["PF5"])

                            def msu(e, sl=sl):
                                for hp in range(4):
                                    ins = e.matmul(PF[0:64, 3, hp * 128:(hp + 1) * 128], lhsT=kz[:, sl, hp * 64:(hp + 1) * 64],
                                                   rhs=vb[:, sl, hp * 128:(hp + 1) * 128], start=True, stop=True, skip_group_check=True)
                                return ins
                            S.add("pe", msu, reads=[f"kz{sl}", f"vb{sl}"], writes=["PF3"])
                            S.add("dve", lambda e: e.tensor_tensor(out=Sst[:], in0=Sst[:], in1=g128[:], op=ALU.mult),
                                  reads=["Sst", "const"], writes=["Sst"])
                            for a in range(2):
                                S.add("dve", lambda e, a=a: e.tensor_tensor(
                                    out=Sst[32 * a:32 * a + 32, :, :], in0=Sst[32 * a:32 * a + 32, :, :],
                                    in1=PF[32 * a:32 * a + 32, 3, :].rearrange("p (hp x) -> p hp x", hp=4)[:, :, 64 * a:64 * a + 64],
                                    op=ALU.add), reads=["Sst", "PF3"], writes=["Sst"])
                            for a in range(2):
                                S.add("act", lambda e, a=a: e.activation(out=Sbf[32 * a:32 * a + 32, :, 64 * a:64 * a + 64],
                                                                         in_=Sst[32 * a:32 * a + 32, :, :], func=AF.Copy),
                                      reads=["Sst"], writes=["Sbf"])

                        def y_2(j):
                            sl = j % 2
                            Y3 = PF[:, 5, :].rearrange("p (h e) -> p h e", h=8)
                            S.add("dve", lambda e, Y3=Y3: e.tensor_reduce(out=gst[:, 0:8], in_=Y3, axis=AX.X, op=ALU.add),
                                  reads=["PF5"], writes=["gstA"])
                            S.add("act", lambda e: e.activation(out=t_a[:], in_=PF[:, 5, :], func=AF.Square), reads=["PF5"], writes=["t_a"])
                            S.add("dve", lambda e: e.tensor_reduce(out=gst[:, 8:16], in_=t_a[:].rearrange("p (h e) -> p h e", h=8),
                                                                   axis=AX.X, op=ALU.add), reads=["t_a"], writes=["gstB"])
                            S.add("dve", lambda e: e.tensor_scalar(out=gst[:, 0:16], in0=gst[:, 0:16], scalar1=1.0 / DV, scalar2=None,
                                                                   op0=ALU.mult), reads=["gstA", "gstB"], writes=["gstA", "gstB"])
                            S.add("dve", lambda e: e.tensor_tensor(out=gst[:, 16:24], in0=gst[:, 0:8], in1=gst[:, 0:8], op=ALU.mult),
                                  reads=["gstA"], writes=["gst2"])
                            S.add("dve", lambda e: e.scalar_tensor_tensor(out=gst[:, 16:24], in0=gst[:, 8:16], scalar=EPS, in1=gst[:, 16:24],
                                                                          op0=ALU.add, op1=ALU.subtract),
                                  reads=["gstB", "gst2"], writes=["gst2"])
                            S.add("pool", lambda e: e.tensor_tensor(out=gst[:, 16:24], in0=gst[:, 16:24], in1=neghalf[:, 0:8], op=ALU.pow),
                                  reads=["gst2", "neghalf"], writes=["gst2"])
                            yn3 = t_b[:].rearrange("p (h e) -> p h e", h=8)
                            S.add("dve", lambda e, Y3=Y3, yn3=yn3: e.tensor_tensor(
                                out=yn3, in0=Y3, in1=gst[:, 0:8].unsqueeze(2).broadcast_to([128, 8, 64]), op=ALU.subtract),
                                reads=["PF5", "gstA"], writes=["t_b"])
                        def y_2b(j):
                            sl = j % 2
                            yn3 = t_b[:].rearrange("p (h e) -> p h e", h=8)
                            S.add("dve", lambda e, yn3=yn3: e.tensor_tensor(
                                out=yn3, in0=yn3, in1=gst[:, 16:24].unsqueeze(2).broadcast_to([128, 8, 64]), op=ALU.mult),
                                reads=["t_b", "gst2"], writes=["t_b"])
                            S.add("dve", lambda e, sl=sl: e.tensor_tensor(out=yret[:], in0=t_b[:], in1=gs[:, sl, :], op=ALU.mult),
                                  reads=["t_b", f"gs{sl}"], writes=["yret"])

                            def tryr(e):
                                for i in range(4):
                                    ins = e.transpose(out=PT[:, 0, i * 128:(i + 1) * 128], in_=yret[:, i * 128:(i + 1) * 128],
                                                      identity=ident[:])
                                return ins
                            S.add("pe", tryr, reads=["yret", "ident"], writes=["PT0"])
                            S.add("act", lambda e: e.activation(out=yretT[:], in_=PT[:, 0, 0:512].rearrange("p (i n) -> p i n", i=4),
                                                                func=AF.Copy), reads=["PT0"], writes=["yretT"])

                        def y_3h(j, n):
                            tsl = slice(j * 128, (j + 1) * 128)

                            def mwo(e, tsl=tsl, n=n):
                                for kc in range(8):
                                    lt = yconv[:, kc, tsl] if kc < 4 else yretT[:, kc - 4, :]
                                    ins = e.matmul(PF[:, n, :], lhsT=lt, rhs=w_out_v[:, kc, n * 512:(n + 1) * 512],
                                                   start=(kc == 0), stop=(kc == 7))
                                return ins
                            S.add("pe", mwo, reads=[f"yconv{c}" for c in range(4)] + ["yretT", "R1b"], writes=[f"PF{n}"])

                        def y_3c(j):
                            t = m * 4 + j
                            S.add("dve", lambda e, t=t: e.tensor_tensor(out=h[:, t, :], in0=h[:, t, :],
                                                                        in1=PF[:, 0:2, :].rearrange("p a b -> p (a b)"), op=ALU.add),
                                  reads=[f"h{t}", "PF0", "PF1"], writes=[f"h{t}"])

                        x_pe(0)
                        x_ew(0)
                        x_rot(0)
                        for j in range(4):
                            y_1a(j)
                            if j > 0:
                                y_3h(j - 1, 0)
                            y_1b(j)
                            if j > 0:
                                y_3h(j - 1, 1)
                                y_3c(j - 1)
                            y_1c(j)
                            if j + 1 < 4:
                                x_pe(j + 1)
                            y_2(j)
                            if j + 1 < 4:
                                x_ew(j + 1)
                                x_rot(j + 1)
                            y_2b(j)
                        y_3h(3, 0)
                        y_3h(3, 1)
                        y_3c(3)
                    checkpoint(f"A{ps_}", [("h", h[:], [128, NTP, D], F32, [])])
                    S.barrier()

                with contextlib.ExitStack() as st:
                    RSTD_MODE[0] = "act"
                    W2 = sbuf(st, "W2", [128, 16384], BF16)
                    xq_v = W2[:, 0:8192].rearrange("p (k n) -> p k n", k=8)
                    xo_v = W2[:, 8192:16384].rearrange("p (k n) -> p k n", k=8)
                    load_g(0, 1)
                    load_weight("k_xq", xq_v, xq_w, 0, D, 0, 1024, ["R2a"])
                    load_weight("k_xo", xo_v, xo_w, 0, D, 0, 1024, ["R2b"])

                    def slot_views(s_):
                        if s_ < 2:
                            base = W[:, s_ * 16384:(s_ + 1) * 16384]
                        else:
                            base = W2[:, :]
                        return (base[:, 0:8192].rearrange("p (k n) -> p k n", k=8),
                                base[:, 8192:16384].rearrange("p (k n) -> p k n", k=8))

                    def load_quarter(qd, s_):
                        upv, dnv = slot_views(s_)
                        load_weight(f"k_u{s_}", upv, up_w, 0, D, qd * 1024, 1024, [f"R{s_}a"])
                        load_weight(f"k_d{s_}", dnv, down_w, qd * 1024, 1024, 0, 1024, [f"R{s_}b"])

                    with contextlib.ExitStack() as stb:
                        xnb = sbuf(stb, "xnB", [128, 2, D], BF16)
                        xnTb = sbuf(stb, "xnTB", [128, 8, 512], BF16)
                        qT = sbuf(stb, "qT", [128, 8, 512], BF16)
                        pn = sbuf(stb, "pn", [128, 3, 4, 256], BF16)
                        pTm = sbuf(stb, "pTm", [128, 8, 512], BF16)
                        oT = sbuf(stb, "oT", [128, 8, 512], BF16)
                        sst = sbuf(stb, "sst", [128, 3, 16], F32)
                        for m in range(NM):
                            items = []
                            for j in range(4):
                                t = m * 4 + j
                                items.append(((h[:, t, :], f"h{t}", 0, 16 + t, xnb[:, t % 2, :], f"xnb{t % 2}"),
                                              (xnb[:, t % 2, :], f"xnb{t % 2}", 0, xnTb[:, :, j * 128:(j + 1) * 128], f"xnTb{j}")))
                            norm_group(items)
                            xnT_res = [f"xnTb{j}" for j in range(4)]
                            load_quarter(m, m)
                            for c in range(8):
                                bk = c % 2

                                def mq(e, c=c, bk=bk):
                                    for k in range(8):
                                        ins = e.matmul(PF[:, bk, :], lhsT=xq_v[:, k, c * 128:(c + 1) * 128], rhs=xnTb[:, k, :],
                                                       start=(k == 0), stop=(k == 7))
                                    return ins
                                S.add("pe", mq, reads=xnT_res + ["R2a"], writes=[f"PF{bk}"])
                                if c % 2 == 0:
                                    S.add("act", lambda e, c=c, bk=bk: e.activation(out=qT[:, c, :], in_=PF[:, bk, :], func=AF.Copy),
                                          reads=[f"PF{bk}"], writes=[f"qT{c}"])
                                else:
                                    S.add("dve", lambda e, c=c, bk=bk: e.tensor_copy(out=qT[:, c, :], in_=PF[:, bk, :]),
                                          reads=[f"PF{bk}"], writes=[f"qT{c}"])
                            def b_msc(j):
                                tsl = slice(j * 128, (j + 1) * 128)
                                b0 = 2 * (j % 3)

                                def msc(e, tsl=tsl, b0=b0):
                                    for hd in range(4):
                                        for dd in range(2):
                                            ins = e.matmul(PF[:, b0 + hd // 2, (hd % 2) * 256:(hd % 2 + 1) * 256],
                                                           lhsT=qT[:, 2 * hd + dd, tsl], rhs=kTm[:, 2 * hd + dd, :],
                                                           start=(dd == 0), stop=(dd == 1), skip_group_check=True)
                                    return ins
                                S.add("pe", msc, reads=[f"qT{c}" for c in range(8)] + ["kTm"], writes=[f"PF{b0}", f"PF{b0 + 1}"])

                            def b_soft(j):
                                tsl = slice(j * 128, (j + 1) * 128)
                                par = j % 3
                                b0 = 2 * par
                                ptb = j % 2
                                SC3 = PF[:, b0:b0 + 2, :].rearrange("p a (b m) -> p (a b) m", b=2)
                                S.add("dve", lambda e, SC3=SC3, par=par: e.tensor_reduce(out=sst[:, par, 0:4], in_=SC3, axis=AX.X, op=ALU.max),
                                      reads=[f"PF{b0}", f"PF{b0 + 1}"], writes=[f"sst0{par}"])
                                S.add("dve", lambda e, par=par: e.tensor_scalar(out=sst[:, par, 4:8], in0=sst[:, par, 0:4], scalar1=-1.0 / 16.0,
                                                                                 scalar2=None, op0=ALU.mult),
                                      reads=[f"sst0{par}"], writes=[f"sst1{par}"])
                                for hd in range(4):
                                    S.add("act", lambda e, hd=hd, par=par, b0=b0: e.activation(
                                        out=pn[:, par, hd, :], in_=PF[:, b0 + hd // 2, (hd % 2) * 256:(hd % 2 + 1) * 256], func=AF.Exp,
                                        scale=1.0 / 16.0, bias=sst[:, par, 4 + hd:5 + hd], accum_out=sst[:, par, 8 + hd:9 + hd]),
                                        reads=[f"PF{b0 + hd // 2}", f"sst1{par}"], writes=[f"pn{par}_{hd}", f"sst2{par}_{hd}"])
                                S.add("dve", lambda e, par=par: e.reciprocal(out=sst[:, par, 12:16], in_=sst[:, par, 8:12]),
                                      reads=[f"sst2{par}_{hd}" for hd in range(4)], writes=[f"sst3{par}"])
                                S.add("dve", lambda e, par=par: e.tensor_tensor(
                                    out=pn[:, par, :, :], in0=pn[:, par, :, :],
                                    in1=sst[:, par, 12:16].unsqueeze(2).broadcast_to([128, 4, 256]), op=ALU.mult),
                                    reads=[f"pn{par}_{hd}" for hd in range(4)] + [f"sst3{par}"], writes=[f"pnn{par}"])

                                def trp(e, par=par, ptb=ptb):
                                    for hd in range(4):
                                        for mc in range(2):
                                            i = hd * 2 + mc
                                            ins = e.transpose(out=PT[:, ptb, i * 128:(i + 1) * 128],
                                                              in_=pn[:, par, hd, mc * 128:(mc + 1) * 128], identity=ident[:])
                                    return ins
                                S.add("pe", trp, reads=[f"pnn{par}", "ident"] + [f"pn{par}_{hd}" for hd in range(4)], writes=[f"PT{ptb}"])
                                S.add("act", lambda e, tsl=tsl, ptb=ptb: e.activation(out=pTm[:, :, tsl],
                                                                                      in_=PT[:, ptb, :].rearrange("p (i n) -> p i n", i=8),
                                                                                      func=AF.Copy), reads=[f"PT{ptb}"], writes=[f"pTm{j}"])
                            for s_i in range(6):
                                if s_i < 4:
                                    b_msc(s_i)
                                if s_i >= 2:
                                    b_soft(s_i - 2)
                            for c in range(8):
                                hd = c // 2
                                bk = c % 2

                                def mpv(e, c=c, hd=hd, bk=bk):
                                    for mc in range(2):
                                        ins = e.matmul(PF[:, bk, :], lhsT=vm[:, mc, c * 128:(c + 1) * 128], rhs=pTm[:, hd * 2 + mc, :],
                                                       start=(mc == 0), stop=(mc == 1))
                                    return ins
                                S.add("pe", mpv, reads=[f"pTm{j}" for j in range(4)] + ["vm"], writes=[f"PF{bk}"])
                                if c % 2 == 0:
                                    S.add("act", lambda e, c=c, bk=bk: e.activation(out=oT[:, c, :], in_=PF[:, bk, :], func=AF.Copy),
                                          reads=[f"PF{bk}"], writes=[f"oT{c}"])
                                else:
                                    S.add("dve", lambda e, c=c, bk=bk: e.tensor_copy(out=oT[:, c, :], in_=PF[:, bk, :]),
                                          reads=[f"PF{bk}"], writes=[f"oT{c}"])
                            for j in range(4):
                                t = m * 4 + j
                                tsl = slice(j * 128, (j + 1) * 128)
                                b0 = 2 + 2 * (j % 2)

                                def mxo(e, tsl=tsl, b0=b0):
                                    for n in range(2):
                                        for kc in range(8):
                                            ins = e.matmul(PF[:, b0 + n, :], lhsT=oT[:, kc, tsl], rhs=xo_v[:, kc, n * 512:(n + 1) * 512],
                                                           start=(kc == 0), stop=(kc == 7))
                                    return ins
                                S.add("pe", mxo, reads=[f"oT{c}" for c in range(8)] + ["R2b"], writes=[f"PF{b0}", f"PF{b0 + 1}"])
                                S.add("dve", lambda e, t=t, b0=b0: e.tensor_tensor(out=h[:, t, :], in0=h[:, t, :],
                                                                                   in1=PF[:, b0:b0 + 2, :].rearrange("p a b -> p (a b)"),
                                                                                   op=ALU.add),
                                      reads=[f"h{t}", f"PF{b0}", f"PF{b0 + 1}"], writes=[f"h{t}"])
                        checkpoint(f"B{ps_}", [("h", h[:], [128, NTP, D], F32, [])])
                        S.barrier()

                    with contextlib.ExitStack() as stc:
                        xnc = sbuf(stc, "xnC", [128, 2, D], BF16)
                        xnTa = sbuf(stc, "xnTa", [128, 8, TH], BF16)
                        rl = sbuf(stc, "rl", [128, 2, 512], F32)
                        hT = sbuf(stc, "hT", [128, 8, 512], BF16)
                        ot = sbuf(stc, "ot", [128, 2, D], F32)
                        load_g(1, 2)
                        load_g(0, 3)
                        items = []
                        for t in range(NTP):
                            items.append(((h[:, t, :], f"h{t}", 1, 24 + t, xnc[:, t % 2, :], f"xnc{t % 2}"),
                                          (xnc[:, t % 2, :], f"xnc{t % 2}", t % 2, xnTa[:, :, t * 128:(t + 1) * 128], f"xnTa{t}")))
                        norm_group(items)
                        load_quarter(3, 2)
                        def final_tile(t):
                            gt = T0 + t
                            col = 32 + t
                            S.add("act", lambda e, t=t, col=col: e.activation(out=junk[:], in_=h[:, t, :], func=AF.Square,
                                                                              accum_out=ss[:, col:col + 1]),
                                  reads=[f"h{t}"], writes=["junk", f"ss{col}"])
                            S.add("act", lambda e, col=col: e.activation(out=rstd[:, col:col + 1], in_=ss[:, col:col + 1], func=AF.Ln,
                                                                         scale=1.0 / D, bias=EPS),
                                  reads=[f"ss{col}"], writes=[f"rs{col}"])
                            S.add("act", lambda e, col=col: e.activation(out=rstd[:, col:col + 1], in_=rstd[:, col:col + 1], func=AF.Exp,
                                                                         scale=-0.5),
                                  reads=[f"rs{col}"], writes=[f"rs{col}"])
                            S.add("dve", lambda e, t=t, col=col: e.scalar_tensor_tensor(
                                out=ot[:, t % 2, :], in0=h[:, t, :], scalar=rstd[:, col:col + 1], in1=gsl[:, 0, :],
                                op0=ALU.mult, op1=ALU.mult), reads=[f"h{t}", f"rs{col}", "g0"], writes=[f"ot{t % 2}"])

                            def sto(e, s, t=t, gt=gt):
                                e.dma_start(out=out_d[gt * 128:(gt + 1) * 128, :], in_=ot[:, t % 2, :]).then_inc(s, 16)
                            dma("sp", f"st{t % 2}", sto, 1, reads=[f"ot{t % 2}"], writes=[f"out{t % 2}"])

                        qslots = [0, 1, 0, 2]
                        for qd in range(4):
                            s_ = qslots[qd]
                            if qd == 1:
                                load_quarter(2, 0)
                            if ps_ + 1 < NPASS and qd == 2:
                                load_weight("k_out", w_out_v, w_out, 0, D, 0, 1024, ["R1b"])
                            if ps_ + 1 < NPASS and qd == 3:
                                load_weight("k_in", w_in_v, w_in, 0, D, 0, 2560, ["R0a", "R0b", "R1a"])
                            upv, dnv = slot_views(s_)
                            for m in range(NM):
                                for f in range(8):
                                    bk = f % 2

                                    def mup(e, f=f, bk=bk, m=m, upv=upv):
                                        for k in range(8):
                                            ins = e.matmul(PF[:, bk, :], lhsT=upv[:, k, f * 128:(f + 1) * 128],
                                                           rhs=xnTa[:, k, m * 512:(m + 1) * 512], start=(k == 0), stop=(k == 7))
                                        return ins
                                    S.add("pe", mup, reads=[f"xnTa{m * 4 + j}" for j in range(4)] + [f"R{s_}a"], writes=[f"PF{bk}"])
                                    S.add("act", lambda e, f=f, bk=bk: e.activation(out=rl[:, f % 2, :], in_=PF[:, bk, :], func=AF.Relu),
                                          reads=[f"PF{bk}"], writes=[f"rl{f % 2}"])
                                    if f % 2 == 0:
                                        S.add("act", lambda e, f=f: e.activation(out=hT[:, f, :], in_=rl[:, f % 2, :], func=AF.Square),
                                              reads=[f"rl{f % 2}"], writes=[f"hT{f}"])
                                    else:
                                        S.add("dve", lambda e, f=f: e.tensor_tensor(out=hT[:, f, :], in0=rl[:, f % 2, :], in1=rl[:, f % 2, :],
                                                                                    op=ALU.mult), reads=[f"rl{f % 2}"], writes=[f"hT{f}"])
                                for j in range(4):
                                    t = m * 4 + j
                                    tsl = slice(j * 128, (j + 1) * 128)

                                    db = 2 + 2 * (j % 2)

                                    def mdn(e, tsl=tsl, dnv=dnv, db=db):
                                        for n in range(2):
                                            for f in range(8):
                                                ins = e.matmul(PF[:, db + n, :], lhsT=hT[:, f, tsl], rhs=dnv[:, f, n * 512:(n + 1) * 512],
                                                               start=(f == 0), stop=(f == 7))
                                        return ins
                                    S.add("pe", mdn, reads=[f"hT{f}" for f in range(8)] + [f"R{s_}b"], writes=[f"PF{db}", f"PF{db + 1}"])
                                    S.add("dve", lambda e, t=t, db=db: e.tensor_tensor(out=h[:, t, :], in0=h[:, t, :],
                                                                                       in1=PF[:, db:db + 2, :].rearrange("p a b -> p (a b)"),
                                                                                       op=ALU.add),
                                          reads=[f"h{t}", f"PF{db}", f"PF{db + 1}"], writes=[f"h{t}"])
                                    if qd == 3:
                                        final_tile(t)
                                        if ps_ + 1 < NPASS:
                                            def ldh2(e, s, t=t, gt2=T0 + NTP + t):
                                                e.dma_start(out=h[:, t, :], in_=xmain[gt2 * 128:(gt2 + 1) * 128, :]).then_inc(s, 16)
                                            dma("sp", f"h{t}", ldh2, 1, reads=[f"ot{t % 2}"], writes=[f"h{t}"])
                        S.barrier()


            for ps_ in range(NPASS):
                run_pass(ps_)

      except _Stop:
          pass
      if True:
        S.add("sp", None, reads=["out0", "out1"])

        dsems = {k: es.enter_context(nc.semaphore("d_" + k)) for k in sorted(dma_key_names)}
        block = es.enter_context(nc.Block())
        S.emit({"pe": block.tensor, "act": block.scalar, "dve": block.vector, "pool": block.gpsimd, "sp": block.sync},
               sems, dsems)
    return nc, S


_CACHE = {}
STOP = None
DBG_OUT = []


class _Stop(Exception):
    pass


def kernel(x, mem, positions, norm_mix_w, w_in, conv_w, conv_b, conv_ln_w, conv_ln_b, ret_gn_w, w_out,
           norm_xattn_w, norm_mem_w, xq_w, xkv_w, xo_w, norm_mlp_w, mlp_up_w, mlp_down_w, norm_f_w):
    f32 = np.float32
    x = np.asarray(x, dtype=f32)
    mem = np.asarray(mem, dtype=f32)
    positions = np.asarray(positions).astype(np.int32)
    B = x.shape[0]
    consts = _host_consts()
    shared = {
        "gvec": np.stack([np.asarray(v, dtype=f32) for v in (norm_mix_w, norm_xattn_w, norm_mlp_w, norm_f_w, norm_mem_w)], 0),
        "w_in": np.ascontiguousarray(np.asarray(w_in, dtype=f32)),
        "w_out": np.ascontiguousarray(np.asarray(w_out, dtype=f32)),
        "xq_w": np.ascontiguousarray(np.asarray(xq_w, dtype=f32)),
        "xkv_w": np.ascontiguousarray(np.asarray(xkv_w, dtype=f32)),
        "xo_w": np.ascontiguousarray(np.asarray(xo_w, dtype=f32)),
        "up_w": np.ascontiguousarray(np.asarray(mlp_up_w, dtype=f32)),
        "down_w": np.ascontiguousarray(np.asarray(mlp_down_w, dtype=f32)),
        "convw": np.ascontiguousarray(np.asarray(conv_w, dtype=f32).reshape(CW, 4, 128).transpose(2, 1, 0)),
        "convb": np.ascontiguousarray(np.asarray(conv_b, dtype=f32).reshape(4, 128).T),
        "lnw": np.ascontiguousarray(np.asarray(conv_ln_w, dtype=f32).reshape(4, 128).T),
        "lnb": np.ascontiguousarray(np.asarray(conv_ln_b, dtype=f32).reshape(4, 128).T),
        "gnw": np.asarray(ret_gn_w, dtype=f32).reshape(1, 512),
    }
    shared.update(consts)
    in_maps = []
    npre_tok = NPRE * 128
    for c in range(NCORE):
        b, q = c // 4, c % 4
        t0 = q * TPC
        xpre = np.zeros((npre_tok, D), dtype=f32)
        ppre = np.zeros((npre_tok,), dtype=np.int32)
        if t0 > 0:
            xpre[npre_tok - t0:] = x[b, 0:t0]
            ppre[npre_tok - t0:] = positions[b, 0:t0]
        m = dict(shared)
        m["xmain"] = np.ascontiguousarray(x[b, t0:t0 + TPC])
        m["xpre"] = xpre
        m["posm"] = np.ascontiguousarray(positions[b, t0:t0 + TPC].reshape(TPC // 128, 128).T)
        m["posp"] = np.ascontiguousarray(ppre.reshape(NPRE, 128).T)
        m["memx"] = np.ascontiguousarray(mem[b])
        in_maps.append(m)
    if "nc" not in _CACHE:
        _CACHE["nc"] = build_nc()[0]
    nc = _CACHE["nc"]
    res = run_bass_kernel_spmd(nc, in_maps, core_ids=list(range(NCORE)))
    out = np.zeros((B, SEQ, D), dtype=f32)
    for c in range(NCORE):
        b, q = c // 4, c % 4
        out[b, q * TPC:(q + 1) * TPC] = res.results[c]["out"]
    return out
```
